# Optimizing a Trainium2 kernel written in Bass

```python
import jax
import jax.numpy as jnp
from jax import lax
import numpy as np

D_MODEL = 1024
BATCH = 1
SEQ = 16384
DEPTH = 2
DEC_BATCH = 32
DEC_SEQ = 16
PAST_LEN = 4096

CHUNK = 64
N_HEADS = 8
QK_NOPE = 128
QK_ROPE = 64
V_DIM = 128
Q_LORA = 384
KV_LORA = 512
ATTN_WIDTH = N_HEADS * V_DIM
GM_CHUNK = 128
GM_WIDTH = 1024
GM_GROUPS = 8
GM_GROUP_DIM = GM_WIDTH // GM_GROUPS
D_FF = 2816
PLE_DIM = 256
ROPE_THETA = 10000.0
Q_BLOCK = 128
EPS = 1e-6
SCALE = (QK_NOPE + QK_ROPE) ** -0.5
SPLITS = [Q_LORA,
          Q_LORA + KV_LORA,
          Q_LORA + KV_LORA + QK_ROPE,
          Q_LORA + KV_LORA + QK_ROPE + GM_WIDTH,
          Q_LORA + KV_LORA + QK_ROPE + 2 * GM_WIDTH,
          Q_LORA + KV_LORA + QK_ROPE + 2 * GM_WIDTH + D_MODEL]
D_IN = Q_LORA + KV_LORA + QK_ROPE + 2 * GM_WIDTH + 2 * D_MODEL

kernel_name = 'hybrid_mla_gmlp_streaming_step'


def rmsnorm(x, g):
    xf = x.astype(jnp.float32)
    y = xf * lax.rsqrt(jnp.mean(xf * xf, axis=-1, keepdims=True) + EPS)
    return (y * g.astype(jnp.float32)).astype(x.dtype)


def rope(x, pos):
    half = QK_ROPE // 2
    inv = ROPE_THETA ** (-jnp.arange(half, dtype=jnp.float32) / half)
    ang = pos.astype(jnp.float32)[:, None] * inv[None, :]
    cos = jnp.cos(ang)[None, :, None, :].astype(x.dtype)
    sin = jnp.sin(ang)[None, :, None, :].astype(x.dtype)
    x1, x2 = x[..., :half], x[..., half:]
    return jnp.concatenate([x1 * cos - x2 * sin, x1 * sin + x2 * cos], axis=-1)


def swiglu(x, w_gate, w_up, w_down):
    return (jax.nn.silu(x @ w_gate) * (x @ w_up)) @ w_down


def prompt_attention(q_nope, q_rope, k_nope, k_rope, v):
    B, S = q_nope.shape[:2]
    nb = S // Q_BLOCK
    qn = q_nope.reshape(B, nb, Q_BLOCK, N_HEADS, QK_NOPE).transpose(1, 0, 2, 3, 4)
    qr = q_rope.reshape(B, nb, Q_BLOCK, N_HEADS, QK_ROPE).transpose(1, 0, 2, 3, 4)
    key_chunk = jnp.arange(S) // CHUNK
    neg = jnp.finfo(jnp.float32).min

    def block(args):
        qn_b, qr_b, start = args
        s = (jnp.einsum('bqhd,bkhd->bhqk', qn_b, k_nope)
             + jnp.einsum('bqhr,bkr->bhqk', qr_b, k_rope)).astype(jnp.float32) * SCALE
        q_chunk = (start + jnp.arange(Q_BLOCK)) // CHUNK
        mask = key_chunk[None, :] <= q_chunk[:, None]
        s = jnp.where(mask[None, None], s, neg)
        p = jax.nn.softmax(s, axis=-1).astype(v.dtype)
        return jnp.einsum('bhqk,bkhd->bqhd', p, v)

    out = lax.map(block, (qn, qr, jnp.arange(nb, dtype=jnp.int32) * Q_BLOCK))
    return out.transpose(1, 0, 2, 3, 4).reshape(B, S, ATTN_WIDTH)


def sample_attention(q_nope, q_rope, k_nope, k_rope, v):
    B, Q = q_nope.shape[:2]
    s = (jnp.einsum('bqhd,bkhd->bhqk', q_nope, k_nope)
         + jnp.einsum('bqhr,bkr->bhqk', q_rope, k_rope)).astype(jnp.float32) * SCALE
    p = jax.nn.softmax(s, axis=-1).astype(v.dtype)
    return jnp.einsum('bhqk,bkhd->bqhd', p, v).reshape(B, Q, ATTN_WIDTH)


def spatial_gating(u, v, w_s, b_s):
    B, S, _ = v.shape
    L = GM_CHUNK if S >= GM_CHUNK else S
    nc = S // L
    tri = jnp.tril(jnp.ones((L, L), dtype=bool))
    w = jnp.where(tri[None], w_s[:, :L, :L], jnp.zeros((), w_s.dtype))
    vc = v.reshape(B, nc, L, GM_GROUPS, GM_GROUP_DIM)
    mixed = jnp.einsum('gts,bcsgd->bctgd', w, vc) + b_s[:, :L].T[None, None, :, :, None]
    return u * mixed.reshape(B, S, GM_WIDTH)


def run_layer(h, p_l, pos, past_c, past_kr, wl):
    (f1_norm, f1_wg, f1_wu, f1_wd, mix_norm, w_in, q_a_norm, w_uq, q_nope_norm,
     q_rope_norm, kv_a_norm, k_rope_norm, w_uk, w_uv, k_nope_norm, gm_v_norm,
     gm_w_s, gm_b_s, w_o, f2_norm, f2_wg, f2_wu, f2_wd, ple_norm, ple_w_gate,
     ple_w_proj) = wl
    B, S, _ = h.shape
    h = h + 0.5 * swiglu(rmsnorm(h, f1_norm), f1_wg, f1_wu, f1_wd)
    n = rmsnorm(h, mix_norm)
    z = n @ w_in
    q_lat, kv_lat, kr_raw, u, v, g_a, g_b = jnp.split(z, SPLITS, axis=-1)
    q = (rmsnorm(q_lat, q_a_norm) @ w_uq).reshape(B, S, N_HEADS, QK_NOPE + QK_ROPE)
    q_nope = rmsnorm(q[..., :QK_NOPE], q_nope_norm)
    q_rope = rope(rmsnorm(q[..., QK_NOPE:], q_rope_norm), pos)
    c = rmsnorm(kv_lat, kv_a_norm)
    kr = rope(rmsnorm(kr_raw, k_rope_norm)[:, :, None, :], pos)[:, :, 0, :]
    if past_c is None:
        c_all, kr_all = c, kr
    else:
        c_all = jnp.concatenate([past_c, c], axis=1)
        kr_all = jnp.concatenate([past_kr, kr], axis=1)
    k_nope = rmsnorm(jnp.einsum('bsc,chd->bshd', c_all, w_uk), k_nope_norm)
    v_a = jnp.einsum('bsc,chd->bshd', c_all, w_uv)
    if past_c is None:
        o_a = prompt_attention(q_nope, q_rope, k_nope, kr_all, v_a)
    else:
        o_a = sample_attention(q_nope, q_rope, k_nope, kr_all, v_a)
    v_n = rmsnorm(v, gm_v_norm)
    o_b = spatial_gating(u, v_n, gm_w_s, gm_b_s)
    mixed = jax.nn.sigmoid(g_a) * o_a + jax.nn.sigmoid(g_b) * o_b
    h = h + mixed @ w_o
    h = h + 0.5 * swiglu(rmsnorm(h, f2_norm), f2_wg, f2_wu, f2_wd)
    h = h + jax.nn.sigmoid(rmsnorm(h, ple_norm) @ ple_w_gate) * (p_l @ ple_w_proj)
    return h, c, kr, v_n


def setup_inputs(seed: int = 0) -> dict:
    key = jax.random.key(seed)
    ks = iter(jax.random.split(key, 40))

    def nrm(shape, scale=1.0):
        return jax.random.normal(next(ks), shape, jnp.float32) * scale

    def gain(n):
        return 1.0 + 0.05 * nrm((DEPTH, n))

    def lin(fan_in, *shape):
        return nrm((DEPTH, fan_in) + tuple(shape), fan_in ** -0.5)

    return {
        'x_prompt': nrm((BATCH, SEQ, D_MODEL)),
        'x_sample': nrm((DEC_BATCH, DEC_SEQ, D_MODEL)),
        'cache_kv_latent': nrm((DEPTH, DEC_BATCH, PAST_LEN, KV_LORA)),
        'cache_k_rope': nrm((DEPTH, DEC_BATCH, PAST_LEN, QK_ROPE)),
        'p_prompt': nrm((DEPTH, BATCH, SEQ, PLE_DIM)),
        'p_sample': nrm((DEPTH, DEC_BATCH, DEC_SEQ, PLE_DIM)),
        'ffn1_norm': gain(D_MODEL),
        'ffn1_w_gate': lin(D_MODEL, D_FF),
        'ffn1_w_up': lin(D_MODEL, D_FF),
        'ffn1_w_down': lin(D_FF, D_MODEL),
        'mix_norm': gain(D_MODEL),
        'w_in': lin(D_MODEL, D_IN),
        'q_a_norm': gain(Q_LORA),
        'w_uq': lin(Q_LORA, N_HEADS * (QK_NOPE + QK_ROPE)),
        'q_nope_norm': gain(QK_NOPE),
        'q_rope_norm': gain(QK_ROPE),
        'kv_a_norm': gain(KV_LORA),
        'k_rope_norm': gain(QK_ROPE),
        'w_uk': lin(KV_LORA, N_HEADS, QK_NOPE),
        'w_uv': lin(KV_LORA, N_HEADS, V_DIM),
        'k_nope_norm': gain(QK_NOPE),
        'gm_v_norm': gain(GM_WIDTH),
        'gm_w_s': nrm((DEPTH, GM_GROUPS, GM_CHUNK, GM_CHUNK), GM_CHUNK ** -0.5),
        'gm_b_s': 1.0 + 0.1 * nrm((DEPTH, GM_GROUPS, GM_CHUNK)),
        'w_o': lin(D_MODEL, D_MODEL),
        'ffn2_norm': gain(D_MODEL),
        'ffn2_w_gate': lin(D_MODEL, D_FF),
        'ffn2_w_up': lin(D_MODEL, D_FF),
        'ffn2_w_down': lin(D_FF, D_MODEL),
        'ple_norm': gain(D_MODEL),
        'ple_w_gate': lin(D_MODEL, D_MODEL),
        'ple_w_proj': lin(PLE_DIM, D_MODEL),
    }


def reference(x_prompt, x_sample, cache_kv_latent, cache_k_rope, p_prompt, p_sample,
              ffn1_norm, ffn1_w_gate, ffn1_w_up, ffn1_w_down, mix_norm, w_in,
              q_a_norm, w_uq, q_nope_norm, q_rope_norm, kv_a_norm, k_rope_norm,
              w_uk, w_uv, k_nope_norm, gm_v_norm, gm_w_s, gm_b_s, w_o,
              ffn2_norm, ffn2_w_gate, ffn2_w_up, ffn2_w_down,
              ple_norm, ple_w_gate, ple_w_proj):
    stacked = (ffn1_norm, ffn1_w_gate, ffn1_w_up, ffn1_w_down, mix_norm, w_in,
               q_a_norm, w_uq, q_nope_norm, q_rope_norm, kv_a_norm, k_rope_norm,
               w_uk, w_uv, k_nope_norm, gm_v_norm, gm_w_s, gm_b_s, w_o,
               ffn2_norm, ffn2_w_gate, ffn2_w_up, ffn2_w_down,
               ple_norm, ple_w_gate, ple_w_proj)
    seq_p = x_prompt.shape[1]
    past = cache_kv_latent.shape[2]
    dec = x_sample.shape[1]
    pos_p = jnp.arange(seq_p, dtype=jnp.int32)
    pos_s = past + jnp.arange(dec, dtype=jnp.int32)
    hp, hs = x_prompt, x_sample
    pc, pkr, sc, skr, sv = [], [], [], [], []
    for l in range(DEPTH):
        wl = tuple(w[l] for w in stacked)
        hp, c_p, kr_p, _ = run_layer(hp, p_prompt[l], pos_p, None, None, wl)
        hs, c_s, kr_s, v_s = run_layer(hs, p_sample[l], pos_s,
                                       cache_kv_latent[l], cache_k_rope[l], wl)
        pc.append(c_p)
        pkr.append(kr_p)
        sc.append(c_s)
        skr.append(kr_s)
        sv.append(v_s)
    prompt_kv_latent = jnp.stack(pc)
    prompt_k_rope = jnp.stack(pkr)
    sample_kv_latent = jnp.stack(sc)
    sample_k_rope = jnp.stack(skr)
    sample_gm_v = jnp.stack(sv)
    return (hp, hs, prompt_kv_latent, prompt_k_rope, sample_kv_latent, sample_k_rope, sample_gm_v)
```

```python
import contextlib
import os
import numpy as np
import ml_dtypes
import concourse.bass as bass
import concourse.mybir as mybir
from concourse.bass_utils import run_bass_kernel_spmd

F32 = mybir.dt.float32
BF16 = mybir.dt.bfloat16
AF = mybir.ActivationFunctionType
ALU = mybir.AluOpType
ENGS = ("pe", "act", "dve", "pool", "sp")

D = 1024; DFF = 2816; NH = 8; QLORA = 384; KVL = 512; ROPE = 64; DIN = 5056; PLE = 256
SEQ = 16384; PAST = 4096; DEC = 16; NB = 32; DEPTH = 2
NT = 2112
EPS = 1e-6
SCALE = 192 ** -0.5
TILES = [(0, 512), (512, 512), (1024, 512), (1536, 512), (2048, 64)]
NEG = -30000.0

WSHAPES = [("f1g", D, DFF), ("f1u", D, DFF), ("f1d", DFF, D), ("win", D, DIN), ("wuq", QLORA, 1536),
           ("wuk", KVL, 1024), ("wuv", KVL, 1024), ("wo", D, D), ("f2g", D, DFF), ("f2u", D, DFF),
           ("f2d", DFF, D), ("pg", D, D), ("pp", PLE, D)]
WOFF = {}
_o = 0
for _n, _r, _c in WSHAPES:
    WOFF[_n] = (_o, _r // 128, _c)
    _o += (_r // 128) * _c
WL = _o
WLP = ((WL + 511) // 512) * 512
GCOLS = {"f1n": 0, "mixn": 8, "qan": 16, "qnn": 19, "qrn": 20, "kvan": 21, "krn": 25, "knn": 26, "f2n": 27, "plen": 35}
NG = 43


class Buf:
    __slots__ = ("name", "lw", "rd", "dcnt", "pw")

    def __init__(self, name):
        self.name = name
        self.lw = None
        self.rd = []
        self.dcnt = 0
        self.pw = {}


class Op:
    __slots__ = ("fn", "waits", "inc", "seq", "isdma", "n")

    def __init__(self, fn, waits, inc, seq, isdma, n=1):
        self.fn = fn; self.waits = waits; self.inc = inc; self.seq = seq; self.isdma = isdma; self.n = n


class _Rec:
    def __getattr__(self, name):
        def f(*a, **k):
            return (name, a, k)
        return f


_REC = _Rec()


class MK:
    def __init__(self):
        self.ops = {e: [] for e in ENGS}
        self.seen = {e: {} for e in ENGS}
        self.needed = {e: set() for e in ENGS}
        self.dsems = {}
        self.ccnt = {e: 0 for e in ENGS}

    @staticmethod
    def _flat(bs):
        out = []
        for b in bs:
            if isinstance(b, (list, tuple)):
                out.extend(MK._flat(b))
            else:
                out.append(b)
        return out

    def _deps(self, eng, reads, writes, isdma):
        reads = self._flat(reads); writes = self._flat(writes)
        deps = []
        for b in reads:
            if b.lw is not None:
                deps.append(b.lw)
            for k, v in b.pw.items():
                deps.append((k, v, "dma"))
        for b in writes:
            if b.lw is not None and (isdma or b.lw[2] != eng):
                deps.append(b.lw)
            for r in b.rd:
                if isdma or r[2] != eng:
                    deps.append(r)
        seen = self.seen[eng]
        mx = {}
        for (key, val, _e) in deps:
            if seen.get(key, 0) < val:
                seen[key] = val
                mx[key] = val
        for key, val in mx.items():
            if key[0] == "E":
                self.needed[key[1]].add(val)
        return list(mx.items())

    def _commit(self, ev, reads, writes, pwrites=()):
        reads = self._flat(reads); writes = self._flat(writes); pwrites = self._flat(pwrites)
        for b in reads:
            b.rd.append(ev)
        for b in writes:
            b.lw = ev
            b.rd = []
            b.pw = {}
        for b in pwrites:
            b.pw[ev[0]] = max(b.pw.get(ev[0], 0), ev[1])

    def op(self, eng, fn, reads=(), writes=()):
        waits = self._deps(eng, reads, writes, False)
        self.ccnt[eng] += 1
        seq = self.ccnt[eng]
        ev = (("E", eng), seq, eng)
        self.ops[eng].append(Op(fn(_REC), waits, ("E", eng), seq, False))
        self._commit(ev, reads, writes)
        return ev

    def dma(self, eng, fn, sbuf, reads=(), writes=(), pwrites=(), n=1, inc=16):
        waits = self._deps(eng, reads, writes, True)
        self.dsems[sbuf.name] = True
        sbuf.dcnt += inc * n
        key = ("D", sbuf.name)
        ev = (key, sbuf.dcnt, "dma")
        self.ops[eng].append(Op(fn(_REC), waits, key, None, True, inc))
        self._commit(ev, reads, writes, pwrites)
        return ev

    def wait_all(self, eng, bufs):
        waits = self._deps(eng, bufs, (), True)
        self.ops[eng].append(Op(None, waits, None, None, False))

    def emit(self, nc):
        rank = {}
        for e in ENGS:
            for i, s in enumerate(sorted(self.needed[e])):
                rank[(e, s)] = i + 1
        with contextlib.ExitStack() as st:
            sems = {}
            for e in ENGS:
                sems[("E", e)] = st.enter_context(nc.semaphore("se_" + e))
            for i, name in enumerate(self.dsems):
                sems[("D", name)] = st.enter_context(nc.semaphore("sd%d" % i))
            block = st.enter_context(nc.Block())

            def run(engname):
                def body(eng):
                    for o in self.ops[engname]:
                        for (key, val) in o.waits:
                            v = rank[(key[1], val)] if key[0] == "E" else val
                            eng.wait_ge(sems[key], v)
                        if o.fn is None:
                            continue
                        if o.isdma:
                            for (nm_, a_, k_) in o.fn:
                                getattr(eng, nm_)(*a_, **k_).then_inc(sems[o.inc], o.n)
                        else:
                            nm_, a_, k_ = o.fn
                            r = getattr(eng, nm_)(*a_, **k_)
                            if (engname, o.seq) in rank:
                                r.then_inc(sems[o.inc], 1)
                return body

            block.tensor(run("pe"))
            block.scalar(run("act"))
            block.vector(run("dve"))
            block.gpsimd(run("pool"))
            block.sync(run("sp"))
        return {e: len(self.ops[e]) for e in ENGS}, len(self.dsems)


class T:
    def __init__(self, t, name):
        self.t = t
        self.b = Buf(name)


def build(stop="full"):
    nc = bass.Bass("TRN2", target_bir_lowering=False, num_devices=8)
    mk = MK()
    staged = stop in ("s1", "s2", "s3")
    KIND = {"in": "ExternalInput", "out": "ExternalOutput", "int": "Internal"}
    dt_ = lambda n, s, d, role: nc.dram_tensor(n, s, d, kind=KIND[role]).ap()
    di = lambda n, s, d=F32: dt_(n, s, d, "in")
    do = lambda n, s, d=F32: dt_(n, s, d, "out")
    dx = lambda n, s, d=BF16: dt_(n, s, d, "int")

    class PerLayer:
        def __init__(self, aps):
            self.aps = aps

        def __getitem__(self, idx):
            if isinstance(idx, tuple):
                return self.aps[idx[0]][idx[1:]] if len(idx) > 1 else self.aps[idx[0]]
            return self.aps[idx]

    def role(l, what):
        if not staged:
            return "out" if what in ("O", "V") else "int"
        prodA = {"s1": 0, "s2": 1}.get(stop)
        cons = {"s2": 0, "s3": 1}.get(stop)
        if what in ("A", "O"):
            if l == prodA:
                return "out"
            if l == cons and what == "A":
                return "in"
            return "int"
        if what == "G":
            return "in" if l == cons else "int"
        if what == "V":
            return "out" if l == cons else "int"
        return "int"

    xin = di("xin", [NT, D]) if stop in ("full", "s1") or not staged else dx("xin", [NT, D], F32)
    big_role = "int" if stop == "s1" else "in"
    pin = dt_("pin", [DEPTH, NT, PLE], F32, big_role)
    cch = dt_("cch", [DEPTH, 4, PAST, KVL], F32, big_role); ckr = dt_("ckr", [DEPTH, 4, PAST, ROPE], F32, big_role)
    slab = di("slab", [DEPTH, 128, WLP]); gains = di("gains", [128, DEPTH * NG])
    gmtab = di("gmtab", [DEPTH, 128, 2560]); gmw = di("gmw", [DEPTH, 128, 1536])
    cst = di("cst", [128, 528])
    rope_t = di("rope_t", [64, 2 * NT])
    y_o = dt_("y_o", [NT, D], F32, "out" if stop in ("full", "s3") or not staged else "int")
    c_o = PerLayer([dt_("c_o%d" % l, [NT, KVL], F32, role(l, "O")) for l in range(DEPTH)])
    kr_o = PerLayer([dt_("kr_o%d" % l, [NT, ROPE], F32, role(l, "O")) for l in range(DEPTH)])
    gv_o = PerLayer([dt_("gv_o%d" % l, [64, D], F32, role(l, "V")) for l in range(DEPTH)])
    wbf = dx("wbf", [DEPTH, 128, WLP])
    qs_n = PerLayer([dt_("qs_n%d" % l, [NH, 128, NT], BF16, role(l, "A")) for l in range(DEPTH)])
    qs_r = PerLayer([dt_("qs_r%d" % l, [NH, 64, NT], BF16, role(l, "A")) for l in range(DEPTH)])
    ownc = PerLayer([dt_("ownc%d" % l, [128, 4, 2048], BF16, role(l, "A")) for l in range(DEPTH)])
    ownr = PerLayer([dt_("ownr%d" % l, [64, 4, 512], BF16, role(l, "A")) for l in range(DEPTH)])
    cbSd = PerLayer([dt_("cbSd%d" % l, [128, 4, 64], BF16, role(l, "A")) for l in range(DEPTH)])
    krSd = PerLayer([dt_("krSd%d" % l, [64, 64], BF16, role(l, "A")) for l in range(DEPTH)])
    xc = [dx("xc%d" % l, [8 * 128, 4 * 2048]) for l in range(DEPTH)]
    xr = [dx("xr%d" % l, [8 * 64, 4 * 512]) for l in range(DEPTH)]
    xco = [dt_("xco%d" % l, [8 * 128, 4 * 2048], BF16, role(l, "G")) for l in range(DEPTH)]
    xro = [dt_("xro%d" % l, [8 * 64, 4 * 512], BF16, role(l, "G")) for l in range(DEPTH)]
    hst_i = dt_("hst_i", [128, 8, NT], F32, "in" if stop in ("s2", "s3") else "int")
    hst_o = dt_("hst_o", [128, 8, NT], F32, "out" if stop in ("s1", "s2") else "int")
    kts = dx("kts", [DEPTH, 36, 128, NH, 512]); vs = dx("vs", [DEPTH, 36, 128, NH, 512])
    DB = {n: Buf(n) for n in ["wbf0", "wbf1", "qs0", "qs1", "xc0", "xc1", "xco0", "xco1", "xr0", "xr1", "xro0", "xro1",
                              "own0", "own1", "kv0", "kv1", "outs"]}

    with contextlib.ExitStack() as st:
        def sb(name, shape, dt=F32):
            return T(st.enter_context(nc.sbuf_tensor(name, shape, dt)), name)

        def ring(name, n, shape, dt=F32):
            return [sb("%s%d" % (name, i), shape, dt) for i in range(n)]

        class V:
            def __init__(self, t, b):
                self.t = t; self.b = b

        hT = sb("hT", [128, 8, NT])
        hB = [Buf("h%d" % i) for i in range(5)]
        BA = st.enter_context(nc.sbuf_tensor("BA", [128, 24 * 512], BF16))
        baB = [Buf("ba%d" % i) for i in range(24)]
        FA = st.enter_context(nc.sbuf_tensor("FA", [128, 7 * 512], F32))
        faB = [Buf("fa%d" % i) for i in range(7)]
        WA = st.enter_context(nc.sbuf_tensor("WA", [128, 16 * 512], BF16))
        waB = [Buf("wa%d" % i) for i in range(16)]

        def bav(lo, n, f=512):
            if n == 1:
                return V(BA[:, lo * 512:(lo + 1) * 512], baB[lo:lo + 1])
            return V(BA[:, lo * 512:(lo + n) * 512].rearrange("p (a f) -> p a f", f=f), baB[lo:lo + n])

        def fav(lo, n):
            return V(FA[:, lo * 512:(lo + n) * 512].rearrange("p (a f) -> p a f", f=512), faB[lo:lo + n])

        actT = bav(0, 22)
        qaT = bav(0, 3); cbP = bav(3, 4); mringT = bav(7, 4); krbP = bav(11, 1)
        qn_t = [bav(12, 1), bav(13, 1)]; qr_t = [bav(14, 1), bav(15, 1)]
        cbt = [bav(0, 4), bav(4, 4)]; kt8 = bav(8, 8); v8 = bav(16, 8)
        katt = [bav(0, 1), bav(1, 1), bav(2, 1)]; vatt = [bav(3, 1), bav(4, 1), bav(5, 1)]
        qnA = [bav(6, 1), bav(7, 1)]; oaT = bav(8, 8); kratt = [bav(16, 1), bav(17, 1), bav(18, 1)]; qrA = [bav(19, 1), bav(20, 1)]
        mxT = bav(0, 8); vtm = bav(16, 8, f=1024); pT = bav(22, 2)
        zq = fav(0, 3); zk = fav(3, 4)
        gmt = V(FA[:, 0:2560], faB[0:5])
        wuq = V(WA[:, 0:4608].rearrange("p (a f) -> p a f", f=1536), waB[0:9])
        wuk = V(WA[:, 0:4096].rearrange("p (a f) -> p a f", f=1024), waB[0:8])
        wuv = V(WA[:, 4096:8192].rearrange("p (a f) -> p a f", f=1024), waB[8:16])

        xn = ring("xn", 1, [128, 8, 512], BF16)
        wr = ring("wr", 20, [128, 512], BF16)
        t32 = ring("t32", 8, [128, 512])
        tb16 = ring("tb16", 5, [128, 512], BF16)
        big32 = ring("big32", 2, [128, 4, 512])
        krt = ring("krt", 2, [64, 512], BF16)
        cbS = sb("cbS", [128, 4, 64], BF16)
        krS = sb("krS", [64, 64], BF16)
        oaS = sb("oaS", [128, NH, 64], BF16)
        qSn = sb("qSn", [128, NH, 16], BF16)
        qSr = sb("qSr", [64, NH, 16], BF16)
        krr = sb("krr", [64, 512])
        gmwT = sb("gmwT", [128, 1536], BF16)
        gn = sb("gn", [128, DEPTH * NG])
        cs = sb("cs", [128, 528])
        csb = sb("csb", [128, 528], BF16)
        ropet = sb("ropet", [64, 1024])
        small = sb("small", [128, 16])
        smr = ring("smr", 4, [128, 8])
        dacc = ring("dacc", 2, [128, 512])
        ps = [T(st.enter_context(nc.psum_tensor("ps%d" % i, [128, 512], F32)), "ps%d" % i) for i in range(8)]
        held = [False] * 8
        pctr = [0]
        rctr = {}

        def pget(hold=False):
            for _ in range(16):
                i = pctr[0] % 8
                pctr[0] += 1
                if not held[i]:
                    if hold:
                        held[i] = True
                    return i
            raise RuntimeError("psum exhausted")

        def rel(*pis):
            for p in pis:
                held[p] = False

        def nxt(r, key):
            i = rctr.get(key, 0)
            rctr[key] = i + 1
            return r[i % len(r)]

        RT_b = csb.t[0:64, 256:320]
        zero_b = small.t[:, 1:2]

        def gcol(l, name, c=0, kp=128):
            o = l * NG + GCOLS[name] + c
            return gn.t[0:kp, o:o + 1]

        def sem_of(b):
            return b[0] if isinstance(b, list) else b

        def load(dst, dst_ap, src_ap, reads=()):
            mk.dma("sp", lambda e: [e.dma_start(out=dst_ap, in_=src_ap)], sem_of(dst.b), reads=list(reads), writes=[dst.b])

        def store(src, dst_ap, src_ap, dbuf, eng="pool"):
            mk.dma(eng, lambda e: [e.dma_start(out=dst_ap, in_=src_ap)], sem_of(src.b), reads=[src.b], pwrites=[dbuf])

        def wslot(l, name, kc, c0, ncols):
            off, nk, C = WOFF[name]
            s = nxt(wr, "wr")
            a = off + kc * C + c0
            load(s, s.t[:, 0:ncols], wbf[l, :, a:a + ncols], reads=[DB["wbf%d" % l]])
            return s

        def mm(pi, out_ap, lhsT, rhs, start, stop, reads):
            mk.op("pe", lambda e: e.matmul(out_ap, lhsT=lhsT, rhs=rhs, start=start, stop=stop), list(reads), [ps[pi].b])

        def act(func, out_ap, in_ap, reads, writes, **kw):
            mk.op("act", lambda e: e.activation(out=out_ap, in_=in_ap, func=func, **kw), reads, writes)

        def stt(out_ap, in0, scalar, in1, op0, op1, reads, writes):
            mk.op("dve", lambda e: e.scalar_tensor_tensor(out=out_ap, in0=in0, scalar=scalar, in1=in1, op0=op0, op1=op1), reads, writes)

        def tt(out_ap, in0, in1, op, reads, writes):
            mk.op("dve", lambda e: e.tensor_tensor(out=out_ap, in0=in0, in1=in1, op=op), reads, writes)

        def rstd_from(pi, kp, Tn, Dn):
            r = nxt(t32, "t32")
            act(AF.Sqrt, r.t[0:kp, 0:Tn], ps[pi].t[0:kp, 0:Tn], [ps[pi].b, small.b], [r.b], scale=1.0 / Dn, bias=small.t[0:kp, 0:1])
            mk.op("dve", lambda e: e.reciprocal(out=r.t[0:kp, 0:Tn], in_=r.t[0:kp, 0:Tn]), [r.b], [r.b])
            return r

        def sumsq(chunks, Tn):
            pi = pget(hold=True)
            n = len(chunks)
            for i, (ap, kp, bufs) in enumerate(chunks):
                sq = nxt(tb16, "tb16")
                act(AF.Square, sq.t[0:kp, 0:Tn], ap, bufs, [sq.b])
                mm(pi, ps[pi].t[:, 0:Tn], csb.t[0:kp, 128:256], sq.t[0:kp, 0:Tn], i == 0, i == n - 1, [sq.b, csb.b])
            return pi

        def rms_h(l, ti, gname):
            c0, Tn = TILES[ti]
            chunks = [(hT.t[:, c, c0:c0 + Tn], 128, [hB[ti]]) for c in range(8)]
            pi = sumsq(chunks, Tn)
            r = rstd_from(pi, 128, Tn, D)
            rel(pi)
            x = nxt(xn, "xn")
            for c in range(8):
                stt(x.t[:, c, 0:Tn], hT.t[:, c, c0:c0 + Tn], gcol(l, gname, c), r.t[:, 0:Tn], ALU.mult, ALU.mult, [hB[ti], r.b, gn.b], [x.b])
            return x

        def lin_group(l, wname, K, col0, mlist, rhs_fn, Tn, rbufs):
            ncols = max(co + M for co, M in mlist)
            pis = [pget(hold=True) for _ in mlist]
            for kc in range(K):
                s = wslot(l, wname, kc, col0, ncols)
                for j, (co, M) in enumerate(mlist):
                    mm(pis[j], ps[pis[j]].t[0:M, 0:Tn], s.t[:, co:co + M], rhs_fn(kc), kc == 0, kc == K - 1, [s.b] + rbufs)
            return pis

        C4 = [(0, 128), (128, 128), (256, 128), (384, 128)]

        def ffn(l, ti, nname, gname, uname, dname):
            c0, Tn = TILES[ti]
            x = rms_h(l, ti, nname)
            for g4 in range(6):
                nch = 4 if g4 < 5 else 2
                gs = [wslot(l, gname, kc, g4 * 512, nch * 128) for kc in range(8)]
                us = [wslot(l, uname, kc, g4 * 512, nch * 128) for kc in range(8)]
                for j in range(nch):
                    pg_ = pget(hold=True); pu_ = pget(hold=True)
                    for kc in range(8):
                        mm(pg_, ps[pg_].t[:, 0:Tn], gs[kc].t[:, j * 128:(j + 1) * 128], x.t[:, kc, 0:Tn], kc == 0, kc == 7, [gs[kc].b, x.b])
                    for kc in range(8):
                        mm(pu_, ps[pu_].t[:, 0:Tn], us[kc].t[:, j * 128:(j + 1) * 128], x.t[:, kc, 0:Tn], kc == 0, kc == 7, [us[kc].b, x.b])
                    sg = nxt(t32, "t32")
                    act(AF.Silu, sg.t[:, 0:Tn], ps[pg_].t[:, 0:Tn], [ps[pg_].b], [sg.b])
                    jj = g4 * 4 + j
                    tt(actT.t[:, jj, 0:Tn], sg.t[:, 0:Tn], ps[pu_].t[:, 0:Tn], ALU.mult, [ps[pu_].b, sg.b], [actT.b])
                    rel(pg_, pu_)
            for half in range(2):
                pis = [pget(hold=True) for _ in range(4)]
                for j in range(22):
                    s = wslot(l, dname, j, half * 512, 512)
                    for m in range(4):
                        mm(pis[m], ps[pis[m]].t[:, 0:Tn], s.t[:, m * 128:(m + 1) * 128], actT.t[:, j, 0:Tn], j == 0, j == 21, [s.b, actT.b])
                for m in range(4):
                    c = half * 4 + m
                    stt(hT.t[:, c, c0:c0 + Tn], ps[pis[m]].t[:, 0:Tn], 0.5, hT.t[:, c, c0:c0 + Tn], ALU.mult, ALU.add, [ps[pis[m]].b, hB[ti]], [hB[ti]])
                    rel(pis[m])

        def rope_apply(src32, srcb, Tn, dst_list):
            pr = pget(hold=True)
            mm(pr, ps[pr].t[0:64, 0:Tn], RT_b, srcb.t[0:64, 0:Tn], True, True, [srcb.b, csb.b])
            t1 = nxt(t32, "t32"); t2 = nxt(t32, "t32")
            tt(t1.t[0:64, 0:Tn], src32.t[0:64, 0:Tn], ropet.t[:, 0:Tn], ALU.mult, [src32.b, ropet.b], [t1.b])
            tt(t2.t[0:64, 0:Tn], ps[pr].t[0:64, 0:Tn], ropet.t[:, 512:512 + Tn], ALU.mult, [ps[pr].b, ropet.b], [t2.b])
            rel(pr)
            for ap, tobj in dst_list:
                tt(ap, t1.t[0:64, 0:Tn], t2.t[0:64, 0:Tn], ALU.add, [t1.b, t2.b], [tobj.b])

        def transpose_out(src_fn, nchunk, kp, Tn, dst_fn, dbuf, sbufs):
            for i in range((Tn + 127) // 128):
                nt_ = min(128, Tn - i * 128)
                o = nxt(big32, "big32")
                for cg in range(0, nchunk, 4):
                    ncg = min(4, nchunk - cg)
                    pi = pget(hold=True)
                    for c in range(ncg):
                        mk.op("pe", lambda e: e.transpose(ps[pi].t[0:nt_, c * kp:(c + 1) * kp], src_fn(cg + c)[:, i * 128:i * 128 + nt_], cs.t[0:kp, 0:kp]),
                              sbufs + [cs.b], [ps[pi].b])
                    act(AF.Copy, o.t[0:nt_, cg // 4, 0:ncg * kp], ps[pi].t[0:nt_, 0:ncg * kp], [ps[pi].b], [o.b])
                    rel(pi)
                W = nchunk * kp
                if W <= 512:
                    store(o, dst_fn(i, nt_), o.t[0:nt_, 0, 0:W], dbuf)
                else:
                    store(o, dst_fn(i, nt_).rearrange("t (a f) -> t a f", f=512), o.t[0:nt_, 0:W // 512, :], dbuf)

        def expand(l, ct_t, ct_b, kp, k8, v8_):
            for h in range(NH):
                pk = pget(hold=True)
                for kc in range(4):
                    mm(pk, ps[pk].t[:, 0:kp], wuk.t[:, kc, h * 128:(h + 1) * 128], ct_t[:, kc, 0:kp], kc == 0, kc == 3, [wuk.b, ct_b])
                pq = sumsq([(ps[pk].t[:, 0:kp], 128, [ps[pk].b])], kp)
                r = rstd_from(pq, 128, kp, 128)
                rel(pq)
                stt(k8.t[:, h, 0:kp], ps[pk].t[:, 0:kp], gcol(l, "knn"), r.t[:, 0:kp], ALU.mult, ALU.mult, [ps[pk].b, r.b, gn.b], [k8.b])
                rel(pk)
            for sub in range((kp + 127) // 128):
                nk = min(128, kp - sub * 128)
                for hf in range(2):
                    pv = pget(hold=True)
                    for kc in range(4):
                        mm(pv, ps[pv].t[0:nk, :], ct_t[:, kc, sub * 128:sub * 128 + nk], wuv.t[:, kc, hf * 512:(hf + 1) * 512], kc == 0, kc == 3, [wuv.b, ct_b])
                    act(AF.Copy, v8_.t[0:nk, hf * 4:(hf + 1) * 4, sub * 128:(sub + 1) * 128], ps[pv].t[0:nk, :].rearrange("p (h d) -> p h d", h=4), [ps[pv].b], [v8_.b])
                    rel(pv)

        LOOKAHEAD = 2

        def run_steps(po, pd, steps, auto=False, dacc=None):
            it = iter(steps)
            pend = []
            state = {"n": 0}

            def do_pv(pP, t_, last):
                kp, nq, oc0 = t_["kp"], t_["nq"], t_["oc0"]
                st_, sp_ = t_["start"], t_["stop"]
                i = state["n"]; state["n"] += 1
                if auto:
                    st_ = (i == 0)
                    sp_ = last
                p = nxt(tb16, "tb16")
                act(AF.Exp, p.t[0:kp, 0:nq], ps[pP].t[0:kp, 0:nq], [ps[pP].b, cs.b, small.b], [p.b], scale=SCALE, bias=t_["bias"])
                rel(pP)
                mm(po, ps[po].t[:, oc0:oc0 + nq], t_["V"], p.t[0:kp, 0:nq], st_, sp_, t_["vreads"] + [p.b])
                if dacc is None:
                    mm(pd, ps[pd].t[:, oc0:oc0 + nq], csb.t[0:kp, 128:256], p.t[0:kp, 0:nq], st_, sp_, [csb.b, p.b])
                else:
                    a = dacc[i % 2]
                    eng = "dve" if i % 2 == 0 else "pool"
                    if i < 2:
                        assert kp == 128 and nq == 512 and oc0 == 0
                        mk.op(eng, lambda e: e.tensor_copy(out=a.t[:, :], in_=p.t[:, :]), [p.b], [a.b])
                    else:
                        mk.op(eng, lambda e: e.tensor_tensor(out=a.t[0:kp, oc0:oc0 + nq], in0=a.t[0:kp, oc0:oc0 + nq], in1=p.t[0:kp, 0:nq], op=ALU.add), [a.b, p.b], [a.b])

            while True:
                s = next(it, None)
                if s is not None:
                    pS = pget(hold=True)
                    kp, nq = s["kp"], s["nq"]
                    mm(pS, ps[pS].t[0:kp, 0:nq], s["K"], s["qn"], True, False, s["kreads"])
                    mm(pS, ps[pS].t[0:kp, 0:nq], s["KR"], s["qr"], False, True, s["kreads"])
                    pend.append((pS, s))
                    if len(pend) > LOOKAHEAD:
                        pP, t_ = pend.pop(0)
                        do_pv(pP, t_, False)
                else:
                    while pend:
                        pP, t_ = pend.pop(0)
                        do_pv(pP, t_, len(pend) == 0)
                    break

        load(cs, cs.t[:], cst)
        load(gn, gn.t[:], gains)
        mk.op("dve", lambda e: e.tensor_copy(out=csb.t[:], in_=cs.t[:]), [cs.b], [csb.b])
        mk.op("pool", lambda e: e.memset(small.t[:], 0.0), [], [small.b])
        mk.op("pool", lambda e: e.memset(small.t[:, 0:1], EPS), [], [small.b])
        cast_engs = ["act", "dve", "pool"]
        cstate = [0]

        def convert(l, col_lo, col_hi):
            for c in range(col_lo // 512, (col_hi + 511) // 512):
                a = nxt(t32, "t32"); b_ = nxt(wr, "wr")
                load(a, a.t[:], slab[l, :, c * 512:(c + 1) * 512])
                eng = cast_engs[cstate[0] % 3]; cstate[0] += 1
                if eng == "act":
                    act(AF.Copy, b_.t[:], a.t[:], [a.b], [b_.b])
                else:
                    mk.op(eng, lambda e: e.tensor_copy(out=b_.t[:], in_=a.t[:]), [a.b], [b_.b])
                store(b_, wbf[l, :, c * 512:(c + 1) * 512], b_.t[:], DB["wbf%d" % l], eng="pool")

        PRE = WOFF["wuk"][0]
        SUF = WOFF["win"][0]

        def load_wuq(l):
            offs = [WOFF["wuq"][0] + kc * 1536 for kc in range(3)]
            mk.dma("sp", lambda e: [e.dma_start(out=wuq.t[:, kc, :], in_=wbf[l, :, offs[kc]:offs[kc] + 1536]) for kc in range(3)],
                   sem_of(wuq.b), reads=[DB["wbf%d" % l]], writes=[wuq.b], n=3)

        def load_wukv(l):
            for w, nm in ((wuk, "wuk"), (wuv, "wuv")):
                offs = [WOFF[nm][0] + kc * 1024 for kc in range(4)]
                mk.dma("sp", lambda e: [e.dma_start(out=w.t[:, kc, :], in_=wbf[l, :, offs[kc]:offs[kc] + 1024]) for kc in range(4)],
                       sem_of(w.b), reads=[DB["wbf%d" % l]], writes=[w.b], n=4)

        def load_gmw(l):
            a = nxt(big32, "big32")
            load(a, a.t[:, 0:3, :], gmw[l].rearrange("p (a f) -> p a f", f=512))
            for g in range(8):
                tt(gmwT.t[:, g * 128:(g + 1) * 128], a.t[:, g // 4, (g % 4) * 128:(g % 4 + 1) * 128], cs.t[:, 320:448], ALU.mult, [a.b, cs.b], [gmwT.b])
                tt(gmwT.t[0:64, 1024 + g * 64:1024 + (g + 1) * 64], a.t[0:64, 2, g * 64:(g + 1) * 64], cs.t[0:64, 448:512], ALU.mult, [a.b, cs.b], [gmwT.b])

        def load_x(ti):
            c0, Tn = TILES[ti]
            for i in range((Tn + 127) // 128):
                nt_ = min(128, Tn - i * 128)
                a = nxt(big32, "big32")
                load(a, a.t[0:nt_, 0:2, :], xin[c0 + i * 128:c0 + i * 128 + nt_, :].rearrange("t (a f) -> t a f", f=512))
                for cg in range(2):
                    pi = pget(hold=True)
                    for c in range(4):
                        mk.op("pe", lambda e: e.transpose(ps[pi].t[:, c * 128:c * 128 + nt_], a.t[0:nt_, cg, c * 128:(c + 1) * 128], cs.t[0:nt_, 0:nt_]),
                              [a.b, cs.b], [ps[pi].b])
                    act(AF.Copy, hT.t[:, cg * 4:(cg + 1) * 4, c0 + i * 128:c0 + i * 128 + nt_],
                        ps[pi].t[:].rearrange("p (c f) -> p c f", f=128)[:, :, 0:nt_], [ps[pi].b], [hB[ti]])
                    rel(pi)

        def phase_a(l, ti):
            c0, Tn = TILES[ti]
            prompt = ti < 4
            if l == 0:
                load_x(ti)
            ffn(l, ti, "f1n", "f1g", "f1u", "f1d")
            x = rms_h(l, ti, "mixn")
            rf = lambda kc: x.t[:, kc, 0:Tn]
            p1 = lin_group(l, "win", 8, 0, C4, rf, Tn, [x.b])
            for j in range(3):
                act(AF.Copy, zq.t[:, j, 0:Tn], ps[p1[j]].t[:, 0:Tn], [ps[p1[j]].b], [zq.b])
            act(AF.Copy, zk.t[:, 0, 0:Tn], ps[p1[3]].t[:, 0:Tn], [ps[p1[3]].b], [zk.b])
            rel(*p1)
            p2 = lin_group(l, "win", 8, 512, [(0, 128), (128, 128), (256, 128), (384, 64)], rf, Tn, [x.b])
            for j in range(3):
                act(AF.Copy, zk.t[:, 1 + j, 0:Tn], ps[p2[j]].t[:, 0:Tn], [ps[p2[j]].b], [zk.b])
            act(AF.Copy, krr.t[0:64, 0:Tn], ps[p2[3]].t[0:64, 0:Tn], [ps[p2[3]].b], [krr.b])
            rel(*p2)
            mk.dma("sp", lambda e: [e.dma_start(out=ropet.t[:, 0:Tn], in_=rope_t[:, c0:c0 + Tn]), e.dma_start(out=ropet.t[:, 512:512 + Tn], in_=rope_t[:, NT + c0:NT + c0 + Tn])],
                   ropet.b, writes=[ropet.b], n=2)
            pi = sumsq([(zq.t[:, j, 0:Tn], 128, [zq.b]) for j in range(3)], Tn)
            r = rstd_from(pi, 128, Tn, QLORA)
            rel(pi)
            for j in range(3):
                stt(qaT.t[:, j, 0:Tn], zq.t[:, j, 0:Tn], gcol(l, "qan", j), r.t[:, 0:Tn], ALU.mult, ALU.mult, [zq.b, r.b, gn.b], [qaT.b])
            for h in range(NH):
                pn = pget(hold=True)
                for kc in range(3):
                    mm(pn, ps[pn].t[:, 0:Tn], wuq.t[:, kc, h * 192:h * 192 + 128], qaT.t[:, kc, 0:Tn], kc == 0, kc == 2, [wuq.b, qaT.b])
                pr_ = pget(hold=True)
                for kc in range(3):
                    mm(pr_, ps[pr_].t[0:64, 0:Tn], wuq.t[:, kc, h * 192 + 128:h * 192 + 192], qaT.t[:, kc, 0:Tn], kc == 0, kc == 2, [wuq.b, qaT.b])
                pq = sumsq([(ps[pn].t[:, 0:Tn], 128, [ps[pn].b])], Tn)
                r1 = rstd_from(pq, 128, Tn, 128)
                rel(pq)
                qo = nxt(qn_t, "qn_t")
                stt(qo.t[:, 0:Tn], ps[pn].t[:, 0:Tn], gcol(l, "qnn"), r1.t[:, 0:Tn], ALU.mult, ALU.mult, [ps[pn].b, r1.b, gn.b], [qo.b])
                rel(pn)
                store(qo, qs_n[l, h, :, c0:c0 + Tn], qo.t[:, 0:Tn], DB["qs%d" % l])
                pq = sumsq([(ps[pr_].t[0:64, 0:Tn], 64, [ps[pr_].b])], Tn)
                r2 = rstd_from(pq, 64, Tn, 64)
                rel(pq)
                x32 = nxt(t32, "t32"); xb = nxt(tb16, "tb16")
                stt(x32.t[0:64, 0:Tn], ps[pr_].t[0:64, 0:Tn], gcol(l, "qrn", 0, 64), r2.t[0:64, 0:Tn], ALU.mult, ALU.mult, [ps[pr_].b, r2.b, gn.b], [x32.b])
                rel(pr_)
                act(AF.Copy, xb.t[0:64, 0:Tn], x32.t[0:64, 0:Tn], [x32.b], [xb.b])
                qro = nxt(qr_t, "qr_t")
                rope_apply(x32, xb, Tn, [(qro.t[0:64, 0:Tn], qro)])
                store(qro, qs_r[l, h, :, c0:c0 + Tn], qro.t[0:64, 0:Tn], DB["qs%d" % l])
            pi = sumsq([(zk.t[:, j, 0:Tn], 128, [zk.b]) for j in range(4)], Tn)
            r = rstd_from(pi, 128, Tn, KVL)
            rel(pi)
            for j in range(4):
                stt(zk.t[:, j, 0:Tn], zk.t[:, j, 0:Tn], gcol(l, "kvan", j), r.t[:, 0:Tn], ALU.mult, ALU.mult, [zk.b, r.b, gn.b], [zk.b])
            cb = cbP if prompt else cbS
            for j in range(4):
                act(AF.Copy, cb.t[:, j, 0:Tn], zk.t[:, j, 0:Tn], [zk.b], [cb.b])
            transpose_out(lambda c: zk.t[:, c, 0:Tn], 4, 128, Tn, lambda i, n: c_o[l, c0 + i * 128:c0 + i * 128 + n, :], DB["outs"], [zk.b])
            if prompt:
                store(cb, ownc[l, :, ti, :].rearrange("p (a f) -> p a f", f=512), cb.t[:, :, :], DB["own%d" % l])
                for rr in range(0 if staged else 8):
                    mk.op("dve", lambda e: e.tensor_scalar_mul(out=mringT.t[:], in0=cb.t[:], scalar1=cs.t[:, 520 + rr:521 + rr]), [cb.b, cs.b], [mringT.b])
                    store(mringT, xc[l][rr * 128:(rr + 1) * 128, ti * 2048:(ti + 1) * 2048].rearrange("p (a f) -> p a f", f=512), mringT.t[:], DB["xc%d" % l])
            pi = sumsq([(krr.t[0:64, 0:Tn], 64, [krr.b])], Tn)
            r = rstd_from(pi, 64, Tn, 64)
            rel(pi)
            stt(krr.t[0:64, 0:Tn], krr.t[0:64, 0:Tn], gcol(l, "krn", 0, 64), r.t[0:64, 0:Tn], ALU.mult, ALU.mult, [krr.b, r.b, gn.b], [krr.b])
            kb_ = nxt(tb16, "tb16")
            act(AF.Copy, kb_.t[0:64, 0:Tn], krr.t[0:64, 0:Tn], [krr.b], [kb_.b])
            kr32 = nxt(t32, "t32")
            krb = krbP if prompt else krS
            rope_apply(krr, kb_, Tn, [(kr32.t[0:64, 0:Tn], kr32), (krb.t[0:64, 0:Tn], krb)])
            transpose_out(lambda c: kr32.t[0:64, 0:Tn], 1, 64, Tn, lambda i, n: kr_o[l, c0 + i * 128:c0 + i * 128 + n, :], DB["outs"], [kr32.b])
            if prompt:
                store(krb, ownr[l, :, ti, :], krb.t[0:64, :], DB["own%d" % l])
                for rr in range(0 if staged else 8):
                    m = nxt(tb16, "tb16")
                    mk.op("dve", lambda e: e.tensor_scalar_mul(out=m.t[0:64, :], in0=krb.t[0:64, :], scalar1=cs.t[0:64, 520 + rr:521 + rr]), [krb.b, cs.b], [m.b])
                    store(m, xr[l][rr * 64:(rr + 1) * 64, ti * 512:(ti + 1) * 512], m.t[0:64, :], DB["xr%d" % l])

        ccb = [Buf("cc%d" % i) for i in range(4)]

        def exchange(l):
            mk.dma("pool", lambda e: [e.collective_compute("AllReduce", ALU.add, replica_groups=[list(range(8))], ins=[xc[l]], outs=[xco[l]])],
                   ccb[2 * l], reads=[DB["xc%d" % l]], writes=[DB["xco%d" % l]], inc=1)
            mk.dma("pool", lambda e: [e.collective_compute("AllReduce", ALU.add, replica_groups=[list(range(8))], ins=[xr[l]], outs=[xro[l]])],
                   ccb[2 * l + 1], reads=[DB["xr%d" % l]], writes=[DB["xro%d" % l]], inc=1)
            mk.wait_all("pool", [DB["xco%d" % l], DB["xro%d" % l]])

        def expand_prompt(l):
            for blk in range(36):
                ct = nxt(cbt, "cbt")
                if blk < 32:
                    r_, s_ = blk % 8, blk // 8
                    load(ct, ct.t[:], xco[l][r_ * 128:(r_ + 1) * 128, s_ * 2048:(s_ + 1) * 2048].rearrange("p (a f) -> p a f", f=512), reads=[DB["xco%d" % l]])
                else:
                    load(ct, ct.t[:], ownc[l, :, blk - 32, :].rearrange("p (a f) -> p a f", f=512), reads=[DB["own%d" % l]])
                expand(l, ct.t, ct.b, 512, kt8, v8)
                store(kt8, kts[l, blk], kt8.t[:], DB["kv%d" % l])
                store(v8, vs[l, blk], v8.t[:], DB["kv%d" % l])

        def att_sample(l):
            for b in range(4):
                q0 = 2048 + 16 * b
                load(qSn, qSn.t[:], qs_n[l, :, :, q0:q0 + 16].rearrange("h p t -> p h t"), reads=[DB["qs%d" % l]])
                load(qSr, qSr.t[:], qs_r[l, :, :, q0:q0 + 16].rearrange("h p t -> p h t"), reads=[DB["qs%d" % l]])
                po = pget(hold=True); pd = pget(hold=True)
                for kb in range(9):
                    if kb < 8:
                        a = nxt(big32, "big32")
                        load(a, a.t[:], cch[l, b, kb * 512:(kb + 1) * 512, :].rearrange("(s p) f -> p s f", p=128))
                        ct = nxt(cbt, "cbt")
                        for ch in range(4):
                            pi = pget(hold=True)
                            for sub in range(4):
                                mk.op("pe", lambda e: e.transpose(ps[pi].t[:, sub * 128:(sub + 1) * 128], a.t[:, sub, ch * 128:(ch + 1) * 128], cs.t[:, 0:128]), [a.b, cs.b], [ps[pi].b])
                            act(AF.Copy, ct.t[:, ch, :], ps[pi].t[:, :], [ps[pi].b], [ct.b])
                            rel(pi)
                        a2 = nxt(t32, "t32")
                        load(a2, a2.t[:, 0:256].rearrange("p (s f) -> p s f", f=64), ckr[l, b, kb * 512:(kb + 1) * 512, :].rearrange("(s p) f -> p s f", p=128))
                        pi = pget(hold=True)
                        for sub in range(4):
                            mk.op("pe", lambda e: e.transpose(ps[pi].t[0:64, sub * 128:(sub + 1) * 128], a2.t[:, sub * 64:(sub + 1) * 64], cs.t[:, 0:128]), [a2.b, cs.b], [ps[pi].b])
                        krx = nxt(krt, "krt")
                        act(AF.Copy, krx.t[0:64, :], ps[pi].t[0:64, :], [ps[pi].b], [krx.b])
                        rel(pi)
                        kp = 512; ct_t, ct_b = ct.t, ct.b; kr_t, kr_b = krx.t, krx.b
                    else:
                        kp = 16; ct_t, ct_b = cbS.t[:, :, 16 * b:16 * b + 16], cbS.b
                        kr_t, kr_b = krS.t[:, 16 * b:16 * b + 16], krS.b
                    expand(l, ct_t, ct_b, kp, kt8, v8)
                    steps = []
                    for h in range(NH):
                        for sub in range((kp + 127) // 128):
                            nk = min(128, kp - sub * 128)
                            steps.append(dict(K=kt8.t[:, h, sub * 128:sub * 128 + nk], KR=kr_t[0:64, sub * 128:sub * 128 + nk], V=v8.t[0:nk, h, sub * 128:(sub + 1) * 128],
                                              kp=nk, nq=16, oc0=h * 16, qn=qSn.t[:, h, :], qr=qSr.t[0:64, h, :], bias=small.t[0:nk, 1:2],
                                              kreads=[kt8.b, kr_b, qSn.b, qSr.b], vreads=[v8.b], start=(kb == 0 and sub == 0), stop=(kb == 8)))
                    run_steps(po, pd, steps)
                r = nxt(t32, "t32")
                mk.op("dve", lambda e: e.reciprocal(out=r.t[:, 0:128], in_=ps[pd].t[:, 0:128]), [ps[pd].b], [r.b])
                tt(oaS.t[:, :, 16 * b:16 * b + 16], ps[po].t[:, 0:128].rearrange("p (h q) -> p h q", h=8), r.t[:, 0:128].rearrange("p (h q) -> p h q", h=8), ALU.mult, [ps[po].b, r.b], [oaS.b])
                rel(po, pd)

        def att_prompt(l, s_):
            c0 = s_ * 512
            blist = [(8 * s2 + r2, None, s2, r2) for s2 in range(s_) for r2 in range(8)] + [(8 * s_ + r2, r2, s_, r2) for r2 in range(8)] + [(32 + s_, "diag", s_, 0)]
            for h in range(NH):
                qn = nxt(qnA, "qnA"); qr = nxt(qrA, "qrA")
                load(qn, qn.t[:, :], qs_n[l, h, :, c0:c0 + 512], reads=[DB["qs%d" % l]])
                load(qr, qr.t[0:64, :], qs_r[l, h, :, c0:c0 + 512], reads=[DB["qs%d" % l]])
                po = pget(hold=True)

                def gen(qn=qn, qr=qr):
                    for (blk, mode, s2, r2) in blist:
                        ka = nxt(katt, "katt"); va = nxt(vatt, "vatt"); kra = nxt(kratt, "kratt")
                        load(ka, ka.t[:, :], kts[l, blk, :, h, :], reads=[DB["kv%d" % l]])
                        load(va, va.t[:, :], vs[l, blk, :, h, :], reads=[DB["kv%d" % l]])
                        if blk < 32:
                            load(kra, kra.t[0:64, :], xro[l][r2 * 64:(r2 + 1) * 64, s2 * 512:(s2 + 1) * 512], reads=[DB["xro%d" % l]])
                        else:
                            load(kra, kra.t[0:64, :], ownr[l, :, s_, :], reads=[DB["own%d" % l]])
                        kreads = [ka.b, kra.b, qn.b, qr.b]
                        for sub in range(4):
                            ks = slice(sub * 128, (sub + 1) * 128)
                            if mode != "diag":
                                bias = small.t[:, 1:2] if mode is None else cs.t[:, 512 + mode:513 + mode]
                                yield dict(K=ka.t[:, ks], KR=kra.t[0:64, ks], V=va.t[:, ks], kp=128, nq=512, oc0=0, qn=qn.t[:, :], qr=qr.t[0:64, :],
                                           bias=bias, kreads=kreads, vreads=[va.b], start=False, stop=False)
                            else:
                                qa_ = 128 * sub + 64
                                if qa_ < 512:
                                    yield dict(K=ka.t[:, ks], KR=kra.t[0:64, ks], V=va.t[:, ks], kp=128, nq=512 - qa_, oc0=qa_, qn=qn.t[:, qa_:512], qr=qr.t[0:64, qa_:512],
                                               bias=small.t[:, 1:2], kreads=kreads, vreads=[va.b], start=False, stop=False)
                                k2 = slice(sub * 128, sub * 128 + 64)
                                yield dict(K=ka.t[:, k2], KR=kra.t[0:64, k2], V=va.t[0:64, ks], kp=64, nq=64, oc0=128 * sub, qn=qn.t[:, 128 * sub:128 * sub + 64],
                                           qr=qr.t[0:64, 128 * sub:128 * sub + 64], bias=small.t[0:64, 1:2], kreads=kreads, vreads=[va.b], start=False, stop=False)

                run_steps(po, None, gen(), auto=True, dacc=dacc)
                pd = pget(hold=True)
                mm(pd, ps[pd].t[:, :], cs.t[:, 128:256], dacc[0].t[:, :], True, False, [cs.b, dacc[0].b])
                mm(pd, ps[pd].t[:, :], cs.t[:, 128:256], dacc[1].t[:, :], False, True, [cs.b, dacc[1].b])
                r = nxt(t32, "t32")
                mk.op("dve", lambda e: e.reciprocal(out=r.t[:, :], in_=ps[pd].t[:, :]), [ps[pd].b], [r.b])
                tt(oaT.t[:, h, :], ps[po].t[:, :], r.t[:, :], ALU.mult, [ps[po].b, r.b], [oaT.b])
                rel(po, pd)

        def phase_c(l, ti):
            c0, Tn = TILES[ti]
            prompt = ti < 4
            oa = oaT if prompt else oaS
            nsub = (Tn + 127) // 128
            load(gmt, gmt.t[:, :], gmtab[l])
            x = rms_h(l, ti, "mixn")
            rf = lambda kc: x.t[:, kc, 0:Tn]
            for i in range(nsub):
                nt_ = min(128, Tn - i * 128)
                pv = [pget(hold=True), pget(hold=True)]
                for hf in range(2):
                    for kc in range(8):
                        s = wslot(l, "win", kc, 1984 + hf * 512, 512)
                        mm(pv[hf], ps[pv[hf]].t[0:nt_, :], x.t[:, kc, i * 128:i * 128 + nt_], s.t[:, 0:512], kc == 0, kc == 7, [s.b, x.b])
                sm = nxt(smr, "smr")
                for hf in range(2):
                    sq = nxt(t32, "t32")
                    act(AF.Square, sq.t[0:nt_, :], ps[pv[hf]].t[0:nt_, :], [ps[pv[hf]].b], [sq.b, sm.b], accum_out=sm.t[0:nt_, hf:hf + 1])
                tt(sm.t[0:nt_, 2:3], sm.t[0:nt_, 0:1], sm.t[0:nt_, 1:2], ALU.add, [sm.b], [sm.b])
                act(AF.Sqrt, sm.t[0:nt_, 3:4], sm.t[0:nt_, 2:3], [sm.b, small.b], [sm.b], scale=1.0 / 1024, bias=small.t[0:nt_, 0:1])
                mk.op("dve", lambda e: e.reciprocal(out=sm.t[0:nt_, 4:5], in_=sm.t[0:nt_, 3:4]), [sm.b], [sm.b])
                for hf in range(2):
                    stt(vtm.t[0:nt_, i, hf * 512:(hf + 1) * 512], ps[pv[hf]].t[0:nt_, :], sm.t[0:nt_, 4:5], gmt.t[0:nt_, hf * 512:(hf + 1) * 512], ALU.mult, ALU.mult,
                        [ps[pv[hf]].b, sm.b, gmt.b], [vtm.b])
                if not prompt:
                    g32 = nxt(big32, "big32")
                    for hf in range(2):
                        stt(g32.t[0:nt_, hf, :], ps[pv[hf]].t[0:nt_, :], sm.t[0:nt_, 4:5], gmt.t[0:nt_, hf * 512:(hf + 1) * 512], ALU.mult, ALU.mult,
                            [ps[pv[hf]].b, sm.b, gmt.b], [g32.b])
                    store(g32, gv_o[l].rearrange("t (a f) -> t a f", f=512), g32.t[0:64, 0:2, :], DB["outs"])
                rel(*pv)
            for mg in range(2):
                pu = lin_group(l, "win", 8, 960 + mg * 512, C4, rf, Tn, [x.b])
                for j in range(4):
                    m = mg * 4 + j
                    pm = pget(hold=True)
                    t = nxt(t32, "t32")
                    for i in range(nsub):
                        nt_ = min(128, Tn - i * 128)
                        if prompt:
                            wsT = gmwT.t[:, m * 128:(m + 1) * 128]; bia = gmt.t[:, 1024 + m * 128:1024 + (m + 1) * 128]
                        else:
                            wsT = gmwT.t[0:64, 1024 + m * 64:1024 + (m + 1) * 64]; bia = gmt.t[:, 2048 + m * 64:2048 + (m + 1) * 64]
                        mm(pm, ps[pm].t[:, i * 128:i * 128 + nt_], vtm.t[0:nt_, i, m * 128:(m + 1) * 128], wsT, True, True, [vtm.b, gmwT.b])
                        tt(t.t[:, i * 128:i * 128 + nt_], ps[pm].t[:, i * 128:i * 128 + nt_], bia, ALU.add, [ps[pm].b, gmt.b], [t.b])
                    tt(mxT.t[:, m, 0:Tn], t.t[:, 0:Tn], ps[pu[j]].t[:, 0:Tn], ALU.mult, [t.b, ps[pu[j]].b], [mxT.b])
                    rel(pm, pu[j])
            for mg in range(2):
                pg_ = lin_group(l, "win", 8, 4032 + mg * 512, C4, rf, Tn, [x.b])
                for j in range(4):
                    m = mg * 4 + j
                    sg = nxt(t32, "t32")
                    act(AF.Sigmoid, sg.t[:, 0:Tn], ps[pg_[j]].t[:, 0:Tn], [ps[pg_[j]].b], [sg.b])
                    rel(pg_[j])
                    tt(mxT.t[:, m, 0:Tn], mxT.t[:, m, 0:Tn], sg.t[:, 0:Tn], ALU.mult, [mxT.b, sg.b], [mxT.b])
            for mg in range(2):
                pg_ = lin_group(l, "win", 8, 3008 + mg * 512, C4, rf, Tn, [x.b])
                for j in range(4):
                    m = mg * 4 + j
                    sg = nxt(t32, "t32")
                    act(AF.Sigmoid, sg.t[:, 0:Tn], ps[pg_[j]].t[:, 0:Tn], [ps[pg_[j]].b], [sg.b])
                    rel(pg_[j])
                    tt(sg.t[:, 0:Tn], sg.t[:, 0:Tn], oa.t[:, m, 0:Tn], ALU.mult, [sg.b, oa.b], [sg.b])
                    tt(mxT.t[:, m, 0:Tn], mxT.t[:, m, 0:Tn], sg.t[:, 0:Tn], ALU.add, [mxT.b, sg.b], [mxT.b])
            for half in range(2):
                pis = lin_group(l, "wo", 8, half * 512, C4, lambda kc: mxT.t[:, kc, 0:Tn], Tn, [mxT.b])
                for m in range(4):
                    c = half * 4 + m
                    tt(hT.t[:, c, c0:c0 + Tn], ps[pis[m]].t[:, 0:Tn], hT.t[:, c, c0:c0 + Tn], ALU.add, [ps[pis[m]].b, hB[ti]], [hB[ti]])
                    rel(pis[m])
            ffn(l, ti, "f2n", "f2g", "f2u", "f2d")
            x = rms_h(l, ti, "plen")
            for i in range(nsub):
                nt_ = min(128, Tn - i * 128)
                a = nxt(t32, "t32")
                load(a, a.t[0:nt_, 0:256], pin[l, c0 + i * 128:c0 + i * 128 + nt_, :])
                pi = pget(hold=True)
                for c in range(2):
                    mk.op("pe", lambda e: e.transpose(ps[pi].t[:, c * 128:c * 128 + nt_], a.t[0:nt_, c * 128:(c + 1) * 128], cs.t[0:nt_, 0:nt_]), [a.b, cs.b], [ps[pi].b])
                act(AF.Copy, pT.t[:, 0:2, i * 128:i * 128 + nt_], ps[pi].t[:, 0:256].rearrange("p (c f) -> p c f", f=128)[:, :, 0:nt_], [ps[pi].b], [pT.b])
                rel(pi)
            for q4 in range(4):
                C2 = [(0, 128), (128, 128)]
                pg_ = lin_group(l, "pg", 8, q4 * 256, C2, lambda kc: x.t[:, kc, 0:Tn], Tn, [x.b])
                pp_ = lin_group(l, "pp", 2, q4 * 256, C2, lambda kc: pT.t[:, kc, 0:Tn], Tn, [pT.b])
                for j in range(2):
                    c = q4 * 2 + j
                    sg = nxt(t32, "t32")
                    act(AF.Sigmoid, sg.t[:, 0:Tn], ps[pg_[j]].t[:, 0:Tn], [ps[pg_[j]].b], [sg.b])
                    tt(sg.t[:, 0:Tn], sg.t[:, 0:Tn], ps[pp_[j]].t[:, 0:Tn], ALU.mult, [sg.b, ps[pp_[j]].b], [sg.b])
                    tt(hT.t[:, c, c0:c0 + Tn], hT.t[:, c, c0:c0 + Tn], sg.t[:, 0:Tn], ALU.add, [hB[ti], sg.b], [hB[ti]])
                    rel(pg_[j], pp_[j])

        finals = [DB["outs"]]

        def save_state(l):
            mk.dma("pool", lambda e: [e.dma_start(out=hst_o, in_=hT.t[:])], hB[0], reads=[hB], pwrites=[DB["outs"]])
            store(cbS, cbSd[l], cbS.t[:], DB["outs"])
            store(krS, krSd[l], krS.t[:], DB["outs"])
            finals.extend([DB["qs%d" % l], DB["own%d" % l]])

        def load_state(l):
            mk.dma("sp", lambda e: [e.dma_start(out=hT.t[:], in_=hst_i)], hB[0], writes=[hB])
            load(cbS, cbS.t[:], cbSd[l])
            load(krS, krS.t[:], krSd[l])

        def rest_of_layer(l):
            load_wukv(l)
            load_gmw(l)
            att_sample(l)
            phase_c(l, 4)
            expand_prompt(l)
            for s_ in range(4):
                att_prompt(l, s_)
                phase_c(l, s_)

        def final_y():
            for ti in range(5):
                c0, Tn = TILES[ti]
                transpose_out(lambda c: hT.t[:, c, c0:c0 + Tn], 8, 128, Tn, lambda i, n: y_o[c0 + i * 128:c0 + i * 128 + n, :], DB["outs"], [hB[ti]])

        if stop == "full":
            for l in range(DEPTH):
                convert(l, 0, WLP)
            for l in range(DEPTH):
                load_wuq(l)
                for ti in range(5):
                    phase_a(l, ti)
                exchange(l)
                rest_of_layer(l)
            final_y()
        elif stop == "s1":
            convert(0, 0, PRE)
            load_wuq(0)
            for ti in range(5):
                phase_a(0, ti)
            save_state(0)
        elif stop == "s2":
            convert(0, SUF, WLP)
            convert(1, 0, PRE)
            load_state(0)
            rest_of_layer(0)
            load_wuq(1)
            for ti in range(5):
                phase_a(1, ti)
            save_state(1)
        elif stop == "s3":
            convert(1, SUF, WLP)
            load_state(1)
            rest_of_layer(1)
            final_y()
        mk.wait_all("sp", finals)
        stats = mk.emit(nc)
    return nc, stats


_CACHE = {}
DEFAULT_MODE = "staged"


def _consts(r):
    c = np.zeros((128, 528), np.float32)
    c[:, 0:128] = np.eye(128)
    c[:, 128:256] = 1.0
    RT = np.zeros((64, 64), np.float32)
    for m in range(32):
        RT[m + 32, m] = -1.0
    for m in range(32, 64):
        RT[m - 32, m] = 1.0
    c[0:64, 256:320] = RT
    s_ = np.arange(128)
    c[:, 320:448] = (s_[:, None] <= s_[None, :]).astype(np.float32)
    s6 = np.arange(64)
    c[0:64, 448:512] = ((s6[:, None] <= s6[None, :]) & (s6[:, None] // 16 == s6[None, :] // 16)).astype(np.float32)
    c[:, 512:520] = np.where(np.arange(8)[None, :] < r, 0.0, NEG)
    c[:, 520:528] = (np.arange(8)[None, :] == r).astype(np.float32)
    return c


def _rope_table(pos):
    inv = (10000.0 ** (-np.arange(32, dtype=np.float32) / np.float32(32))).astype(np.float32)
    ang = pos.astype(np.float32)[None, :] * inv[:, None]
    cos = np.cos(ang).astype(np.float32); sin = np.sin(ang).astype(np.float32)
    return np.concatenate([np.concatenate([cos, cos], 0), np.concatenate([sin, sin], 0)], 1)


def _get(stage):
    if stage not in _CACHE:
        _CACHE[stage] = build(stage)
    return _CACHE[stage][0]


def kernel(**inp):
    mode = os.environ.get("MK_MODE", DEFAULT_MODE)
    f = lambda k: np.asarray(inp[k], np.float32)
    xp = f("x_prompt")[0]; xs = f("x_sample").reshape(512, D)
    pp = f("p_prompt")[:, 0]; psm = f("p_sample").reshape(DEPTH, 512, PLE)
    cc = f("cache_kv_latent"); ck = f("cache_k_rope")
    names = {"f1g": "ffn1_w_gate", "f1u": "ffn1_w_up", "f1d": "ffn1_w_down", "win": "w_in", "wuq": "w_uq", "wuk": "w_uk", "wuv": "w_uv",
             "wo": "w_o", "f2g": "ffn2_w_gate", "f2u": "ffn2_w_up", "f2d": "ffn2_w_down", "pg": "ple_w_gate", "pp": "ple_w_proj"}
    slab = np.zeros((DEPTH, 128, WLP), np.float32)
    for n_, (off, nk, C) in WOFF.items():
        w = f(names[n_]).reshape(DEPTH, nk, 128, C)
        slab[:, :, off:off + nk * C] = w.transpose(0, 2, 1, 3).reshape(DEPTH, 128, nk * C)
    gains = np.zeros((128, DEPTH * NG), np.float32)
    gsrc = {"f1n": ("ffn1_norm", 8), "mixn": ("mix_norm", 8), "qan": ("q_a_norm", 3), "qnn": ("q_nope_norm", 1), "qrn": ("q_rope_norm", 1),
            "kvan": ("kv_a_norm", 4), "krn": ("k_rope_norm", 1), "knn": ("k_nope_norm", 1), "f2n": ("ffn2_norm", 8), "plen": ("ple_norm", 8)}
    for l in range(DEPTH):
        for g_, (nm, ncol) in gsrc.items():
            v = f(nm)[l]
            if v.shape[0] == 64:
                gains[0:64, l * NG + GCOLS[g_]] = v
            else:
                gains[:, l * NG + GCOLS[g_]:l * NG + GCOLS[g_] + ncol] = v.reshape(ncol, 128).T
    gmtab = np.zeros((DEPTH, 128, 2560), np.float32)
    gmw = np.zeros((DEPTH, 128, 1536), np.float32)
    ws = f("gm_w_s"); bs = f("gm_b_s"); gv = f("gm_v_norm")
    for l in range(DEPTH):
        gmtab[l, :, 0:1024] = gv[l][None, :]
        for g in range(8):
            gmtab[l, :, 1024 + g * 128:1024 + (g + 1) * 128] = bs[l, g][None, :]
            gmtab[l, :, 2048 + g * 64:2048 + (g + 1) * 64] = np.tile(bs[l, g, :16], 4)[None, :]
            gmw[l, :, g * 128:(g + 1) * 128] = ws[l, g].T
            for b in range(4):
                gmw[l, b * 16:(b + 1) * 16, 1024 + g * 64 + b * 16:1024 + g * 64 + (b + 1) * 16] = ws[l, g, :16, :16].T
    base = []
    for r in range(8):
        blocks = [8 * s + r for s in range(4)]
        xin = np.concatenate([xp[b * 512:(b + 1) * 512] for b in blocks] + [xs[r * 64:(r + 1) * 64]], 0)
        pin = np.concatenate([np.concatenate([pp[:, b * 512:(b + 1) * 512] for b in blocks], 1), psm[:, r * 64:(r + 1) * 64]], 1)
        pos = np.concatenate([np.arange(b * 512, (b + 1) * 512) for b in blocks] + [PAST + np.arange(16)] * 4)
        base.append({"xin": np.ascontiguousarray(xin), "pin": np.ascontiguousarray(pin),
                     "cch": np.ascontiguousarray(cc[:, r * 4:(r + 1) * 4]), "ckr": np.ascontiguousarray(ck[:, r * 4:(r + 1) * 4]),
                     "slab": slab, "gains": gains, "gmtab": gmtab, "gmw": gmw, "cst": _consts(r), "rope_t": _rope_table(pos)})
    common = ["slab", "gains", "gmtab", "gmw", "cst", "rope_t"]
    outs = [dict() for _ in range(8)]
    if mode == "fused":
        res = run_bass_kernel_spmd(_get("full"), base, core_ids=list(range(8)))
        outs = res.results
    else:
        res1 = run_bass_kernel_spmd(_get("s1"), [{k: base[r][k] for k in common + ["xin"]} for r in range(8)], core_ids=list(range(8))).results
        prev = res1
        for l, stage in ((0, "s2"), (1, "s3")):
            xco = np.concatenate([np.asarray(prev[r]["ownc%d" % l]).reshape(128, 8192) for r in range(8)], 0)
            xro = np.concatenate([np.asarray(prev[r]["ownr%d" % l]).reshape(64, 2048) for r in range(8)], 0)
            ims = []
            for r in range(8):
                m = {k: base[r][k] for k in common + ["pin", "cch", "ckr"]}
                for k in ("qs_n", "qs_r", "ownc", "ownr", "cbSd", "krSd"):
                    m["%s%d" % (k, l)] = np.asarray(prev[r]["%s%d" % (k, l)])
                m["xco%d" % l] = xco; m["xro%d" % l] = xro
                m["hst_i"] = np.asarray(prev[r]["hst_o"])
                ims.append(m)
            cur = run_bass_kernel_spmd(_get(stage), ims, core_ids=list(range(8))).results
            for r in range(8):
                outs[r].update({k: v for k, v in prev[r].items() if k.startswith(("c_o", "kr_o"))})
                outs[r].update({k: v for k, v in cur[r].items() if k.startswith(("c_o", "kr_o", "gv_o", "y_o"))})
            prev = cur
    y_p = np.zeros((1, SEQ, D), np.float32); y_s = np.zeros((32, DEC, D), np.float32)
    pc = np.zeros((DEPTH, 1, SEQ, KVL), np.float32); pk = np.zeros((DEPTH, 1, SEQ, ROPE), np.float32)
    sc = np.zeros((DEPTH, 32, DEC, KVL), np.float32); sk = np.zeros((DEPTH, 32, DEC, ROPE), np.float32)
    sv = np.zeros((DEPTH, 32, DEC, D), np.float32)
    for r in range(8):
        o = outs[r]
        yo = np.asarray(o["y_o"])
        for l in range(DEPTH):
            co = np.asarray(o["c_o%d" % l]); ko = np.asarray(o["kr_o%d" % l])
            for s in range(4):
                b = 8 * s + r
                pc[l, 0, b * 512:(b + 1) * 512] = co[s * 512:(s + 1) * 512]
                pk[l, 0, b * 512:(b + 1) * 512] = ko[s * 512:(s + 1) * 512]
            sc[l, r * 4:(r + 1) * 4] = co[2048:].reshape(4, DEC, KVL)
            sk[l, r * 4:(r + 1) * 4] = ko[2048:].reshape(4, DEC, ROPE)
            sv[l, r * 4:(r + 1) * 4] = np.asarray(o["gv_o%d" % l]).reshape(4, DEC, D)
        for s in range(4):
            b = 8 * s + r
            y_p[0, b * 512:(b + 1) * 512] = yo[s * 512:(s + 1) * 512]
        y_s[r * 4:(r + 1) * 4] = yo[2048:].reshape(4, DEC, D)
    return (y_p, y_s, pc, pk, sc, sk, sv)
```

```python
import contextlib
import os
import numpy as np
import ml_dtypes
import concourse.bass as bass
import concourse.mybir as mybir
from concourse.bass_utils import run_bass_kernel_spmd

F32 = mybir.dt.float32
BF16 = mybir.dt.bfloat16
AF = mybir.ActivationFunctionType
ALU = mybir.AluOpType
ENGS = ("pe", "act", "dve", "pool", "sp")

D = 1024; DFF = 2816; NH = 8; QLORA = 384; KVL = 512; ROPE = 64; DIN = 5056; PLE = 256
SEQ = 16384; PAST = 4096; DEC = 16; NB = 32; DEPTH = 2
NT = 2112
EPS = 1e-6
SCALE = 192 ** -0.5
TILES = [(0, 512), (512, 512), (1024, 512), (1536, 512), (2048, 64)]
NEG = -30000.0

WSHAPES = [("f1g", D, DFF), ("f1u", D, DFF), ("f1d", DFF, D), ("win", D, DIN), ("wuq", QLORA, 1536),
           ("wuk", KVL, 1024), ("wuv", KVL, 1024), ("wo", D, D), ("f2g", D, DFF), ("f2u", D, DFF),
           ("f2d", DFF, D), ("pg", D, D), ("pp", PLE, D)]
WOFF = {}
_o = 0
for _n, _r, _c in WSHAPES:
    WOFF[_n] = (_o, _r // 128, _c)
    _o += (_r // 128) * _c
WL = _o
WLP = ((WL + 511) // 512) * 512
GCOLS = {"f1n": 0, "mixn": 8, "qan": 16, "qnn": 19, "qrn": 20, "kvan": 21, "krn": 25, "knn": 26, "f2n": 27, "plen": 35}
NG = 43


class Buf:
    __slots__ = ("name", "lw", "rd", "dcnt", "pw")

    def __init__(self, name):
        self.name = name
        self.lw = None
        self.rd = []
        self.dcnt = 0
        self.pw = {}


class Op:
    __slots__ = ("fn", "waits", "inc", "seq", "isdma", "n")

    def __init__(self, fn, waits, inc, seq, isdma, n=1):
        self.fn = fn; self.waits = waits; self.inc = inc; self.seq = seq; self.isdma = isdma; self.n = n


class _Rec:
    def __getattr__(self, name):
        def f(*a, **k):
            return (name, a, k)
        return f


_REC = _Rec()


class MK:
    def __init__(self):
        self.ops = {e: [] for e in ENGS}
        self.seen = {e: {} for e in ENGS}
        self.needed = {e: set() for e in ENGS}
        self.dsems = {}
        self.ccnt = {e: 0 for e in ENGS}

    @staticmethod
    def _flat(bs):
        out = []
        for b in bs:
            if isinstance(b, (list, tuple)):
                out.extend(MK._flat(b))
            else:
                out.append(b)
        return out

    def _deps(self, eng, reads, writes, isdma):
        reads = self._flat(reads); writes = self._flat(writes)
        deps = []
        for b in reads:
            if b.lw is not None:
                deps.append(b.lw)
            for k, v in b.pw.items():
                deps.append((k, v, "dma"))
        for b in writes:
            if b.lw is not None and (isdma or b.lw[2] != eng):
                deps.append(b.lw)
            for r in b.rd:
                if isdma or r[2] != eng:
                    deps.append(r)
        seen = self.seen[eng]
        mx = {}
        for (key, val, _e) in deps:
            if seen.get(key, 0) < val:
                seen[key] = val
                mx[key] = val
        for key, val in mx.items():
            if key[0] == "E":
                self.needed[key[1]].add(val)
        return list(mx.items())

    def _commit(self, ev, reads, writes, pwrites=()):
        reads = self._flat(reads); writes = self._flat(writes); pwrites = self._flat(pwrites)
        for b in reads:
            b.rd.append(ev)
        for b in writes:
            b.lw = ev
            b.rd = []
            b.pw = {}
        for b in pwrites:
            b.pw[ev[0]] = max(b.pw.get(ev[0], 0), ev[1])

    def op(self, eng, fn, reads=(), writes=()):
        waits = self._deps(eng, reads, writes, False)
        self.ccnt[eng] += 1
        seq = self.ccnt[eng]
        ev = (("E", eng), seq, eng)
        self.ops[eng].append(Op(fn(_REC), waits, ("E", eng), seq, False))
        self._commit(ev, reads, writes)
        return ev

    def dma(self, eng, fn, sbuf, reads=(), writes=(), pwrites=(), n=1, inc=16):
        waits = self._deps(eng, reads, writes, True)
        self.dsems[sbuf.name] = True
        sbuf.dcnt += inc * n
        key = ("D", sbuf.name)
        ev = (key, sbuf.dcnt, "dma")
        self.ops[eng].append(Op(fn(_REC), waits, key, None, True, inc))
        self._commit(ev, reads, writes, pwrites)
        return ev

    def wait_all(self, eng, bufs):
        waits = self._deps(eng, bufs, (), True)
        self.ops[eng].append(Op(None, waits, None, None, False))

    def emit(self, nc):
        rank = {}
        for e in ENGS:
            for i, s in enumerate(sorted(self.needed[e])):
                rank[(e, s)] = i + 1
        with contextlib.ExitStack() as st:
            sems = {}
            for e in ENGS:
                sems[("E", e)] = st.enter_context(nc.semaphore("se_" + e))
            for i, name in enumerate(self.dsems):
                sems[("D", name)] = st.enter_context(nc.semaphore("sd%d" % i))
            block = st.enter_context(nc.Block())

            def run(engname):
                def body(eng):
                    for o in self.ops[engname]:
                        for (key, val) in o.waits:
                            v = rank[(key[1], val)] if key[0] == "E" else val
                            eng.wait_ge(sems[key], v)
                        if o.fn is None:
                            continue
                        if o.isdma:
                            for (nm_, a_, k_) in o.fn:
                                getattr(eng, nm_)(*a_, **k_).then_inc(sems[o.inc], o.n)
                        else:
                            nm_, a_, k_ = o.fn
                            r = getattr(eng, nm_)(*a_, **k_)
                            if (engname, o.seq) in rank:
                                r.then_inc(sems[o.inc], 1)
                return body

            block.tensor(run("pe"))
            block.scalar(run("act"))
            block.vector(run("dve"))
            block.gpsimd(run("pool"))
            block.sync(run("sp"))
        return {e: len(self.ops[e]) for e in ENGS}, len(self.dsems)


class T:
    def __init__(self, t, name):
        self.t = t
        self.b = Buf(name)


def build(stop="full"):
    nc = bass.Bass("TRN2", target_bir_lowering=False, num_devices=8)
    mk = MK()
    staged = stop in ("s1", "s2", "s3")
    KIND = {"in": "ExternalInput", "out": "ExternalOutput", "int": "Internal"}
    dt_ = lambda n, s, d, role: nc.dram_tensor(n, s, d, kind=KIND[role]).ap()
    di = lambda n, s, d=F32: dt_(n, s, d, "in")
    do = lambda n, s, d=F32: dt_(n, s, d, "out")
    dx = lambda n, s, d=BF16: dt_(n, s, d, "int")

    class PerLayer:
        def __init__(self, aps):
            self.aps = aps

        def __getitem__(self, idx):
            if isinstance(idx, tuple):
                return self.aps[idx[0]][idx[1:]] if len(idx) > 1 else self.aps[idx[0]]
            return self.aps[idx]

    def role(l, what):
        if not staged:
            return "out" if what in ("O", "V") else "int"
        prodA = {"s1": 0, "s2": 1}.get(stop)
        cons = {"s2": 0, "s3": 1}.get(stop)
        if what in ("A", "O"):
            if l == prodA:
                return "out"
            if l == cons and what == "A":
                return "in"
            return "int"
        if what == "G":
            return "in" if l == cons else "int"
        if what == "V":
            return "out" if l == cons else "int"
        return "int"

    xin = di("xin", [NT, D]) if stop in ("full", "s1") or not staged else dx("xin", [NT, D], F32)
    big_role = "int" if stop == "s1" else "in"
    pin = dt_("pin", [DEPTH, NT, PLE], F32, big_role)
    cch = dt_("cch", [DEPTH, 4, PAST, KVL], F32, big_role); ckr = dt_("ckr", [DEPTH, 4, PAST, ROPE], F32, big_role)
    slab = di("slab", [DEPTH, 128, WLP]); gains = di("gains", [128, DEPTH * NG])
    gmtab = di("gmtab", [DEPTH, 128, 2560]); gmw = di("gmw", [DEPTH, 128, 1536])
    cst = di("cst", [128, 528])
    rope_t = di("rope_t", [64, 2 * NT])
    y_o = dt_("y_o", [NT, D], F32, "out" if stop in ("full", "s3") or not staged else "int")
    c_o = PerLayer([dt_("c_o%d" % l, [NT, KVL], F32, role(l, "O")) for l in range(DEPTH)])
    kr_o = PerLayer([dt_("kr_o%d" % l, [NT, ROPE], F32, role(l, "O")) for l in range(DEPTH)])
    gv_o = PerLayer([dt_("gv_o%d" % l, [64, D], F32, role(l, "V")) for l in range(DEPTH)])
    wbf = dx("wbf", [DEPTH, 128, WLP])
    qs_n = PerLayer([dt_("qs_n%d" % l, [NH, 128, NT], BF16, role(l, "A")) for l in range(DEPTH)])
    qs_r = PerLayer([dt_("qs_r%d" % l, [NH, 64, NT], BF16, role(l, "A")) for l in range(DEPTH)])
    ownc = PerLayer([dt_("ownc%d" % l, [128, 4, 2048], BF16, role(l, "A")) for l in range(DEPTH)])
    ownr = PerLayer([dt_("ownr%d" % l, [64, 4, 512], BF16, role(l, "A")) for l in range(DEPTH)])
    cbSd = PerLayer([dt_("cbSd%d" % l, [128, 4, 64], BF16, role(l, "A")) for l in range(DEPTH)])
    krSd = PerLayer([dt_("krSd%d" % l, [64, 64], BF16, role(l, "A")) for l in range(DEPTH)])
    xc = [dx("xc%d" % l, [8 * 128, 4 * 2048]) for l in range(DEPTH)]
    xr = [dx("xr%d" % l, [8 * 64, 4 * 512]) for l in range(DEPTH)]
    xco = [dt_("xco%d" % l, [8 * 128, 4 * 2048], BF16, role(l, "G")) for l in range(DEPTH)]
    xro = [dt_("xro%d" % l, [8 * 64, 4 * 512], BF16, role(l, "G")) for l in range(DEPTH)]
    hst_i = dt_("hst_i", [128, 8, NT], F32, "in" if stop in ("s2", "s3") else "int")
    hst_o = dt_("hst_o", [128, 8, NT], F32, "out" if stop in ("s1", "s2") else "int")
    kts = dx("kts", [DEPTH, 36, 128, NH, 512]); vs = dx("vs", [DEPTH, 36, 128, NH, 512])
    DB = {n: Buf(n) for n in ["wbf0", "wbf1", "qs0", "qs1", "xc0", "xc1", "xco0", "xco1", "xr0", "xr1", "xro0", "xro1",
                              "own0", "own1", "kv0", "kv1", "outs"]}

    with contextlib.ExitStack() as st:
        def sb(name, shape, dt=F32):
            return T(st.enter_context(nc.sbuf_tensor(name, shape, dt)), name)

        def ring(name, n, shape, dt=F32):
            return [sb("%s%d" % (name, i), shape, dt) for i in range(n)]

        class V:
            def __init__(self, t, b):
                self.t = t; self.b = b

        hT = sb("hT", [128, 8, NT])
        hB = [Buf("h%d" % i) for i in range(5)]
        BA = st.enter_context(nc.sbuf_tensor("BA", [128, 24 * 512], BF16))
        baB = [Buf("ba%d" % i) for i in range(24)]
        FA = st.enter_context(nc.sbuf_tensor("FA", [128, 7 * 512], F32))
        faB = [Buf("fa%d" % i) for i in range(7)]
        WA = st.enter_context(nc.sbuf_tensor("WA", [128, 16 * 512], BF16))
        waB = [Buf("wa%d" % i) for i in range(16)]

        def bav(lo, n, f=512):
            if n == 1:
                return V(BA[:, lo * 512:(lo + 1) * 512], baB[lo:lo + 1])
            return V(BA[:, lo * 512:(lo + n) * 512].rearrange("p (a f) -> p a f", f=f), baB[lo:lo + n])

        def fav(lo, n):
            return V(FA[:, lo * 512:(lo + n) * 512].rearrange("p (a f) -> p a f", f=512), faB[lo:lo + n])

        actT = bav(0, 22)
        qaT = bav(0, 3); cbP = bav(3, 4); mringT = bav(7, 4); krbP = bav(11, 1)
        qn_t = [bav(12, 1), bav(13, 1)]; qr_t = [bav(14, 1), bav(15, 1)]
        cbt = [bav(0, 4), bav(4, 4)]; kt8 = bav(8, 8); v8 = bav(16, 8)
        katt = [bav(0, 1), bav(1, 1), bav(2, 1)]; vatt = [bav(3, 1), bav(4, 1), bav(5, 1)]
        qnA = [bav(6, 1), bav(7, 1)]; oaT = bav(8, 8); kratt = [bav(16, 1), bav(17, 1), bav(18, 1)]; qrA = [bav(19, 1), bav(20, 1)]
        mxT = bav(0, 8); vtm = bav(16, 8, f=1024); pT = bav(22, 2)
        zq = fav(0, 3); zk = fav(3, 4)
        gmt = V(FA[:, 0:2560], faB[0:5])
        wuq = V(WA[:, 0:4608].rearrange("p (a f) -> p a f", f=1536), waB[0:9])
        wuk = V(WA[:, 0:4096].rearrange("p (a f) -> p a f", f=1024), waB[0:8])
        wuv = V(WA[:, 4096:8192].rearrange("p (a f) -> p a f", f=1024), waB[8:16])

        xn = ring("xn", 1, [128, 8, 512], BF16)
        wr = ring("wr", 20, [128, 512], BF16)
        t32 = ring("t32", 8, [128, 512])
        tb16 = ring("tb16", 5, [128, 512], BF16)
        big32 = ring("big32", 2, [128, 4, 512])
        krt = ring("krt", 2, [64, 512], BF16)
        cbS = sb("cbS", [128, 4, 64], BF16)
        krS = sb("krS", [64, 64], BF16)
        oaS = sb("oaS", [128, NH, 64], BF16)
        qSn = sb("qSn", [128, NH, 16], BF16)
        qSr = sb("qSr", [64, NH, 16], BF16)
        krr = sb("krr", [64, 512])
        gmwT = sb("gmwT", [128, 1536], BF16)
        gn = sb("gn", [128, DEPTH * NG])
        cs = sb("cs", [128, 528])
        csb = sb("csb", [128, 528], BF16)
        ropet = sb("ropet", [64, 1024])
        small = sb("small", [128, 16])
        smr = ring("smr", 4, [128, 8])
        dacc = ring("dacc", 1, [128, 512])
        ps = [T(st.enter_context(nc.psum_tensor("ps%d" % i, [128, 512], F32)), "ps%d" % i) for i in range(8)]
        held = [False] * 8
        pctr = [0]
        rctr = {}

        def pget(hold=False):
            for _ in range(16):
                i = pctr[0] % 8
                pctr[0] += 1
                if not held[i]:
                    if hold:
                        held[i] = True
                    return i
            raise RuntimeError("psum exhausted")

        def rel(*pis):
            for p in pis:
                held[p] = False

        def nxt(r, key):
            i = rctr.get(key, 0)
            rctr[key] = i + 1
            return r[i % len(r)]

        RT_b = csb.t[0:64, 256:320]
        zero_b = small.t[:, 1:2]

        def gcol(l, name, c=0, kp=128):
            o = l * NG + GCOLS[name] + c
            return gn.t[0:kp, o:o + 1]

        def sem_of(b):
            return b[0] if isinstance(b, list) else b

        def load(dst, dst_ap, src_ap, reads=()):
            mk.dma("sp", lambda e: [e.dma_start(out=dst_ap, in_=src_ap)], sem_of(dst.b), reads=list(reads), writes=[dst.b])

        def store(src, dst_ap, src_ap, dbuf, eng="pool"):
            mk.dma(eng, lambda e: [e.dma_start(out=dst_ap, in_=src_ap)], sem_of(src.b), reads=[src.b], pwrites=[dbuf])

        def wb(l, name):
            k = "wbf%d_%s" % (l, name)
            if k not in DB:
                DB[k] = Buf(k)
            return DB[k]

        def wname_of(col):
            for n_, (off_, nk_, C_) in WOFF.items():
                if off_ <= col < off_ + nk_ * C_:
                    return n_
            return "pad"

        def wslot(l, name, kc, c0, ncols):
            off, nk, C = WOFF[name]
            s = nxt(wr, "wr")
            a = off + kc * C + c0
            load(s, s.t[:, 0:ncols], wbf[l, :, a:a + ncols], reads=[wb(l, name)])
            return s

        def mm(pi, out_ap, lhsT, rhs, start, stop, reads):
            mk.op("pe", lambda e: e.matmul(out_ap, lhsT=lhsT, rhs=rhs, start=start, stop=stop), list(reads), [ps[pi].b])

        def act(func, out_ap, in_ap, reads, writes, **kw):
            mk.op("act", lambda e: e.activation(out=out_ap, in_=in_ap, func=func, **kw), reads, writes)

        def stt(out_ap, in0, scalar, in1, op0, op1, reads, writes):
            mk.op("dve", lambda e: e.scalar_tensor_tensor(out=out_ap, in0=in0, scalar=scalar, in1=in1, op0=op0, op1=op1), reads, writes)

        def tt(out_ap, in0, in1, op, reads, writes):
            mk.op("dve", lambda e: e.tensor_tensor(out=out_ap, in0=in0, in1=in1, op=op), reads, writes)

        def rstd_from(pi, kp, Tn, Dn):
            r = nxt(t32, "t32")
            act(AF.Ln, r.t[0:kp, 0:Tn], ps[pi].t[0:kp, 0:Tn], [ps[pi].b, small.b], [r.b], scale=1.0 / Dn, bias=small.t[0:kp, 0:1])
            act(AF.Exp, r.t[0:kp, 0:Tn], r.t[0:kp, 0:Tn], [r.b], [r.b], scale=-0.5)
            return r

        def sumsq(chunks, Tn):
            pi = pget(hold=True)
            n = len(chunks)
            for i, (ap, kp, bufs) in enumerate(chunks):
                sq = nxt(tb16, "tb16")
                act(AF.Square, sq.t[0:kp, 0:Tn], ap, bufs, [sq.b])
                mm(pi, ps[pi].t[:, 0:Tn], csb.t[0:kp, 128:256], sq.t[0:kp, 0:Tn], i == 0, i == n - 1, [sq.b, csb.b])
            return pi

        def rms_h(l, ti, gname):
            c0, Tn = TILES[ti]
            chunks = [(hT.t[:, c, c0:c0 + Tn], 128, [hB[ti]]) for c in range(8)]
            pi = sumsq(chunks, Tn)
            r = rstd_from(pi, 128, Tn, D)
            rel(pi)
            x = nxt(xn, "xn")
            for c in range(8):
                stt(x.t[:, c, 0:Tn], hT.t[:, c, c0:c0 + Tn], gcol(l, gname, c), r.t[:, 0:Tn], ALU.mult, ALU.mult, [hB[ti], r.b, gn.b], [x.b])
            return x

        def lin_group(l, wname, K, col0, mlist, rhs_fn, Tn, rbufs):
            ncols = max(co + M for co, M in mlist)
            pis = [pget(hold=True) for _ in mlist]
            for kc in range(K):
                s = wslot(l, wname, kc, col0, ncols)
                for j, (co, M) in enumerate(mlist):
                    mm(pis[j], ps[pis[j]].t[0:M, 0:Tn], s.t[:, co:co + M], rhs_fn(kc), kc == 0, kc == K - 1, [s.b] + rbufs)
            return pis

        C4 = [(0, 128), (128, 128), (256, 128), (384, 128)]

        def ffn(l, ti, nname, gname, uname, dname):
            c0, Tn = TILES[ti]
            x = rms_h(l, ti, nname)
            for g4 in range(6):
                nch = 4 if g4 < 5 else 2
                gs = [wslot(l, gname, kc, g4 * 512, nch * 128) for kc in range(8)]
                us = [wslot(l, uname, kc, g4 * 512, nch * 128) for kc in range(8)]
                for j in range(nch):
                    pg_ = pget(hold=True); pu_ = pget(hold=True)
                    for kc in range(8):
                        mm(pg_, ps[pg_].t[:, 0:Tn], gs[kc].t[:, j * 128:(j + 1) * 128], x.t[:, kc, 0:Tn], kc == 0, kc == 7, [gs[kc].b, x.b])
                    for kc in range(8):
                        mm(pu_, ps[pu_].t[:, 0:Tn], us[kc].t[:, j * 128:(j + 1) * 128], x.t[:, kc, 0:Tn], kc == 0, kc == 7, [us[kc].b, x.b])
                    sg = nxt(t32, "t32")
                    act(AF.Silu, sg.t[:, 0:Tn], ps[pg_].t[:, 0:Tn], [ps[pg_].b], [sg.b])
                    jj = g4 * 4 + j
                    tt(actT.t[:, jj, 0:Tn], sg.t[:, 0:Tn], ps[pu_].t[:, 0:Tn], ALU.mult, [ps[pu_].b, sg.b], [actT.b])
                    rel(pg_, pu_)
            for half in range(2):
                pis = [pget(hold=True) for _ in range(4)]
                for j in range(22):
                    s = wslot(l, dname, j, half * 512, 512)
                    for m in range(4):
                        mm(pis[m], ps[pis[m]].t[:, 0:Tn], s.t[:, m * 128:(m + 1) * 128], actT.t[:, j, 0:Tn], j == 0, j == 21, [s.b, actT.b])
                for m in range(4):
                    c = half * 4 + m
                    stt(hT.t[:, c, c0:c0 + Tn], ps[pis[m]].t[:, 0:Tn], 0.5, hT.t[:, c, c0:c0 + Tn], ALU.mult, ALU.add, [ps[pis[m]].b, hB[ti]], [hB[ti]])
                    rel(pis[m])

        def rope_apply(src32, srcb, Tn, dst_list):
            pr = pget(hold=True)
            mm(pr, ps[pr].t[0:64, 0:Tn], RT_b, srcb.t[0:64, 0:Tn], True, True, [srcb.b, csb.b])
            t1 = nxt(t32, "t32"); t2 = nxt(t32, "t32")
            tt(t1.t[0:64, 0:Tn], src32.t[0:64, 0:Tn], ropet.t[:, 0:Tn], ALU.mult, [src32.b, ropet.b], [t1.b])
            tt(t2.t[0:64, 0:Tn], ps[pr].t[0:64, 0:Tn], ropet.t[:, 512:512 + Tn], ALU.mult, [ps[pr].b, ropet.b], [t2.b])
            rel(pr)
            for ap, tobj in dst_list:
                tt(ap, t1.t[0:64, 0:Tn], t2.t[0:64, 0:Tn], ALU.add, [t1.b, t2.b], [tobj.b])

        def transpose_out(src_fn, nchunk, kp, Tn, dst_fn, dbuf, sbufs):
            for i in range((Tn + 127) // 128):
                nt_ = min(128, Tn - i * 128)
                o = nxt(big32, "big32")
                for cg in range(0, nchunk, 4):
                    ncg = min(4, nchunk - cg)
                    pi = pget(hold=True)
                    for c in range(ncg):
                        mk.op("pe", lambda e: e.transpose(ps[pi].t[0:nt_, c * kp:(c + 1) * kp], src_fn(cg + c)[:, i * 128:i * 128 + nt_], cs.t[0:kp, 0:kp]),
                              sbufs + [cs.b], [ps[pi].b])
                    act(AF.Copy, o.t[0:nt_, cg // 4, 0:ncg * kp], ps[pi].t[0:nt_, 0:ncg * kp], [ps[pi].b], [o.b])
                    rel(pi)
                W = nchunk * kp
                if W <= 512:
                    store(o, dst_fn(i, nt_), o.t[0:nt_, 0, 0:W], dbuf)
                else:
                    store(o, dst_fn(i, nt_).rearrange("t (a f) -> t a f", f=512), o.t[0:nt_, 0:W // 512, :], dbuf)

        def expand(l, ct_t, ct_b, kp, k8, v8_):
            for h in range(NH):
                pk = pget(hold=True)
                for kc in range(4):
                    mm(pk, ps[pk].t[:, 0:kp], wuk.t[:, kc, h * 128:(h + 1) * 128], ct_t[:, kc, 0:kp], kc == 0, kc == 3, [wuk.b, ct_b])
                pq = sumsq([(ps[pk].t[:, 0:kp], 128, [ps[pk].b])], kp)
                r = rstd_from(pq, 128, kp, 128)
                rel(pq)
                stt(k8.t[:, h, 0:kp], ps[pk].t[:, 0:kp], gcol(l, "knn"), r.t[:, 0:kp], ALU.mult, ALU.mult, [ps[pk].b, r.b, gn.b], [k8.b])
                rel(pk)
            for sub in range((kp + 127) // 128):
                nk = min(128, kp - sub * 128)
                for hf in range(2):
                    pv = pget(hold=True)
                    for kc in range(4):
                        mm(pv, ps[pv].t[0:nk, :], ct_t[:, kc, sub * 128:sub * 128 + nk], wuv.t[:, kc, hf * 512:(hf + 1) * 512], kc == 0, kc == 3, [wuv.b, ct_b])
                    act(AF.Copy, v8_.t[0:nk, hf * 4:(hf + 1) * 4, sub * 128:(sub + 1) * 128], ps[pv].t[0:nk, :].rearrange("p (h d) -> p h d", h=4), [ps[pv].b], [v8_.b])
                    rel(pv)

        LOOKAHEAD = 2

        def run_steps(po, pd, steps, auto=False, dacc=None):
            it = iter(steps)
            pend = []
            state = {"n": 0}

            def do_pv(pP, t_, last):
                kp, nq, oc0 = t_["kp"], t_["nq"], t_["oc0"]
                st_, sp_ = t_["start"], t_["stop"]
                i = state["n"]; state["n"] += 1
                if auto:
                    st_ = (i == 0)
                    sp_ = last
                p = nxt(tb16, "tb16")
                act(AF.Exp, p.t[0:kp, 0:nq], ps[pP].t[0:kp, 0:nq], [ps[pP].b, cs.b, small.b], [p.b], scale=SCALE, bias=t_["bias"])
                rel(pP)
                mm(po, ps[po].t[:, oc0:oc0 + nq], t_["V"], p.t[0:kp, 0:nq], st_, sp_, t_["vreads"] + [p.b])
                if dacc is None:
                    mm(pd, ps[pd].t[:, oc0:oc0 + nq], csb.t[0:kp, 128:256], p.t[0:kp, 0:nq], st_, sp_, [csb.b, p.b])
                else:
                    a = dacc[0]
                    if i == 0:
                        assert kp == 128 and nq == 512 and oc0 == 0
                        mk.op("dve", lambda e: e.tensor_copy(out=a.t[:, :], in_=p.t[:, :]), [p.b], [a.b])
                    else:
                        mk.op("dve", lambda e: e.tensor_tensor(out=a.t[0:kp, oc0:oc0 + nq], in0=a.t[0:kp, oc0:oc0 + nq], in1=p.t[0:kp, 0:nq], op=ALU.add), [a.b, p.b], [a.b])

            while True:
                s = next(it, None)
                if s is not None:
                    pS = pget(hold=True)
                    kp, nq = s["kp"], s["nq"]
                    mm(pS, ps[pS].t[0:kp, 0:nq], s["K"], s["qn"], True, False, s["kreads"])
                    mm(pS, ps[pS].t[0:kp, 0:nq], s["KR"], s["qr"], False, True, s["kreads"])
                    pend.append((pS, s))
                    if len(pend) > LOOKAHEAD:
                        pP, t_ = pend.pop(0)
                        do_pv(pP, t_, False)
                else:
                    while pend:
                        pP, t_ = pend.pop(0)
                        do_pv(pP, t_, len(pend) == 0)
                    break

        load(cs, cs.t[:], cst)
        load(gn, gn.t[:], gains)
        mk.op("dve", lambda e: e.tensor_copy(out=csb.t[:], in_=cs.t[:]), [cs.b], [csb.b])
        mk.op("pool", lambda e: e.memset(small.t[:], 0.0), [], [small.b])
        mk.op("pool", lambda e: e.memset(small.t[:, 0:1], EPS), [], [small.b])
        cast_engs = ["act", "dve", "act", "dve", "pool"]
        cstate = [0]

        def convert(l, col_lo, col_hi):
            for c in range(col_lo // 512, (col_hi + 511) // 512):
                a = nxt(t32, "t32"); b_ = nxt(wr, "wr")
                load(a, a.t[:], slab[l, :, c * 512:(c + 1) * 512])
                eng = cast_engs[cstate[0] % 5]; cstate[0] += 1
                if eng == "act":
                    act(AF.Copy, b_.t[:], a.t[:], [a.b], [b_.b])
                else:
                    mk.op(eng, lambda e: e.tensor_copy(out=b_.t[:], in_=a.t[:]), [a.b], [b_.b])
                store(b_, wbf[l, :, c * 512:(c + 1) * 512], b_.t[:], wb(l, wname_of(c * 512)), eng="pool")

        PRE = WOFF["wuk"][0]
        SUF = WOFF["win"][0]

        def load_wuq(l):
            offs = [WOFF["wuq"][0] + kc * 1536 for kc in range(3)]
            mk.dma("sp", lambda e: [e.dma_start(out=wuq.t[:, kc, :], in_=wbf[l, :, offs[kc]:offs[kc] + 1536]) for kc in range(3)],
                   sem_of(wuq.b), reads=[wb(l, "wuq")], writes=[wuq.b], n=3)

        def load_wukv(l):
            for w, nm in ((wuk, "wuk"), (wuv, "wuv")):
                offs = [WOFF[nm][0] + kc * 1024 for kc in range(4)]
                mk.dma("sp", lambda e: [e.dma_start(out=w.t[:, kc, :], in_=wbf[l, :, offs[kc]:offs[kc] + 1024]) for kc in range(4)],
                       sem_of(w.b), reads=[wb(l, nm)], writes=[w.b], n=4)

        def load_gmw(l):
            a = nxt(big32, "big32")
            load(a, a.t[:, 0:3, :], gmw[l].rearrange("p (a f) -> p a f", f=512))
            for g in range(8):
                tt(gmwT.t[:, g * 128:(g + 1) * 128], a.t[:, g // 4, (g % 4) * 128:(g % 4 + 1) * 128], cs.t[:, 320:448], ALU.mult, [a.b, cs.b], [gmwT.b])
                tt(gmwT.t[0:64, 1024 + g * 64:1024 + (g + 1) * 64], a.t[0:64, 2, g * 64:(g + 1) * 64], cs.t[0:64, 448:512], ALU.mult, [a.b, cs.b], [gmwT.b])

        def load_x(ti):
            c0, Tn = TILES[ti]
            for i in range((Tn + 127) // 128):
                nt_ = min(128, Tn - i * 128)
                a = nxt(big32, "big32")
                load(a, a.t[0:nt_, 0:2, :], xin[c0 + i * 128:c0 + i * 128 + nt_, :].rearrange("t (a f) -> t a f", f=512))
                for cg in range(2):
                    pi = pget(hold=True)
                    for c in range(4):
                        mk.op("pe", lambda e: e.transpose(ps[pi].t[:, c * 128:c * 128 + nt_], a.t[0:nt_, cg, c * 128:(c + 1) * 128], cs.t[0:nt_, 0:nt_]),
                              [a.b, cs.b], [ps[pi].b])
                    act(AF.Copy, hT.t[:, cg * 4:(cg + 1) * 4, c0 + i * 128:c0 + i * 128 + nt_],
                        ps[pi].t[:].rearrange("p (c f) -> p c f", f=128)[:, :, 0:nt_], [ps[pi].b], [hB[ti]])
                    rel(pi)

        def phase_a(l, ti):
            c0, Tn = TILES[ti]
            prompt = ti < 4
            if l == 0:
                load_x(ti)
            ffn(l, ti, "f1n", "f1g", "f1u", "f1d")
            x = rms_h(l, ti, "mixn")
            rf = lambda kc: x.t[:, kc, 0:Tn]
            p1 = lin_group(l, "win", 8, 0, C4, rf, Tn, [x.b])
            for j in range(3):
                act(AF.Copy, zq.t[:, j, 0:Tn], ps[p1[j]].t[:, 0:Tn], [ps[p1[j]].b], [zq.b])
            act(AF.Copy, zk.t[:, 0, 0:Tn], ps[p1[3]].t[:, 0:Tn], [ps[p1[3]].b], [zk.b])
            rel(*p1)
            p2 = lin_group(l, "win", 8, 512, [(0, 128), (128, 128), (256, 128), (384, 64)], rf, Tn, [x.b])
            for j in range(3):
                act(AF.Copy, zk.t[:, 1 + j, 0:Tn], ps[p2[j]].t[:, 0:Tn], [ps[p2[j]].b], [zk.b])
            act(AF.Copy, krr.t[0:64, 0:Tn], ps[p2[3]].t[0:64, 0:Tn], [ps[p2[3]].b], [krr.b])
            rel(*p2)
            mk.dma("sp", lambda e: [e.dma_start(out=ropet.t[:, 0:Tn], in_=rope_t[:, c0:c0 + Tn]), e.dma_start(out=ropet.t[:, 512:512 + Tn], in_=rope_t[:, NT + c0:NT + c0 + Tn])],
                   ropet.b, writes=[ropet.b], n=2)
            pi = sumsq([(zq.t[:, j, 0:Tn], 128, [zq.b]) for j in range(3)], Tn)
            r = rstd_from(pi, 128, Tn, QLORA)
            rel(pi)
            for j in range(3):
                stt(qaT.t[:, j, 0:Tn], zq.t[:, j, 0:Tn], gcol(l, "qan", j), r.t[:, 0:Tn], ALU.mult, ALU.mult, [zq.b, r.b, gn.b], [qaT.b])
            for h in range(NH):
                pn = pget(hold=True)
                for kc in range(3):
                    mm(pn, ps[pn].t[:, 0:Tn], wuq.t[:, kc, h * 192:h * 192 + 128], qaT.t[:, kc, 0:Tn], kc == 0, kc == 2, [wuq.b, qaT.b])
                pr_ = pget(hold=True)
                for kc in range(3):
                    mm(pr_, ps[pr_].t[0:64, 0:Tn], wuq.t[:, kc, h * 192 + 128:h * 192 + 192], qaT.t[:, kc, 0:Tn], kc == 0, kc == 2, [wuq.b, qaT.b])
                pq = sumsq([(ps[pn].t[:, 0:Tn], 128, [ps[pn].b])], Tn)
                r1 = rstd_from(pq, 128, Tn, 128)
                rel(pq)
                qo = nxt(qn_t, "qn_t")
                stt(qo.t[:, 0:Tn], ps[pn].t[:, 0:Tn], gcol(l, "qnn"), r1.t[:, 0:Tn], ALU.mult, ALU.mult, [ps[pn].b, r1.b, gn.b], [qo.b])
                rel(pn)
                store(qo, qs_n[l, h, :, c0:c0 + Tn], qo.t[:, 0:Tn], DB["qs%d" % l])
                pq = sumsq([(ps[pr_].t[0:64, 0:Tn], 64, [ps[pr_].b])], Tn)
                r2 = rstd_from(pq, 64, Tn, 64)
                rel(pq)
                x32 = nxt(t32, "t32"); xb = nxt(tb16, "tb16")
                stt(x32.t[0:64, 0:Tn], ps[pr_].t[0:64, 0:Tn], gcol(l, "qrn", 0, 64), r2.t[0:64, 0:Tn], ALU.mult, ALU.mult, [ps[pr_].b, r2.b, gn.b], [x32.b])
                rel(pr_)
                act(AF.Copy, xb.t[0:64, 0:Tn], x32.t[0:64, 0:Tn], [x32.b], [xb.b])
                qro = nxt(qr_t, "qr_t")
                rope_apply(x32, xb, Tn, [(qro.t[0:64, 0:Tn], qro)])
                store(qro, qs_r[l, h, :, c0:c0 + Tn], qro.t[0:64, 0:Tn], DB["qs%d" % l])
            pi = sumsq([(zk.t[:, j, 0:Tn], 128, [zk.b]) for j in range(4)], Tn)
            r = rstd_from(pi, 128, Tn, KVL)
            rel(pi)
            for j in range(4):
                stt(zk.t[:, j, 0:Tn], zk.t[:, j, 0:Tn], gcol(l, "kvan", j), r.t[:, 0:Tn], ALU.mult, ALU.mult, [zk.b, r.b, gn.b], [zk.b])
            cb = cbP if prompt else cbS
            for j in range(4):
                act(AF.Copy, cb.t[:, j, 0:Tn], zk.t[:, j, 0:Tn], [zk.b], [cb.b])
            transpose_out(lambda c: zk.t[:, c, 0:Tn], 4, 128, Tn, lambda i, n: c_o[l, c0 + i * 128:c0 + i * 128 + n, :], DB["outs"], [zk.b])
            if prompt:
                store(cb, ownc[l, :, ti, :].rearrange("p (a f) -> p a f", f=512), cb.t[:, :, :], DB["own%d" % l])
                for rr in range(0 if staged else 8):
                    mk.op("dve", lambda e: e.tensor_scalar_mul(out=mringT.t[:], in0=cb.t[:], scalar1=cs.t[:, 520 + rr:521 + rr]), [cb.b, cs.b], [mringT.b])
                    store(mringT, xc[l][rr * 128:(rr + 1) * 128, ti * 2048:(ti + 1) * 2048].rearrange("p (a f) -> p a f", f=512), mringT.t[:], DB["xc%d" % l])
            pi = sumsq([(krr.t[0:64, 0:Tn], 64, [krr.b])], Tn)
            r = rstd_from(pi, 64, Tn, 64)
            rel(pi)
            stt(krr.t[0:64, 0:Tn], krr.t[0:64, 0:Tn], gcol(l, "krn", 0, 64), r.t[0:64, 0:Tn], ALU.mult, ALU.mult, [krr.b, r.b, gn.b], [krr.b])
            kb_ = nxt(tb16, "tb16")
            act(AF.Copy, kb_.t[0:64, 0:Tn], krr.t[0:64, 0:Tn], [krr.b], [kb_.b])
            kr32 = nxt(t32, "t32")
            krb = krbP if prompt else krS
            rope_apply(krr, kb_, Tn, [(kr32.t[0:64, 0:Tn], kr32), (krb.t[0:64, 0:Tn], krb)])
            transpose_out(lambda c: kr32.t[0:64, 0:Tn], 1, 64, Tn, lambda i, n: kr_o[l, c0 + i * 128:c0 + i * 128 + n, :], DB["outs"], [kr32.b])
            if prompt:
                store(krb, ownr[l, :, ti, :], krb.t[0:64, :], DB["own%d" % l])
                for rr in range(0 if staged else 8):
                    m = nxt(tb16, "tb16")
                    mk.op("dve", lambda e: e.tensor_scalar_mul(out=m.t[0:64, :], in0=krb.t[0:64, :], scalar1=cs.t[0:64, 520 + rr:521 + rr]), [krb.b, cs.b], [m.b])
                    store(m, xr[l][rr * 64:(rr + 1) * 64, ti * 512:(ti + 1) * 512], m.t[0:64, :], DB["xr%d" % l])

        ccb = [Buf("cc%d" % i) for i in range(4)]

        def exchange(l):
            mk.dma("pool", lambda e: [e.collective_compute("AllReduce", ALU.add, replica_groups=[list(range(8))], ins=[xc[l]], outs=[xco[l]])],
                   ccb[2 * l], reads=[DB["xc%d" % l]], writes=[DB["xco%d" % l]], inc=1)
            mk.dma("pool", lambda e: [e.collective_compute("AllReduce", ALU.add, replica_groups=[list(range(8))], ins=[xr[l]], outs=[xro[l]])],
                   ccb[2 * l + 1], reads=[DB["xr%d" % l]], writes=[DB["xro%d" % l]], inc=1)
            mk.wait_all("pool", [DB["xco%d" % l], DB["xro%d" % l]])

        def expand_prompt(l):
            for blk in range(36):
                ct = nxt(cbt, "cbt")
                if blk < 32:
                    r_, s_ = blk % 8, blk // 8
                    load(ct, ct.t[:], xco[l][r_ * 128:(r_ + 1) * 128, s_ * 2048:(s_ + 1) * 2048].rearrange("p (a f) -> p a f", f=512), reads=[DB["xco%d" % l]])
                else:
                    load(ct, ct.t[:], ownc[l, :, blk - 32, :].rearrange("p (a f) -> p a f", f=512), reads=[DB["own%d" % l]])
                expand(l, ct.t, ct.b, 512, kt8, v8)
                store(kt8, kts[l, blk], kt8.t[:], DB["kv%d" % l])
                store(v8, vs[l, blk], v8.t[:], DB["kv%d" % l])

        def att_sample(l):
            for b in range(4):
                q0 = 2048 + 16 * b
                load(qSn, qSn.t[:], qs_n[l, :, :, q0:q0 + 16].rearrange("h p t -> p h t"), reads=[DB["qs%d" % l]])
                load(qSr, qSr.t[:], qs_r[l, :, :, q0:q0 + 16].rearrange("h p t -> p h t"), reads=[DB["qs%d" % l]])
                po = pget(hold=True); pd = pget(hold=True)
                for kb in range(9):
                    if kb < 8:
                        a = nxt(big32, "big32")
                        load(a, a.t[:], cch[l, b, kb * 512:(kb + 1) * 512, :].rearrange("(s p) f -> p s f", p=128))
                        ct = nxt(cbt, "cbt")
                        for ch in range(4):
                            pi = pget(hold=True)
                            for sub in range(4):
                                mk.op("pe", lambda e: e.transpose(ps[pi].t[:, sub * 128:(sub + 1) * 128], a.t[:, sub, ch * 128:(ch + 1) * 128], cs.t[:, 0:128]), [a.b, cs.b], [ps[pi].b])
                            act(AF.Copy, ct.t[:, ch, :], ps[pi].t[:, :], [ps[pi].b], [ct.b])
                            rel(pi)
                        a2 = nxt(t32, "t32")
                        load(a2, a2.t[:, 0:256].rearrange("p (s f) -> p s f", f=64), ckr[l, b, kb * 512:(kb + 1) * 512, :].rearrange("(s p) f -> p s f", p=128))
                        pi = pget(hold=True)
                        for sub in range(4):
                            mk.op("pe", lambda e: e.transpose(ps[pi].t[0:64, sub * 128:(sub + 1) * 128], a2.t[:, sub * 64:(sub + 1) * 64], cs.t[:, 0:128]), [a2.b, cs.b], [ps[pi].b])
                        krx = nxt(krt, "krt")
                        act(AF.Copy, krx.t[0:64, :], ps[pi].t[0:64, :], [ps[pi].b], [krx.b])
                        rel(pi)
                        kp = 512; ct_t, ct_b = ct.t, ct.b; kr_t, kr_b = krx.t, krx.b
                    else:
                        kp = 16; ct_t, ct_b = cbS.t[:, :, 16 * b:16 * b + 16], cbS.b
                        kr_t, kr_b = krS.t[:, 16 * b:16 * b + 16], krS.b
                    expand(l, ct_t, ct_b, kp, kt8, v8)
                    steps = []
                    for h in range(NH):
                        for sub in range((kp + 127) // 128):
                            nk = min(128, kp - sub * 128)
                            steps.append(dict(K=kt8.t[:, h, sub * 128:sub * 128 + nk], KR=kr_t[0:64, sub * 128:sub * 128 + nk], V=v8.t[0:nk, h, sub * 128:(sub + 1) * 128],
                                              kp=nk, nq=16, oc0=h * 16, qn=qSn.t[:, h, :], qr=qSr.t[0:64, h, :], bias=small.t[0:nk, 1:2],
                                              kreads=[kt8.b, kr_b, qSn.b, qSr.b], vreads=[v8.b], start=(kb == 0 and sub == 0), stop=(kb == 8)))
                    run_steps(po, pd, steps)
                r = nxt(t32, "t32")
                mk.op("dve", lambda e: e.reciprocal(out=r.t[:, 0:128], in_=ps[pd].t[:, 0:128]), [ps[pd].b], [r.b])
                tt(oaS.t[:, :, 16 * b:16 * b + 16], ps[po].t[:, 0:128].rearrange("p (h q) -> p h q", h=8), r.t[:, 0:128].rearrange("p (h q) -> p h q", h=8), ALU.mult, [ps[po].b, r.b], [oaS.b])
                rel(po, pd)

        def att_prompt(l, s_):
            c0 = s_ * 512
            blist = [(8 * s2 + r2, None, s2, r2) for s2 in range(s_) for r2 in range(8)] + [(8 * s_ + r2, r2, s_, r2) for r2 in range(8)] + [(32 + s_, "diag", s_, 0)]
            for h in range(NH):
                qn = nxt(qnA, "qnA"); qr = nxt(qrA, "qrA")
                load(qn, qn.t[:, :], qs_n[l, h, :, c0:c0 + 512], reads=[DB["qs%d" % l]])
                load(qr, qr.t[0:64, :], qs_r[l, h, :, c0:c0 + 512], reads=[DB["qs%d" % l]])
                po = pget(hold=True)

                def gen(qn=qn, qr=qr):
                    for (blk, mode, s2, r2) in blist:
                        ka = nxt(katt, "katt"); va = nxt(vatt, "vatt"); kra = nxt(kratt, "kratt")
                        load(ka, ka.t[:, :], kts[l, blk, :, h, :], reads=[DB["kv%d" % l]])
                        load(va, va.t[:, :], vs[l, blk, :, h, :], reads=[DB["kv%d" % l]])
                        if blk < 32:
                            load(kra, kra.t[0:64, :], xro[l][r2 * 64:(r2 + 1) * 64, s2 * 512:(s2 + 1) * 512], reads=[DB["xro%d" % l]])
                        else:
                            load(kra, kra.t[0:64, :], ownr[l, :, s_, :], reads=[DB["own%d" % l]])
                        kreads = [ka.b, kra.b, qn.b, qr.b]
                        for sub in range(4):
                            ks = slice(sub * 128, (sub + 1) * 128)
                            if mode != "diag":
                                bias = small.t[:, 1:2] if mode is None else cs.t[:, 512 + mode:513 + mode]
                                yield dict(K=ka.t[:, ks], KR=kra.t[0:64, ks], V=va.t[:, ks], kp=128, nq=512, oc0=0, qn=qn.t[:, :], qr=qr.t[0:64, :],
                                           bias=bias, kreads=kreads, vreads=[va.b], start=False, stop=False)
                            else:
                                qa_ = 128 * sub + 64
                                if qa_ < 512:
                                    yield dict(K=ka.t[:, ks], KR=kra.t[0:64, ks], V=va.t[:, ks], kp=128, nq=512 - qa_, oc0=qa_, qn=qn.t[:, qa_:512], qr=qr.t[0:64, qa_:512],
                                               bias=small.t[:, 1:2], kreads=kreads, vreads=[va.b], start=False, stop=False)
                                k2 = slice(sub * 128, sub * 128 + 64)
                                yield dict(K=ka.t[:, k2], KR=kra.t[0:64, k2], V=va.t[0:64, ks], kp=64, nq=64, oc0=128 * sub, qn=qn.t[:, 128 * sub:128 * sub + 64],
                                           qr=qr.t[0:64, 128 * sub:128 * sub + 64], bias=small.t[0:64, 1:2], kreads=kreads, vreads=[va.b], start=False, stop=False)

                run_steps(po, None, gen(), auto=True, dacc=dacc)
                pd = pget(hold=True)
                mm(pd, ps[pd].t[:, :], cs.t[:, 128:256], dacc[0].t[:, :], True, True, [cs.b, dacc[0].b])
                r = nxt(t32, "t32")
                mk.op("dve", lambda e: e.reciprocal(out=r.t[:, :], in_=ps[pd].t[:, :]), [ps[pd].b], [r.b])
                tt(oaT.t[:, h, :], ps[po].t[:, :], r.t[:, :], ALU.mult, [ps[po].b, r.b], [oaT.b])
                rel(po, pd)

        def phase_c(l, ti):
            c0, Tn = TILES[ti]
            prompt = ti < 4
            oa = oaT if prompt else oaS
            nsub = (Tn + 127) // 128
            load(gmt, gmt.t[:, :], gmtab[l])
            x = rms_h(l, ti, "mixn")
            rf = lambda kc: x.t[:, kc, 0:Tn]
            for i in range(nsub):
                nt_ = min(128, Tn - i * 128)
                pv = [pget(hold=True), pget(hold=True)]
                for hf in range(2):
                    for kc in range(8):
                        s = wslot(l, "win", kc, 1984 + hf * 512, 512)
                        mm(pv[hf], ps[pv[hf]].t[0:nt_, :], x.t[:, kc, i * 128:i * 128 + nt_], s.t[:, 0:512], kc == 0, kc == 7, [s.b, x.b])
                sm = nxt(smr, "smr")
                for hf in range(2):
                    sq = nxt(t32, "t32")
                    act(AF.Square, sq.t[0:nt_, :], ps[pv[hf]].t[0:nt_, :], [ps[pv[hf]].b], [sq.b, sm.b], accum_out=sm.t[0:nt_, hf:hf + 1])
                tt(sm.t[0:nt_, 2:3], sm.t[0:nt_, 0:1], sm.t[0:nt_, 1:2], ALU.add, [sm.b], [sm.b])
                act(AF.Sqrt, sm.t[0:nt_, 3:4], sm.t[0:nt_, 2:3], [sm.b, small.b], [sm.b], scale=1.0 / 1024, bias=small.t[0:nt_, 0:1])
                mk.op("dve", lambda e: e.reciprocal(out=sm.t[0:nt_, 4:5], in_=sm.t[0:nt_, 3:4]), [sm.b], [sm.b])
                for hf in range(2):
                    stt(vtm.t[0:nt_, i, hf * 512:(hf + 1) * 512], ps[pv[hf]].t[0:nt_, :], sm.t[0:nt_, 4:5], gmt.t[0:nt_, hf * 512:(hf + 1) * 512], ALU.mult, ALU.mult,
                        [ps[pv[hf]].b, sm.b, gmt.b], [vtm.b])
                if not prompt:
                    g32 = nxt(big32, "big32")
                    for hf in range(2):
                        stt(g32.t[0:nt_, hf, :], ps[pv[hf]].t[0:nt_, :], sm.t[0:nt_, 4:5], gmt.t[0:nt_, hf * 512:(hf + 1) * 512], ALU.mult, ALU.mult,
                            [ps[pv[hf]].b, sm.b, gmt.b], [g32.b])
                    store(g32, gv_o[l].rearrange("t (a f) -> t a f", f=512), g32.t[0:64, 0:2, :], DB["outs"])
                rel(*pv)
            for mg in range(2):
                pu = lin_group(l, "win", 8, 960 + mg * 512, C4, rf, Tn, [x.b])
                for j in range(4):
                    m = mg * 4 + j
                    pm = pget(hold=True)
                    t = nxt(t32, "t32")
                    for i in range(nsub):
                        nt_ = min(128, Tn - i * 128)
                        if prompt:
                            wsT = gmwT.t[:, m * 128:(m + 1) * 128]; bia = gmt.t[:, 1024 + m * 128:1024 + (m + 1) * 128]
                        else:
                            wsT = gmwT.t[0:64, 1024 + m * 64:1024 + (m + 1) * 64]; bia = gmt.t[:, 2048 + m * 64:2048 + (m + 1) * 64]
                        mm(pm, ps[pm].t[:, i * 128:i * 128 + nt_], vtm.t[0:nt_, i, m * 128:(m + 1) * 128], wsT, True, True, [vtm.b, gmwT.b])
                        tt(t.t[:, i * 128:i * 128 + nt_], ps[pm].t[:, i * 128:i * 128 + nt_], bia, ALU.add, [ps[pm].b, gmt.b], [t.b])
                    tt(mxT.t[:, m, 0:Tn], t.t[:, 0:Tn], ps[pu[j]].t[:, 0:Tn], ALU.mult, [t.b, ps[pu[j]].b], [mxT.b])
                    rel(pm, pu[j])
            for mg in range(2):
                pg_ = lin_group(l, "win", 8, 4032 + mg * 512, C4, rf, Tn, [x.b])
                for j in range(4):
                    m = mg * 4 + j
                    sg = nxt(t32, "t32")
                    act(AF.Sigmoid, sg.t[:, 0:Tn], ps[pg_[j]].t[:, 0:Tn], [ps[pg_[j]].b], [sg.b])
                    rel(pg_[j])
                    tt(mxT.t[:, m, 0:Tn], mxT.t[:, m, 0:Tn], sg.t[:, 0:Tn], ALU.mult, [mxT.b, sg.b], [mxT.b])
            for mg in range(2):
                pg_ = lin_group(l, "win", 8, 3008 + mg * 512, C4, rf, Tn, [x.b])
                for j in range(4):
                    m = mg * 4 + j
                    sg = nxt(t32, "t32")
                    act(AF.Sigmoid, sg.t[:, 0:Tn], ps[pg_[j]].t[:, 0:Tn], [ps[pg_[j]].b], [sg.b])
                    rel(pg_[j])
                    tt(sg.t[:, 0:Tn], sg.t[:, 0:Tn], oa.t[:, m, 0:Tn], ALU.mult, [sg.b, oa.b], [sg.b])
                    tt(mxT.t[:, m, 0:Tn], mxT.t[:, m, 0:Tn], sg.t[:, 0:Tn], ALU.add, [mxT.b, sg.b], [mxT.b])
            for half in range(2):
                pis = lin_group(l, "wo", 8, half * 512, C4, lambda kc: mxT.t[:, kc, 0:Tn], Tn, [mxT.b])
                for m in range(4):
                    c = half * 4 + m
                    tt(hT.t[:, c, c0:c0 + Tn], ps[pis[m]].t[:, 0:Tn], hT.t[:, c, c0:c0 + Tn], ALU.add, [ps[pis[m]].b, hB[ti]], [hB[ti]])
                    rel(pis[m])
            ffn(l, ti, "f2n", "f2g", "f2u", "f2d")
            x = rms_h(l, ti, "plen")
            for i in range(nsub):
                nt_ = min(128, Tn - i * 128)
                a = nxt(t32, "t32")
                load(a, a.t[0:nt_, 0:256], pin[l, c0 + i * 128:c0 + i * 128 + nt_, :])
                pi = pget(hold=True)
                for c in range(2):
                    mk.op("pe", lambda e: e.transpose(ps[pi].t[:, c * 128:c * 128 + nt_], a.t[0:nt_, c * 128:(c + 1) * 128], cs.t[0:nt_, 0:nt_]), [a.b, cs.b], [ps[pi].b])
                act(AF.Copy, pT.t[:, 0:2, i * 128:i * 128 + nt_], ps[pi].t[:, 0:256].rearrange("p (c f) -> p c f", f=128)[:, :, 0:nt_], [ps[pi].b], [pT.b])
                rel(pi)
            for q4 in range(4):
                C2 = [(0, 128), (128, 128)]
                pg_ = lin_group(l, "pg", 8, q4 * 256, C2, lambda kc: x.t[:, kc, 0:Tn], Tn, [x.b])
                pp_ = lin_group(l, "pp", 2, q4 * 256, C2, lambda kc: pT.t[:, kc, 0:Tn], Tn, [pT.b])
                for j in range(2):
                    c = q4 * 2 + j
                    sg = nxt(t32, "t32")
                    act(AF.Sigmoid, sg.t[:, 0:Tn], ps[pg_[j]].t[:, 0:Tn], [ps[pg_[j]].b], [sg.b])
                    tt(sg.t[:, 0:Tn], sg.t[:, 0:Tn], ps[pp_[j]].t[:, 0:Tn], ALU.mult, [sg.b, ps[pp_[j]].b], [sg.b])
                    tt(hT.t[:, c, c0:c0 + Tn], hT.t[:, c, c0:c0 + Tn], sg.t[:, 0:Tn], ALU.add, [hB[ti], sg.b], [hB[ti]])
                    rel(pg_[j], pp_[j])

        finals = [DB["outs"]]

        def save_state(l):
            mk.dma("pool", lambda e: [e.dma_start(out=hst_o, in_=hT.t[:])], hB[0], reads=[hB], pwrites=[DB["outs"]])
            store(cbS, cbSd[l], cbS.t[:], DB["outs"])
            store(krS, krSd[l], krS.t[:], DB["outs"])
            finals.extend([DB["qs%d" % l], DB["own%d" % l]])

        def load_state(l):
            mk.dma("sp", lambda e: [e.dma_start(out=hT.t[:], in_=hst_i)], hB[0], writes=[hB])
            load(cbS, cbS.t[:], cbSd[l])
            load(krS, krS.t[:], krSd[l])

        def rest_of_layer(l):
            load_wukv(l)
            load_gmw(l)
            att_sample(l)
            phase_c(l, 4)
            expand_prompt(l)
            for s_ in range(4):
                att_prompt(l, s_)
                phase_c(l, s_)

        def final_y():
            for ti in range(5):
                c0, Tn = TILES[ti]
                transpose_out(lambda c: hT.t[:, c, c0:c0 + Tn], 8, 128, Tn, lambda i, n: y_o[c0 + i * 128:c0 + i * 128 + n, :], DB["outs"], [hB[ti]])

        if stop == "full":
            for l in range(DEPTH):
                convert(l, 0, WLP)
            for l in range(DEPTH):
                load_wuq(l)
                for ti in range(5):
                    phase_a(l, ti)
                exchange(l)
                rest_of_layer(l)
            final_y()
        elif stop == "s1":
            convert(0, 0, PRE)
            load_wuq(0)
            for ti in range(5):
                phase_a(0, ti)
            save_state(0)
        elif stop == "s2":
            convert(0, SUF, WLP)
            convert(1, 0, PRE)
            load_state(0)
            rest_of_layer(0)
            load_wuq(1)
            for ti in range(5):
                phase_a(1, ti)
            save_state(1)
        elif stop == "s3":
            convert(1, SUF, WLP)
            load_state(1)
            rest_of_layer(1)
            final_y()
        mk.wait_all("sp", finals)
        stats = mk.emit(nc)
    return nc, stats


_CACHE = {}
DEFAULT_MODE = "staged"


def _consts(r):
    c = np.zeros((128, 528), np.float32)
    c[:, 0:128] = np.eye(128)
    c[:, 128:256] = 1.0
    RT = np.zeros((64, 64), np.float32)
    for m in range(32):
        RT[m + 32, m] = -1.0
    for m in range(32, 64):
        RT[m - 32, m] = 1.0
    c[0:64, 256:320] = RT
    s_ = np.arange(128)
    c[:, 320:448] = (s_[:, None] <= s_[None, :]).astype(np.float32)
    s6 = np.arange(64)
    c[0:64, 448:512] = ((s6[:, None] <= s6[None, :]) & (s6[:, None] // 16 == s6[None, :] // 16)).astype(np.float32)
    c[:, 512:520] = np.where(np.arange(8)[None, :] < r, 0.0, NEG)
    c[:, 520:528] = (np.arange(8)[None, :] == r).astype(np.float32)
    return c


def _rope_table(pos):
    inv = (10000.0 ** (-np.arange(32, dtype=np.float32) / np.float32(32))).astype(np.float32)
    ang = pos.astype(np.float32)[None, :] * inv[:, None]
    cos = np.cos(ang).astype(np.float32); sin = np.sin(ang).astype(np.float32)
    return np.concatenate([np.concatenate([cos, cos], 0), np.concatenate([sin, sin], 0)], 1)


def _get(stage):
    if stage not in _CACHE:
        _CACHE[stage] = build(stage)
    return _CACHE[stage][0]


def kernel(**inp):
    mode = os.environ.get("MK_MODE", DEFAULT_MODE)
    f = lambda k: np.asarray(inp[k], np.float32)
    xp = f("x_prompt")[0]; xs = f("x_sample").reshape(512, D)
    pp = f("p_prompt")[:, 0]; psm = f("p_sample").reshape(DEPTH, 512, PLE)
    cc = f("cache_kv_latent"); ck = f("cache_k_rope")
    names = {"f1g": "ffn1_w_gate", "f1u": "ffn1_w_up", "f1d": "ffn1_w_down", "win": "w_in", "wuq": "w_uq", "wuk": "w_uk", "wuv": "w_uv",
             "wo": "w_o", "f2g": "ffn2_w_gate", "f2u": "ffn2_w_up", "f2d": "ffn2_w_down", "pg": "ple_w_gate", "pp": "ple_w_proj"}
    slab = np.zeros((DEPTH, 128, WLP), np.float32)
    for n_, (off, nk, C) in WOFF.items():
        w = f(names[n_]).reshape(DEPTH, nk, 128, C)
        slab[:, :, off:off + nk * C] = w.transpose(0, 2, 1, 3).reshape(DEPTH, 128, nk * C)
    gains = np.zeros((128, DEPTH * NG), np.float32)
    gsrc = {"f1n": ("ffn1_norm", 8), "mixn": ("mix_norm", 8), "qan": ("q_a_norm", 3), "qnn": ("q_nope_norm", 1), "qrn": ("q_rope_norm", 1),
            "kvan": ("kv_a_norm", 4), "krn": ("k_rope_norm", 1), "knn": ("k_nope_norm", 1), "f2n": ("ffn2_norm", 8), "plen": ("ple_norm", 8)}
    for l in range(DEPTH):
        for g_, (nm, ncol) in gsrc.items():
            v = f(nm)[l]
            if v.shape[0] == 64:
                gains[0:64, l * NG + GCOLS[g_]] = v
            else:
                gains[:, l * NG + GCOLS[g_]:l * NG + GCOLS[g_] + ncol] = v.reshape(ncol, 128).T
    gmtab = np.zeros((DEPTH, 128, 2560), np.float32)
    gmw = np.zeros((DEPTH, 128, 1536), np.float32)
    ws = f("gm_w_s"); bs = f("gm_b_s"); gv = f("gm_v_norm")
    for l in range(DEPTH):
        gmtab[l, :, 0:1024] = gv[l][None, :]
        for g in range(8):
            gmtab[l, :, 1024 + g * 128:1024 + (g + 1) * 128] = bs[l, g][None, :]
            gmtab[l, :, 2048 + g * 64:2048 + (g + 1) * 64] = np.tile(bs[l, g, :16], 4)[None, :]
            gmw[l, :, g * 128:(g + 1) * 128] = ws[l, g].T
            for b in range(4):
                gmw[l, b * 16:(b + 1) * 16, 1024 + g * 64 + b * 16:1024 + g * 64 + (b + 1) * 16] = ws[l, g, :16, :16].T
    base = []
    for r in range(8):
        blocks = [8 * s + r for s in range(4)]
        xin = np.concatenate([xp[b * 512:(b + 1) * 512] for b in blocks] + [xs[r * 64:(r + 1) * 64]], 0)
        pin = np.concatenate([np.concatenate([pp[:, b * 512:(b + 1) * 512] for b in blocks], 1), psm[:, r * 64:(r + 1) * 64]], 1)
        pos = np.concatenate([np.arange(b * 512, (b + 1) * 512) for b in blocks] + [PAST + np.arange(16)] * 4)
        base.append({"xin": np.ascontiguousarray(xin), "pin": np.ascontiguousarray(pin),
                     "cch": np.ascontiguousarray(cc[:, r * 4:(r + 1) * 4]), "ckr": np.ascontiguousarray(ck[:, r * 4:(r + 1) * 4]),
                     "slab": slab, "gains": gains, "gmtab": gmtab, "gmw": gmw, "cst": _consts(r), "rope_t": _rope_table(pos)})
    common = ["slab", "gains", "gmtab", "gmw", "cst", "rope_t"]
    outs = [dict() for _ in range(8)]
    if mode == "fused":
        res = run_bass_kernel_spmd(_get("full"), base, core_ids=list(range(8)))
        outs = res.results
    else:
        res1 = run_bass_kernel_spmd(_get("s1"), [{k: base[r][k] for k in common + ["xin"]} for r in range(8)], core_ids=list(range(8))).results
        prev = res1
        for l, stage in ((0, "s2"), (1, "s3")):
            xco = np.concatenate([np.asarray(prev[r]["ownc%d" % l]).reshape(128, 8192) for r in range(8)], 0)
            xro = np.concatenate([np.asarray(prev[r]["ownr%d" % l]).reshape(64, 2048) for r in range(8)], 0)
            ims = []
            for r in range(8):
                m = {k: base[r][k] for k in common + ["pin", "cch", "ckr"]}
                for k in ("qs_n", "qs_r", "ownc", "ownr", "cbSd", "krSd"):
                    m["%s%d" % (k, l)] = np.asarray(prev[r]["%s%d" % (k, l)])
                m["xco%d" % l] = xco; m["xro%d" % l] = xro
                m["hst_i"] = np.asarray(prev[r]["hst_o"])
                ims.append(m)
            cur = run_bass_kernel_spmd(_get(stage), ims, core_ids=list(range(8))).results
            for r in range(8):
                outs[r].update({k: v for k, v in prev[r].items() if k.startswith(("c_o", "kr_o"))})
                outs[r].update({k: v for k, v in cur[r].items() if k.startswith(("c_o", "kr_o", "gv_o", "y_o"))})
            prev = cur
    y_p = np.zeros((1, SEQ, D), np.float32); y_s = np.zeros((32, DEC, D), np.float32)
    pc = np.zeros((DEPTH, 1, SEQ, KVL), np.float32); pk = np.zeros((DEPTH, 1, SEQ, ROPE), np.float32)
    sc = np.zeros((DEPTH, 32, DEC, KVL), np.float32); sk = np.zeros((DEPTH, 32, DEC, ROPE), np.float32)
    sv = np.zeros((DEPTH, 32, DEC, D), np.float32)
    for r in range(8):
        o = outs[r]
        yo = np.asarray(o["y_o"])
        for l in range(DEPTH):
            co = np.asarray(o["c_o%d" % l]); ko = np.asarray(o["kr_o%d" % l])
            for s in range(4):
                b = 8 * s + r
                pc[l, 0, b * 512:(b + 1) * 512] = co[s * 512:(s + 1) * 512]
                pk[l, 0, b * 512:(b + 1) * 512] = ko[s * 512:(s + 1) * 512]
            sc[l, r * 4:(r + 1) * 4] = co[2048:].reshape(4, DEC, KVL)
            sk[l, r * 4:(r + 1) * 4] = ko[2048:].reshape(4, DEC, ROPE)
            sv[l, r * 4:(r + 1) * 4] = np.asarray(o["gv_o%d" % l]).reshape(4, DEC, D)
        for s in range(4):
            b = 8 * s + r
            y_p[0, b * 512:(b + 1) * 512] = yo[s * 512:(s + 1) * 512]
        y_s[r * 4:(r + 1) * 4] = yo[2048:].reshape(4, DEC, D)
    return (y_p, y_s, pc, pk, sc, sk, sv)
```

```python
import contextlib
import os
import numpy as np
import ml_dtypes
import concourse.bass as bass
import concourse.mybir as mybir
from concourse.bass_utils import run_bass_kernel_spmd

F32 = mybir.dt.float32
BF16 = mybir.dt.bfloat16
AF = mybir.ActivationFunctionType
ALU = mybir.AluOpType
ENGS = ("pe", "act", "dve", "pool", "sp")

D = 1024; DFF = 2816; NH = 8; QLORA = 384; KVL = 512; ROPE = 64; DIN = 5056; PLE = 256
SEQ = 16384; PAST = 4096; DEC = 16; NB = 32; DEPTH = 2
NT = 2112
EPS = 1e-6
SCALE = 192 ** -0.5
TILES = [(0, 512), (512, 512), (1024, 512), (1536, 512), (2048, 64)]
NEG = -30000.0

WSHAPES = [("f1g", D, DFF), ("f1u", D, DFF), ("f1d", DFF, D), ("win", D, DIN), ("wuq", QLORA, 1536),
           ("wuk", KVL, 1024), ("wuv", KVL, 1024), ("wo", D, D), ("f2g", D, DFF), ("f2u", D, DFF),
           ("f2d", DFF, D), ("pg", D, D), ("pp", PLE, D)]
WOFF = {}
_o = 0
for _n, _r, _c in WSHAPES:
    WOFF[_n] = (_o, _r // 128, _c)
    _o += (_r // 128) * _c
WL = _o
WLP = ((WL + 511) // 512) * 512
GCOLS = {"f1n": 0, "mixn": 8, "qan": 16, "qnn": 19, "qrn": 20, "kvan": 21, "krn": 25, "knn": 26, "f2n": 27, "plen": 35}
NG = 43


class Buf:
    __slots__ = ("name", "lw", "rd", "dcnt", "pw")

    def __init__(self, name):
        self.name = name
        self.lw = None
        self.rd = []
        self.dcnt = 0
        self.pw = {}


class Op:
    __slots__ = ("fn", "waits", "inc", "seq", "isdma", "n")

    def __init__(self, fn, waits, inc, seq, isdma, n=1):
        self.fn = fn; self.waits = waits; self.inc = inc; self.seq = seq; self.isdma = isdma; self.n = n


class _Rec:
    def __getattr__(self, name):
        def f(*a, **k):
            return (name, a, k)
        return f


_REC = _Rec()


class MK:
    def __init__(self):
        self.ops = {e: [] for e in ENGS}
        self.seen = {e: {} for e in ENGS}
        self.needed = {e: set() for e in ENGS}
        self.dsems = {}
        self.ccnt = {e: 0 for e in ENGS}

    @staticmethod
    def _flat(bs):
        out = []
        for b in bs:
            if isinstance(b, (list, tuple)):
                out.extend(MK._flat(b))
            else:
                out.append(b)
        return out

    def _deps(self, eng, reads, writes, isdma):
        reads = self._flat(reads); writes = self._flat(writes)
        deps = []
        for b in reads:
            if b.lw is not None:
                deps.append(b.lw)
            for k, v in b.pw.items():
                deps.append((k, v, "dma"))
        for b in writes:
            if b.lw is not None and (isdma or b.lw[2] != eng):
                deps.append(b.lw)
            for r in b.rd:
                if isdma or r[2] != eng:
                    deps.append(r)
        seen = self.seen[eng]
        mx = {}
        for (key, val, _e) in deps:
            if seen.get(key, 0) < val:
                seen[key] = val
                mx[key] = val
        for key, val in mx.items():
            if key[0] == "E":
                self.needed[key[1]].add(val)
        return list(mx.items())

    def _commit(self, ev, reads, writes, pwrites=()):
        reads = self._flat(reads); writes = self._flat(writes); pwrites = self._flat(pwrites)
        for b in reads:
            b.rd.append(ev)
        for b in writes:
            b.lw = ev
            b.rd = []
            b.pw = {}
        for b in pwrites:
            b.pw[ev[0]] = max(b.pw.get(ev[0], 0), ev[1])

    def op(self, eng, fn, reads=(), writes=()):
        waits = self._deps(eng, reads, writes, False)
        self.ccnt[eng] += 1
        seq = self.ccnt[eng]
        ev = (("E", eng), seq, eng)
        self.ops[eng].append(Op(fn(_REC), waits, ("E", eng), seq, False))
        self._commit(ev, reads, writes)
        return ev

    def dma(self, eng, fn, sbuf, reads=(), writes=(), pwrites=(), n=1, inc=16):
        waits = self._deps(eng, reads, writes, True)
        self.dsems[sbuf.name] = True
        sbuf.dcnt += inc * n
        key = ("D", sbuf.name)
        ev = (key, sbuf.dcnt, "dma")
        self.ops[eng].append(Op(fn(_REC), waits, key, None, True, inc))
        self._commit(ev, reads, writes, pwrites)
        return ev

    def wait_all(self, eng, bufs):
        waits = self._deps(eng, bufs, (), True)
        self.ops[eng].append(Op(None, waits, None, None, False))

    def emit(self, nc):
        rank = {}
        for e in ENGS:
            for i, s in enumerate(sorted(self.needed[e])):
                rank[(e, s)] = i + 1
        with contextlib.ExitStack() as st:
            sems = {}
            for e in ENGS:
                sems[("E", e)] = st.enter_context(nc.semaphore("se_" + e))
            for i, name in enumerate(self.dsems):
                sems[("D", name)] = st.enter_context(nc.semaphore("sd%d" % i))
            block = st.enter_context(nc.Block())

            def run(engname):
                def body(eng):
                    for o in self.ops[engname]:
                        for (key, val) in o.waits:
                            v = rank[(key[1], val)] if key[0] == "E" else val
                            eng.wait_ge(sems[key], v)
                        if o.fn is None:
                            continue
                        if o.isdma:
                            for (nm_, a_, k_) in o.fn:
                                getattr(eng, nm_)(*a_, **k_).then_inc(sems[o.inc], o.n)
                        else:
                            nm_, a_, k_ = o.fn
                            r = getattr(eng, nm_)(*a_, **k_)
                            if (engname, o.seq) in rank:
                                r.then_inc(sems[o.inc], 1)
                return body

            block.tensor(run("pe"))
            block.scalar(run("act"))
            block.vector(run("dve"))
            block.gpsimd(run("pool"))
            block.sync(run("sp"))
        return {e: len(self.ops[e]) for e in ENGS}, len(self.dsems)


class T:
    def __init__(self, t, name):
        self.t = t
        self.b = Buf(name)


def build(stop="full"):
    nc = bass.Bass("TRN2", target_bir_lowering=False, num_devices=8)
    mk = MK()
    staged = stop in ("s1", "s2", "s3")
    KIND = {"in": "ExternalInput", "out": "ExternalOutput", "int": "Internal"}
    dt_ = lambda n, s, d, role: nc.dram_tensor(n, s, d, kind=KIND[role]).ap()
    di = lambda n, s, d=F32: dt_(n, s, d, "in")
    do = lambda n, s, d=F32: dt_(n, s, d, "out")
    dx = lambda n, s, d=BF16: dt_(n, s, d, "int")

    class PerLayer:
        def __init__(self, aps):
            self.aps = aps

        def __getitem__(self, idx):
            if isinstance(idx, tuple):
                return self.aps[idx[0]][idx[1:]] if len(idx) > 1 else self.aps[idx[0]]
            return self.aps[idx]

    def role(l, what):
        if not staged:
            return "out" if what in ("O", "V") else "int"
        prodA = {"s1": 0, "s2": 1}.get(stop)
        cons = {"s2": 0, "s3": 1}.get(stop)
        if what in ("A", "O"):
            if l == prodA:
                return "out"
            if l == cons and what == "A":
                return "in"
            return "int"
        if what == "G":
            return "in" if l == cons else "int"
        if what == "V":
            return "out" if l == cons else "int"
        return "int"

    xin = di("xin", [NT, D]) if stop in ("full", "s1") or not staged else dx("xin", [NT, D], F32)
    big_role = "int" if stop == "s1" else "in"
    pin = dt_("pin", [DEPTH, NT, PLE], F32, big_role)
    cch = dt_("cch", [DEPTH, 4, PAST, KVL], F32, big_role); ckr = dt_("ckr", [DEPTH, 4, PAST, ROPE], F32, big_role)
    slab = di("slab", [DEPTH, 128, WLP]); gains = di("gains", [128, DEPTH * NG])
    gmtab = di("gmtab", [DEPTH, 128, 2560]); gmw = di("gmw", [DEPTH, 128, 1536])
    cst = di("cst", [128, 528])
    rope_t = di("rope_t", [64, 2 * NT])
    y_o = dt_("y_o", [NT, D], F32, "out" if stop in ("full", "s3") or not staged else "int")
    c_o = PerLayer([dt_("c_o%d" % l, [NT, KVL], F32, role(l, "O")) for l in range(DEPTH)])
    kr_o = PerLayer([dt_("kr_o%d" % l, [NT, ROPE], F32, role(l, "O")) for l in range(DEPTH)])
    gv_o = PerLayer([dt_("gv_o%d" % l, [64, D], F32, role(l, "V")) for l in range(DEPTH)])
    wbf = dx("wbf", [DEPTH, 128, WLP])
    qs_n = PerLayer([dt_("qs_n%d" % l, [NH, 128, NT], BF16, role(l, "A")) for l in range(DEPTH)])
    qs_r = PerLayer([dt_("qs_r%d" % l, [NH, 64, NT], BF16, role(l, "A")) for l in range(DEPTH)])
    ownc = PerLayer([dt_("ownc%d" % l, [128, 4, 2048], BF16, role(l, "A")) for l in range(DEPTH)])
    ownr = PerLayer([dt_("ownr%d" % l, [64, 4, 512], BF16, role(l, "A")) for l in range(DEPTH)])
    cbSd = PerLayer([dt_("cbSd%d" % l, [128, 4, 64], BF16, role(l, "A")) for l in range(DEPTH)])
    krSd = PerLayer([dt_("krSd%d" % l, [64, 64], BF16, role(l, "A")) for l in range(DEPTH)])
    xc = [dx("xc%d" % l, [8 * 128, 4 * 2048]) for l in range(DEPTH)]
    xr = [dx("xr%d" % l, [8 * 64, 4 * 512]) for l in range(DEPTH)]
    xco = [dt_("xco%d" % l, [8 * 128, 4 * 2048], BF16, role(l, "G")) for l in range(DEPTH)]
    xro = [dt_("xro%d" % l, [8 * 64, 4 * 512], BF16, role(l, "G")) for l in range(DEPTH)]
    hst_i = dt_("hst_i", [128, 8, NT], F32, "in" if stop in ("s2", "s3") else "int")
    hst_o = dt_("hst_o", [128, 8, NT], F32, "out" if stop in ("s1", "s2") else "int")
    kts = dx("kts", [DEPTH, 36, 128, NH, 512]); vs = dx("vs", [DEPTH, 36, 128, NH, 512])
    DB = {n: Buf(n) for n in ["wbf0", "wbf1", "qs0", "qs1", "xc0", "xc1", "xco0", "xco1", "xr0", "xr1", "xro0", "xro1",
                              "own0", "own1", "kv0", "kv1", "outs"]}

    with contextlib.ExitStack() as st:
        def sb(name, shape, dt=F32):
            return T(st.enter_context(nc.sbuf_tensor(name, shape, dt)), name)

        def ring(name, n, shape, dt=F32):
            return [sb("%s%d" % (name, i), shape, dt) for i in range(n)]

        class V:
            def __init__(self, t, b):
                self.t = t; self.b = b

        hT = sb("hT", [128, 8, NT])
        hB = [Buf("h%d" % i) for i in range(5)]
        BA = st.enter_context(nc.sbuf_tensor("BA", [128, 24 * 512], BF16))
        baB = [Buf("ba%d" % i) for i in range(24)]
        FA = st.enter_context(nc.sbuf_tensor("FA", [128, 7 * 512], F32))
        faB = [Buf("fa%d" % i) for i in range(7)]
        WA = st.enter_context(nc.sbuf_tensor("WA", [128, 16 * 512], BF16))
        waB = [Buf("wa%d" % i) for i in range(16)]

        def bav(lo, n, f=512):
            if n == 1:
                return V(BA[:, lo * 512:(lo + 1) * 512], baB[lo:lo + 1])
            return V(BA[:, lo * 512:(lo + n) * 512].rearrange("p (a f) -> p a f", f=f), baB[lo:lo + n])

        def fav(lo, n):
            return V(FA[:, lo * 512:(lo + n) * 512].rearrange("p (a f) -> p a f", f=512), faB[lo:lo + n])

        actT = bav(0, 22)
        qaT = bav(0, 3); cbP = bav(3, 4); mringT = bav(7, 4); krbP = bav(11, 1)
        qn_t = [bav(12, 1), bav(13, 1)]; qr_t = [bav(14, 1), bav(15, 1)]
        cbt = [bav(0, 4), bav(4, 4)]; kt8 = bav(8, 8); v8 = bav(16, 8)
        katt = [bav(0, 1), bav(1, 1), bav(2, 1)]; vatt = [bav(3, 1), bav(4, 1), bav(5, 1)]
        qnA = [bav(6, 1), bav(7, 1)]; oaT = bav(8, 8); kratt = [bav(16, 1), bav(17, 1), bav(18, 1)]; qrA = [bav(19, 1), bav(20, 1)]
        mxT = bav(0, 8); vtm = bav(16, 8, f=1024); pT = bav(22, 2)
        zq = fav(0, 3); zk = fav(3, 4)
        gmt = V(FA[:, 0:2560], faB[0:5])
        wuq = V(WA[:, 0:4608].rearrange("p (a f) -> p a f", f=1536), waB[0:9])
        wuk = V(WA[:, 0:4096].rearrange("p (a f) -> p a f", f=1024), waB[0:8])
        wuv = V(WA[:, 4096:8192].rearrange("p (a f) -> p a f", f=1024), waB[8:16])

        xn = ring("xn", 1, [128, 8, 512], BF16)
        wr = ring("wr", 20, [128, 512], BF16)
        t32 = ring("t32", 8, [128, 512])
        tb16 = ring("tb16", 5, [128, 512], BF16)
        big32 = ring("big32", 2, [128, 4, 512])
        krt = ring("krt", 2, [64, 512], BF16)
        cbS = sb("cbS", [128, 4, 64], BF16)
        krS = sb("krS", [64, 64], BF16)
        oaS = sb("oaS", [128, NH, 64], BF16)
        qSn = sb("qSn", [128, NH, 16], BF16)
        qSr = sb("qSr", [64, NH, 16], BF16)
        krr = sb("krr", [64, 512])
        gmwT = sb("gmwT", [128, 1536], BF16)
        gn = sb("gn", [128, DEPTH * NG])
        cs = sb("cs", [128, 528])
        csb = sb("csb", [128, 528], BF16)
        ropet = sb("ropet", [64, 1024])
        small = sb("small", [128, 16])
        smr = ring("smr", 4, [128, 8])
        dacc = ring("dacc", 1, [128, 512])
        ps = [T(st.enter_context(nc.psum_tensor("ps%d" % i, [128, 512], F32)), "ps%d" % i) for i in range(8)]
        held = [False] * 8
        pctr = [0]
        rctr = {}

        def pget(hold=False):
            for _ in range(16):
                i = pctr[0] % 8
                pctr[0] += 1
                if not held[i]:
                    if hold:
                        held[i] = True
                    return i
            raise RuntimeError("psum exhausted")

        def rel(*pis):
            for p in pis:
                held[p] = False

        def nxt(r, key):
            i = rctr.get(key, 0)
            rctr[key] = i + 1
            return r[i % len(r)]

        RT_b = csb.t[0:64, 256:320]
        zero_b = small.t[:, 1:2]

        def gcol(l, name, c=0, kp=128):
            o = l * NG + GCOLS[name] + c
            return gn.t[0:kp, o:o + 1]

        def sem_of(b):
            return b[0] if isinstance(b, list) else b

        def load(dst, dst_ap, src_ap, reads=()):
            mk.dma("sp", lambda e: [e.dma_start(out=dst_ap, in_=src_ap)], sem_of(dst.b), reads=list(reads), writes=[dst.b])

        def store(src, dst_ap, src_ap, dbuf, eng="pool"):
            mk.dma(eng, lambda e: [e.dma_start(out=dst_ap, in_=src_ap)], sem_of(src.b), reads=[src.b], pwrites=[dbuf])

        def wb(l, name):
            k = "wbf%d_%s" % (l, name)
            if k not in DB:
                DB[k] = Buf(k)
            return DB[k]

        def wname_of(col):
            for n_, (off_, nk_, C_) in WOFF.items():
                if off_ <= col < off_ + nk_ * C_:
                    return n_
            return "pad"

        def wslot(l, name, kc, c0, ncols):
            off, nk, C = WOFF[name]
            s = nxt(wr, "wr")
            a = off + kc * C + c0
            load(s, s.t[:, 0:ncols], wbf[l, :, a:a + ncols], reads=[wb(l, name)])
            return s

        def mm(pi, out_ap, lhsT, rhs, start, stop, reads):
            mk.op("pe", lambda e: e.matmul(out_ap, lhsT=lhsT, rhs=rhs, start=start, stop=stop), list(reads), [ps[pi].b])

        def act(func, out_ap, in_ap, reads, writes, **kw):
            mk.op("act", lambda e: e.activation(out=out_ap, in_=in_ap, func=func, **kw), reads, writes)

        def stt(out_ap, in0, scalar, in1, op0, op1, reads, writes):
            mk.op("dve", lambda e: e.scalar_tensor_tensor(out=out_ap, in0=in0, scalar=scalar, in1=in1, op0=op0, op1=op1), reads, writes)

        def tt(out_ap, in0, in1, op, reads, writes):
            mk.op("dve", lambda e: e.tensor_tensor(out=out_ap, in0=in0, in1=in1, op=op), reads, writes)

        def rstd_from(pi, kp, Tn, Dn):
            r = nxt(t32, "t32")
            act(AF.Ln, r.t[0:kp, 0:Tn], ps[pi].t[0:kp, 0:Tn], [ps[pi].b, small.b], [r.b], scale=1.0 / Dn, bias=small.t[0:kp, 0:1])
            act(AF.Exp, r.t[0:kp, 0:Tn], r.t[0:kp, 0:Tn], [r.b], [r.b], scale=-0.5)
            return r

        def sumsq(chunks, Tn):
            pi = pget(hold=True)
            n = len(chunks)
            for i, (ap, kp, bufs) in enumerate(chunks):
                sq = nxt(tb16, "tb16")
                act(AF.Square, sq.t[0:kp, 0:Tn], ap, bufs, [sq.b])
                mm(pi, ps[pi].t[:, 0:Tn], csb.t[0:kp, 128:256], sq.t[0:kp, 0:Tn], i == 0, i == n - 1, [sq.b, csb.b])
            return pi

        def rms_h(l, ti, gname):
            c0, Tn = TILES[ti]
            chunks = [(hT.t[:, c, c0:c0 + Tn], 128, [hB[ti]]) for c in range(8)]
            pi = sumsq(chunks, Tn)
            r = rstd_from(pi, 128, Tn, D)
            rel(pi)
            x = nxt(xn, "xn")
            for c in range(8):
                stt(x.t[:, c, 0:Tn], hT.t[:, c, c0:c0 + Tn], gcol(l, gname, c), r.t[:, 0:Tn], ALU.mult, ALU.mult, [hB[ti], r.b, gn.b], [x.b])
            return x

        def lin_group(l, wname, K, col0, mlist, rhs_fn, Tn, rbufs):
            ncols = max(co + M for co, M in mlist)
            pis = [pget(hold=True) for _ in mlist]
            for kc in range(K):
                s = wslot(l, wname, kc, col0, ncols)
                for j, (co, M) in enumerate(mlist):
                    mm(pis[j], ps[pis[j]].t[0:M, 0:Tn], s.t[:, co:co + M], rhs_fn(kc), kc == 0, kc == K - 1, [s.b] + rbufs)
            return pis

        C4 = [(0, 128), (128, 128), (256, 128), (384, 128)]

        def ffn(l, ti, nname, gname, uname, dname):
            c0, Tn = TILES[ti]
            x = rms_h(l, ti, nname)
            for g4 in range(6):
                nch = 4 if g4 < 5 else 2
                gs = [wslot(l, gname, kc, g4 * 512, nch * 128) for kc in range(8)]
                us = [wslot(l, uname, kc, g4 * 512, nch * 128) for kc in range(8)]
                for j in range(nch):
                    pg_ = pget(hold=True); pu_ = pget(hold=True)
                    for kc in range(8):
                        mm(pg_, ps[pg_].t[:, 0:Tn], gs[kc].t[:, j * 128:(j + 1) * 128], x.t[:, kc, 0:Tn], kc == 0, kc == 7, [gs[kc].b, x.b])
                    for kc in range(8):
                        mm(pu_, ps[pu_].t[:, 0:Tn], us[kc].t[:, j * 128:(j + 1) * 128], x.t[:, kc, 0:Tn], kc == 0, kc == 7, [us[kc].b, x.b])
                    sg = nxt(t32, "t32")
                    act(AF.Silu, sg.t[:, 0:Tn], ps[pg_].t[:, 0:Tn], [ps[pg_].b], [sg.b])
                    jj = g4 * 4 + j
                    tt(actT.t[:, jj, 0:Tn], sg.t[:, 0:Tn], ps[pu_].t[:, 0:Tn], ALU.mult, [ps[pu_].b, sg.b], [actT.b])
                    rel(pg_, pu_)
            for half in range(2):
                pis = [pget(hold=True) for _ in range(4)]
                for j in range(22):
                    s = wslot(l, dname, j, half * 512, 512)
                    for m in range(4):
                        mm(pis[m], ps[pis[m]].t[:, 0:Tn], s.t[:, m * 128:(m + 1) * 128], actT.t[:, j, 0:Tn], j == 0, j == 21, [s.b, actT.b])
                for m in range(4):
                    c = half * 4 + m
                    stt(hT.t[:, c, c0:c0 + Tn], ps[pis[m]].t[:, 0:Tn], 0.5, hT.t[:, c, c0:c0 + Tn], ALU.mult, ALU.add, [ps[pis[m]].b, hB[ti]], [hB[ti]])
                    rel(pis[m])

        def rope_apply(src32, srcb, Tn, dst_list):
            pr = pget(hold=True)
            mm(pr, ps[pr].t[0:64, 0:Tn], RT_b, srcb.t[0:64, 0:Tn], True, True, [srcb.b, csb.b])
            t1 = nxt(t32, "t32"); t2 = nxt(t32, "t32")
            tt(t1.t[0:64, 0:Tn], src32.t[0:64, 0:Tn], ropet.t[:, 0:Tn], ALU.mult, [src32.b, ropet.b], [t1.b])
            tt(t2.t[0:64, 0:Tn], ps[pr].t[0:64, 0:Tn], ropet.t[:, 512:512 + Tn], ALU.mult, [ps[pr].b, ropet.b], [t2.b])
            rel(pr)
            for ap, tobj in dst_list:
                tt(ap, t1.t[0:64, 0:Tn], t2.t[0:64, 0:Tn], ALU.add, [t1.b, t2.b], [tobj.b])

        def transpose_out(src_fn, nchunk, kp, Tn, dst_fn, dbuf, sbufs):
            for i in range((Tn + 127) // 128):
                nt_ = min(128, Tn - i * 128)
                o = nxt(big32, "big32")
                for cg in range(0, nchunk, 4):
                    ncg = min(4, nchunk - cg)
                    pi = pget(hold=True)
                    for c in range(ncg):
                        mk.op("pe", lambda e: e.transpose(ps[pi].t[0:nt_, c * kp:(c + 1) * kp], src_fn(cg + c)[:, i * 128:i * 128 + nt_], cs.t[0:kp, 0:kp]),
                              sbufs + [cs.b], [ps[pi].b])
                    act(AF.Copy, o.t[0:nt_, cg // 4, 0:ncg * kp], ps[pi].t[0:nt_, 0:ncg * kp], [ps[pi].b], [o.b])
                    rel(pi)
                W = nchunk * kp
                if W <= 512:
                    store(o, dst_fn(i, nt_), o.t[0:nt_, 0, 0:W], dbuf)
                else:
                    store(o, dst_fn(i, nt_).rearrange("t (a f) -> t a f", f=512), o.t[0:nt_, 0:W // 512, :], dbuf)

        def expand(l, ct_t, ct_b, kp, k8, v8_):
            for h in range(NH):
                pk = pget(hold=True)
                for kc in range(4):
                    mm(pk, ps[pk].t[:, 0:kp], wuk.t[:, kc, h * 128:(h + 1) * 128], ct_t[:, kc, 0:kp], kc == 0, kc == 3, [wuk.b, ct_b])
                pq = sumsq([(ps[pk].t[:, 0:kp], 128, [ps[pk].b])], kp)
                r = rstd_from(pq, 128, kp, 128)
                rel(pq)
                stt(k8.t[:, h, 0:kp], ps[pk].t[:, 0:kp], gcol(l, "knn"), r.t[:, 0:kp], ALU.mult, ALU.mult, [ps[pk].b, r.b, gn.b], [k8.b])
                rel(pk)
            for sub in range((kp + 127) // 128):
                nk = min(128, kp - sub * 128)
                for hf in range(2):
                    pv = pget(hold=True)
                    for kc in range(4):
                        mm(pv, ps[pv].t[0:nk, :], ct_t[:, kc, sub * 128:sub * 128 + nk], wuv.t[:, kc, hf * 512:(hf + 1) * 512], kc == 0, kc == 3, [wuv.b, ct_b])
                    act(AF.Copy, v8_.t[0:nk, hf * 4:(hf + 1) * 4, sub * 128:(sub + 1) * 128], ps[pv].t[0:nk, :].rearrange("p (h d) -> p h d", h=4), [ps[pv].b], [v8_.b])
                    rel(pv)

        LOOKAHEAD = 2

        def run_steps(po, pd, steps, auto=False, dacc=None):
            it = iter(steps)
            pend = []
            state = {"n": 0}

            def do_pv(pP, t_, last):
                kp, nq, oc0 = t_["kp"], t_["nq"], t_["oc0"]
                st_, sp_ = t_["start"], t_["stop"]
                i = state["n"]; state["n"] += 1
                if auto:
                    st_ = (i == 0)
                    sp_ = last
                p = nxt(tb16, "tb16")
                act(AF.Exp, p.t[0:kp, 0:nq], ps[pP].t[0:kp, 0:nq], [ps[pP].b, cs.b, small.b], [p.b], scale=SCALE, bias=t_["bias"])
                rel(pP)
                mm(po, ps[po].t[:, oc0:oc0 + nq], t_["V"], p.t[0:kp, 0:nq], st_, sp_, t_["vreads"] + [p.b])
                if dacc is None:
                    mm(pd, ps[pd].t[:, oc0:oc0 + nq], csb.t[0:kp, 128:256], p.t[0:kp, 0:nq], st_, sp_, [csb.b, p.b])
                else:
                    a = dacc[0]
                    if i == 0:
                        assert kp == 128 and nq == 512 and oc0 == 0
                        mk.op("dve", lambda e: e.tensor_copy(out=a.t[:, :], in_=p.t[:, :]), [p.b], [a.b])
                    else:
                        mk.op("dve", lambda e: e.tensor_tensor(out=a.t[0:kp, oc0:oc0 + nq], in0=a.t[0:kp, oc0:oc0 + nq], in1=p.t[0:kp, 0:nq], op=ALU.add), [a.b, p.b], [a.b])

            while True:
                s = next(it, None)
                if s is not None:
                    pS = pget(hold=True)
                    kp, nq = s["kp"], s["nq"]
                    mm(pS, ps[pS].t[0:kp, 0:nq], s["K"], s["qn"], True, False, s["kreads"])
                    mm(pS, ps[pS].t[0:kp, 0:nq], s["KR"], s["qr"], False, True, s["kreads"])
                    pend.append((pS, s))
                    if len(pend) > LOOKAHEAD:
                        pP, t_ = pend.pop(0)
                        do_pv(pP, t_, False)
                else:
                    while pend:
                        pP, t_ = pend.pop(0)
                        do_pv(pP, t_, len(pend) == 0)
                    break

        load(cs, cs.t[:], cst)
        load(gn, gn.t[:], gains)
        mk.op("dve", lambda e: e.tensor_copy(out=csb.t[:], in_=cs.t[:]), [cs.b], [csb.b])
        mk.op("pool", lambda e: e.memset(small.t[:], 0.0), [], [small.b])
        mk.op("pool", lambda e: e.memset(small.t[:, 0:1], EPS), [], [small.b])
        cast_engs = ["act", "dve", "act", "dve", "pool"]
        cstate = [0]

        cq = []

        def conv_chunk(l, c, eng):
            a = nxt(t32, "t32"); b_ = nxt(wr, "wr")
            load(a, a.t[:], slab[l, :, c * 512:(c + 1) * 512])
            if eng == "act":
                act(AF.Copy, b_.t[:], a.t[:], [a.b], [b_.b])
            else:
                mk.op(eng, lambda e: e.tensor_copy(out=b_.t[:], in_=a.t[:]), [a.b], [b_.b])
            store(b_, wbf[l, :, c * 512:(c + 1) * 512], b_.t[:], wb(l, wname_of(c * 512)), eng="pool")

        def convert(l, col_lo, col_hi, background=False):
            for c in range(col_lo // 512, (col_hi + 511) // 512):
                if background:
                    cq.append((l, c))
                else:
                    eng = cast_engs[cstate[0] % 5]; cstate[0] += 1
                    conv_chunk(l, c, eng)

        def pump(n):
            for _ in range(min(n, len(cq))):
                l_, c_ = cq.pop(0)
                conv_chunk(l_, c_, "pool")

        PRE = WOFF["wuk"][0]
        SUF = WOFF["win"][0]

        def load_wuq(l):
            offs = [WOFF["wuq"][0] + kc * 1536 for kc in range(3)]
            mk.dma("sp", lambda e: [e.dma_start(out=wuq.t[:, kc, :], in_=wbf[l, :, offs[kc]:offs[kc] + 1536]) for kc in range(3)],
                   sem_of(wuq.b), reads=[wb(l, "wuq")], writes=[wuq.b], n=3)

        def load_wukv(l):
            for w, nm in ((wuk, "wuk"), (wuv, "wuv")):
                offs = [WOFF[nm][0] + kc * 1024 for kc in range(4)]
                mk.dma("sp", lambda e: [e.dma_start(out=w.t[:, kc, :], in_=wbf[l, :, offs[kc]:offs[kc] + 1024]) for kc in range(4)],
                       sem_of(w.b), reads=[wb(l, nm)], writes=[w.b], n=4)

        def load_gmw(l):
            a = nxt(big32, "big32")
            load(a, a.t[:, 0:3, :], gmw[l].rearrange("p (a f) -> p a f", f=512))
            for g in range(8):
                tt(gmwT.t[:, g * 128:(g + 1) * 128], a.t[:, g // 4, (g % 4) * 128:(g % 4 + 1) * 128], cs.t[:, 320:448], ALU.mult, [a.b, cs.b], [gmwT.b])
                tt(gmwT.t[0:64, 1024 + g * 64:1024 + (g + 1) * 64], a.t[0:64, 2, g * 64:(g + 1) * 64], cs.t[0:64, 448:512], ALU.mult, [a.b, cs.b], [gmwT.b])

        def load_x(ti):
            c0, Tn = TILES[ti]
            for i in range((Tn + 127) // 128):
                nt_ = min(128, Tn - i * 128)
                a = nxt(big32, "big32")
                load(a, a.t[0:nt_, 0:2, :], xin[c0 + i * 128:c0 + i * 128 + nt_, :].rearrange("t (a f) -> t a f", f=512))
                for cg in range(2):
                    pi = pget(hold=True)
                    for c in range(4):
                        mk.op("pe", lambda e: e.transpose(ps[pi].t[:, c * 128:c * 128 + nt_], a.t[0:nt_, cg, c * 128:(c + 1) * 128], cs.t[0:nt_, 0:nt_]),
                              [a.b, cs.b], [ps[pi].b])
                    act(AF.Copy, hT.t[:, cg * 4:(cg + 1) * 4, c0 + i * 128:c0 + i * 128 + nt_],
                        ps[pi].t[:].rearrange("p (c f) -> p c f", f=128)[:, :, 0:nt_], [ps[pi].b], [hB[ti]])
                    rel(pi)

        def phase_a(l, ti):
            c0, Tn = TILES[ti]
            prompt = ti < 4
            if l == 0:
                load_x(ti)
            ffn(l, ti, "f1n", "f1g", "f1u", "f1d")
            x = rms_h(l, ti, "mixn")
            rf = lambda kc: x.t[:, kc, 0:Tn]
            p1 = lin_group(l, "win", 8, 0, C4, rf, Tn, [x.b])
            for j in range(3):
                act(AF.Copy, zq.t[:, j, 0:Tn], ps[p1[j]].t[:, 0:Tn], [ps[p1[j]].b], [zq.b])
            act(AF.Copy, zk.t[:, 0, 0:Tn], ps[p1[3]].t[:, 0:Tn], [ps[p1[3]].b], [zk.b])
            rel(*p1)
            p2 = lin_group(l, "win", 8, 512, [(0, 128), (128, 128), (256, 128), (384, 64)], rf, Tn, [x.b])
            for j in range(3):
                act(AF.Copy, zk.t[:, 1 + j, 0:Tn], ps[p2[j]].t[:, 0:Tn], [ps[p2[j]].b], [zk.b])
            act(AF.Copy, krr.t[0:64, 0:Tn], ps[p2[3]].t[0:64, 0:Tn], [ps[p2[3]].b], [krr.b])
            rel(*p2)
            mk.dma("sp", lambda e: [e.dma_start(out=ropet.t[:, 0:Tn], in_=rope_t[:, c0:c0 + Tn]), e.dma_start(out=ropet.t[:, 512:512 + Tn], in_=rope_t[:, NT + c0:NT + c0 + Tn])],
                   ropet.b, writes=[ropet.b], n=2)
            pi = sumsq([(zq.t[:, j, 0:Tn], 128, [zq.b]) for j in range(3)], Tn)
            r = rstd_from(pi, 128, Tn, QLORA)
            rel(pi)
            for j in range(3):
                stt(qaT.t[:, j, 0:Tn], zq.t[:, j, 0:Tn], gcol(l, "qan", j), r.t[:, 0:Tn], ALU.mult, ALU.mult, [zq.b, r.b, gn.b], [qaT.b])
            for h in range(NH):
                pn = pget(hold=True)
                for kc in range(3):
                    mm(pn, ps[pn].t[:, 0:Tn], wuq.t[:, kc, h * 192:h * 192 + 128], qaT.t[:, kc, 0:Tn], kc == 0, kc == 2, [wuq.b, qaT.b])
                pr_ = pget(hold=True)
                for kc in range(3):
                    mm(pr_, ps[pr_].t[0:64, 0:Tn], wuq.t[:, kc, h * 192 + 128:h * 192 + 192], qaT.t[:, kc, 0:Tn], kc == 0, kc == 2, [wuq.b, qaT.b])
                pq = sumsq([(ps[pn].t[:, 0:Tn], 128, [ps[pn].b])], Tn)
                r1 = rstd_from(pq, 128, Tn, 128)
                rel(pq)
                qo = nxt(qn_t, "qn_t")
                stt(qo.t[:, 0:Tn], ps[pn].t[:, 0:Tn], gcol(l, "qnn"), r1.t[:, 0:Tn], ALU.mult, ALU.mult, [ps[pn].b, r1.b, gn.b], [qo.b])
                rel(pn)
                store(qo, qs_n[l, h, :, c0:c0 + Tn], qo.t[:, 0:Tn], DB["qs%d" % l])
                pq = sumsq([(ps[pr_].t[0:64, 0:Tn], 64, [ps[pr_].b])], Tn)
                r2 = rstd_from(pq, 64, Tn, 64)
                rel(pq)
                x32 = nxt(t32, "t32"); xb = nxt(tb16, "tb16")
                stt(x32.t[0:64, 0:Tn], ps[pr_].t[0:64, 0:Tn], gcol(l, "qrn", 0, 64), r2.t[0:64, 0:Tn], ALU.mult, ALU.mult, [ps[pr_].b, r2.b, gn.b], [x32.b])
                rel(pr_)
                act(AF.Copy, xb.t[0:64, 0:Tn], x32.t[0:64, 0:Tn], [x32.b], [xb.b])
                qro = nxt(qr_t, "qr_t")
                rope_apply(x32, xb, Tn, [(qro.t[0:64, 0:Tn], qro)])
                store(qro, qs_r[l, h, :, c0:c0 + Tn], qro.t[0:64, 0:Tn], DB["qs%d" % l])
            pi = sumsq([(zk.t[:, j, 0:Tn], 128, [zk.b]) for j in range(4)], Tn)
            r = rstd_from(pi, 128, Tn, KVL)
            rel(pi)
            for j in range(4):
                stt(zk.t[:, j, 0:Tn], zk.t[:, j, 0:Tn], gcol(l, "kvan", j), r.t[:, 0:Tn], ALU.mult, ALU.mult, [zk.b, r.b, gn.b], [zk.b])
            cb = cbP if prompt else cbS
            for j in range(4):
                act(AF.Copy, cb.t[:, j, 0:Tn], zk.t[:, j, 0:Tn], [zk.b], [cb.b])
            transpose_out(lambda c: zk.t[:, c, 0:Tn], 4, 128, Tn, lambda i, n: c_o[l, c0 + i * 128:c0 + i * 128 + n, :], DB["outs"], [zk.b])
            if prompt:
                store(cb, ownc[l, :, ti, :].rearrange("p (a f) -> p a f", f=512), cb.t[:, :, :], DB["own%d" % l])
                for rr in range(0 if staged else 8):
                    mk.op("dve", lambda e: e.tensor_scalar_mul(out=mringT.t[:], in0=cb.t[:], scalar1=cs.t[:, 520 + rr:521 + rr]), [cb.b, cs.b], [mringT.b])
                    store(mringT, xc[l][rr * 128:(rr + 1) * 128, ti * 2048:(ti + 1) * 2048].rearrange("p (a f) -> p a f", f=512), mringT.t[:], DB["xc%d" % l])
            pi = sumsq([(krr.t[0:64, 0:Tn], 64, [krr.b])], Tn)
            r = rstd_from(pi, 64, Tn, 64)
            rel(pi)
            stt(krr.t[0:64, 0:Tn], krr.t[0:64, 0:Tn], gcol(l, "krn", 0, 64), r.t[0:64, 0:Tn], ALU.mult, ALU.mult, [krr.b, r.b, gn.b], [krr.b])
            kb_ = nxt(tb16, "tb16")
            act(AF.Copy, kb_.t[0:64, 0:Tn], krr.t[0:64, 0:Tn], [krr.b], [kb_.b])
            kr32 = nxt(t32, "t32")
            krb = krbP if prompt else krS
            rope_apply(krr, kb_, Tn, [(kr32.t[0:64, 0:Tn], kr32), (krb.t[0:64, 0:Tn], krb)])
            transpose_out(lambda c: kr32.t[0:64, 0:Tn], 1, 64, Tn, lambda i, n: kr_o[l, c0 + i * 128:c0 + i * 128 + n, :], DB["outs"], [kr32.b])
            if prompt:
                store(krb, ownr[l, :, ti, :], krb.t[0:64, :], DB["own%d" % l])
                for rr in range(0 if staged else 8):
                    m = nxt(tb16, "tb16")
                    mk.op("dve", lambda e: e.tensor_scalar_mul(out=m.t[0:64, :], in0=krb.t[0:64, :], scalar1=cs.t[0:64, 520 + rr:521 + rr]), [krb.b, cs.b], [m.b])
                    store(m, xr[l][rr * 64:(rr + 1) * 64, ti * 512:(ti + 1) * 512], m.t[0:64, :], DB["xr%d" % l])

        ccb = [Buf("cc%d" % i) for i in range(4)]

        def exchange(l):
            mk.dma("pool", lambda e: [e.collective_compute("AllReduce", ALU.add, replica_groups=[list(range(8))], ins=[xc[l]], outs=[xco[l]])],
                   ccb[2 * l], reads=[DB["xc%d" % l]], writes=[DB["xco%d" % l]], inc=1)
            mk.dma("pool", lambda e: [e.collective_compute("AllReduce", ALU.add, replica_groups=[list(range(8))], ins=[xr[l]], outs=[xro[l]])],
                   ccb[2 * l + 1], reads=[DB["xr%d" % l]], writes=[DB["xro%d" % l]], inc=1)
            mk.wait_all("pool", [DB["xco%d" % l], DB["xro%d" % l]])

        def expand_prompt(l):
            for blk in range(36):
                ct = nxt(cbt, "cbt")
                if blk < 32:
                    r_, s_ = blk % 8, blk // 8
                    load(ct, ct.t[:], xco[l][r_ * 128:(r_ + 1) * 128, s_ * 2048:(s_ + 1) * 2048].rearrange("p (a f) -> p a f", f=512), reads=[DB["xco%d" % l]])
                else:
                    load(ct, ct.t[:], ownc[l, :, blk - 32, :].rearrange("p (a f) -> p a f", f=512), reads=[DB["own%d" % l]])
                expand(l, ct.t, ct.b, 512, kt8, v8)
                store(kt8, kts[l, blk], kt8.t[:], DB["kv%d" % l])
                store(v8, vs[l, blk], v8.t[:], DB["kv%d" % l])

        def att_sample(l):
            for b in range(4):
                q0 = 2048 + 16 * b
                load(qSn, qSn.t[:], qs_n[l, :, :, q0:q0 + 16].rearrange("h p t -> p h t"), reads=[DB["qs%d" % l]])
                load(qSr, qSr.t[:], qs_r[l, :, :, q0:q0 + 16].rearrange("h p t -> p h t"), reads=[DB["qs%d" % l]])
                po = pget(hold=True); pd = pget(hold=True)
                for kb in range(9):
                    pump(PUMP[0])
                    if kb < 8:
                        a = nxt(big32, "big32")
                        load(a, a.t[:], cch[l, b, kb * 512:(kb + 1) * 512, :].rearrange("(s p) f -> p s f", p=128))
                        ct = nxt(cbt, "cbt")
                        for ch in range(4):
                            pi = pget(hold=True)
                            for sub in range(4):
                                mk.op("pe", lambda e: e.transpose(ps[pi].t[:, sub * 128:(sub + 1) * 128], a.t[:, sub, ch * 128:(ch + 1) * 128], cs.t[:, 0:128]), [a.b, cs.b], [ps[pi].b])
                            act(AF.Copy, ct.t[:, ch, :], ps[pi].t[:, :], [ps[pi].b], [ct.b])
                            rel(pi)
                        a2 = nxt(t32, "t32")
                        load(a2, a2.t[:, 0:256].rearrange("p (s f) -> p s f", f=64), ckr[l, b, kb * 512:(kb + 1) * 512, :].rearrange("(s p) f -> p s f", p=128))
                        pi = pget(hold=True)
                        for sub in range(4):
                            mk.op("pe", lambda e: e.transpose(ps[pi].t[0:64, sub * 128:(sub + 1) * 128], a2.t[:, sub * 64:(sub + 1) * 64], cs.t[:, 0:128]), [a2.b, cs.b], [ps[pi].b])
                        krx = nxt(krt, "krt")
                        act(AF.Copy, krx.t[0:64, :], ps[pi].t[0:64, :], [ps[pi].b], [krx.b])
                        rel(pi)
                        kp = 512; ct_t, ct_b = ct.t, ct.b; kr_t, kr_b = krx.t, krx.b
                    else:
                        kp = 16; ct_t, ct_b = cbS.t[:, :, 16 * b:16 * b + 16], cbS.b
                        kr_t, kr_b = krS.t[:, 16 * b:16 * b + 16], krS.b
                    expand(l, ct_t, ct_b, kp, kt8, v8)
                    steps = []
                    for h in range(NH):
                        for sub in range((kp + 127) // 128):
                            nk = min(128, kp - sub * 128)
                            steps.append(dict(K=kt8.t[:, h, sub * 128:sub * 128 + nk], KR=kr_t[0:64, sub * 128:sub * 128 + nk], V=v8.t[0:nk, h, sub * 128:(sub + 1) * 128],
                                              kp=nk, nq=16, oc0=h * 16, qn=qSn.t[:, h, :], qr=qSr.t[0:64, h, :], bias=small.t[0:nk, 1:2],
                                              kreads=[kt8.b, kr_b, qSn.b, qSr.b], vreads=[v8.b], start=(kb == 0 and sub == 0), stop=(kb == 8)))
                    run_steps(po, pd, steps)
                r = nxt(t32, "t32")
                mk.op("dve", lambda e: e.reciprocal(out=r.t[:, 0:128], in_=ps[pd].t[:, 0:128]), [ps[pd].b], [r.b])
                tt(oaS.t[:, :, 16 * b:16 * b + 16], ps[po].t[:, 0:128].rearrange("p (h q) -> p h q", h=8), r.t[:, 0:128].rearrange("p (h q) -> p h q", h=8), ALU.mult, [ps[po].b, r.b], [oaS.b])
                rel(po, pd)

        def att_prompt(l, s_):
            c0 = s_ * 512
            blist = [(8 * s2 + r2, None, s2, r2) for s2 in range(s_) for r2 in range(8)] + [(8 * s_ + r2, r2, s_, r2) for r2 in range(8)] + [(32 + s_, "diag", s_, 0)]
            for h in range(NH):
                qn = nxt(qnA, "qnA"); qr = nxt(qrA, "qrA")
                load(qn, qn.t[:, :], qs_n[l, h, :, c0:c0 + 512], reads=[DB["qs%d" % l]])
                load(qr, qr.t[0:64, :], qs_r[l, h, :, c0:c0 + 512], reads=[DB["qs%d" % l]])
                po = pget(hold=True)

                def gen(qn=qn, qr=qr):
                    for (blk, mode, s2, r2) in blist:
                        ka = nxt(katt, "katt"); va = nxt(vatt, "vatt"); kra = nxt(kratt, "kratt")
                        load(ka, ka.t[:, :], kts[l, blk, :, h, :], reads=[DB["kv%d" % l]])
                        load(va, va.t[:, :], vs[l, blk, :, h, :], reads=[DB["kv%d" % l]])
                        if blk < 32:
                            load(kra, kra.t[0:64, :], xro[l][r2 * 64:(r2 + 1) * 64, s2 * 512:(s2 + 1) * 512], reads=[DB["xro%d" % l]])
                        else:
                            load(kra, kra.t[0:64, :], ownr[l, :, s_, :], reads=[DB["own%d" % l]])
                        kreads = [ka.b, kra.b, qn.b, qr.b]
                        for sub in range(4):
                            ks = slice(sub * 128, (sub + 1) * 128)
                            if mode != "diag":
                                bias = small.t[:, 1:2] if mode is None else cs.t[:, 512 + mode:513 + mode]
                                yield dict(K=ka.t[:, ks], KR=kra.t[0:64, ks], V=va.t[:, ks], kp=128, nq=512, oc0=0, qn=qn.t[:, :], qr=qr.t[0:64, :],
                                           bias=bias, kreads=kreads, vreads=[va.b], start=False, stop=False)
                            else:
                                qa_ = 128 * sub + 64
                                if qa_ < 512:
                                    yield dict(K=ka.t[:, ks], KR=kra.t[0:64, ks], V=va.t[:, ks], kp=128, nq=512 - qa_, oc0=qa_, qn=qn.t[:, qa_:512], qr=qr.t[0:64, qa_:512],
                                               bias=small.t[:, 1:2], kreads=kreads, vreads=[va.b], start=False, stop=False)
                                k2 = slice(sub * 128, sub * 128 + 64)
                                yield dict(K=ka.t[:, k2], KR=kra.t[0:64, k2], V=va.t[0:64, ks], kp=64, nq=64, oc0=128 * sub, qn=qn.t[:, 128 * sub:128 * sub + 64],
                                           qr=qr.t[0:64, 128 * sub:128 * sub + 64], bias=small.t[0:64, 1:2], kreads=kreads, vreads=[va.b], start=False, stop=False)

                run_steps(po, None, gen(), auto=True, dacc=dacc)
                pd = pget(hold=True)
                mm(pd, ps[pd].t[:, :], cs.t[:, 128:256], dacc[0].t[:, :], True, True, [cs.b, dacc[0].b])
                r = nxt(t32, "t32")
                mk.op("dve", lambda e: e.reciprocal(out=r.t[:, :], in_=ps[pd].t[:, :]), [ps[pd].b], [r.b])
                tt(oaT.t[:, h, :], ps[po].t[:, :], r.t[:, :], ALU.mult, [ps[po].b, r.b], [oaT.b])
                rel(po, pd)

        def phase_c(l, ti):
            c0, Tn = TILES[ti]
            prompt = ti < 4
            oa = oaT if prompt else oaS
            nsub = (Tn + 127) // 128
            load(gmt, gmt.t[:, :], gmtab[l])
            x = rms_h(l, ti, "mixn")
            rf = lambda kc: x.t[:, kc, 0:Tn]
            for i in range(nsub):
                nt_ = min(128, Tn - i * 128)
                pv = [pget(hold=True), pget(hold=True)]
                for hf in range(2):
                    for kc in range(8):
                        s = wslot(l, "win", kc, 1984 + hf * 512, 512)
                        mm(pv[hf], ps[pv[hf]].t[0:nt_, :], x.t[:, kc, i * 128:i * 128 + nt_], s.t[:, 0:512], kc == 0, kc == 7, [s.b, x.b])
                sm = nxt(smr, "smr")
                for hf in range(2):
                    sq = nxt(t32, "t32")
                    act(AF.Square, sq.t[0:nt_, :], ps[pv[hf]].t[0:nt_, :], [ps[pv[hf]].b], [sq.b, sm.b], accum_out=sm.t[0:nt_, hf:hf + 1])
                tt(sm.t[0:nt_, 2:3], sm.t[0:nt_, 0:1], sm.t[0:nt_, 1:2], ALU.add, [sm.b], [sm.b])
                act(AF.Sqrt, sm.t[0:nt_, 3:4], sm.t[0:nt_, 2:3], [sm.b, small.b], [sm.b], scale=1.0 / 1024, bias=small.t[0:nt_, 0:1])
                mk.op("dve", lambda e: e.reciprocal(out=sm.t[0:nt_, 4:5], in_=sm.t[0:nt_, 3:4]), [sm.b], [sm.b])
                for hf in range(2):
                    stt(vtm.t[0:nt_, i, hf * 512:(hf + 1) * 512], ps[pv[hf]].t[0:nt_, :], sm.t[0:nt_, 4:5], gmt.t[0:nt_, hf * 512:(hf + 1) * 512], ALU.mult, ALU.mult,
                        [ps[pv[hf]].b, sm.b, gmt.b], [vtm.b])
                if not prompt:
                    g32 = nxt(big32, "big32")
                    for hf in range(2):
                        stt(g32.t[0:nt_, hf, :], ps[pv[hf]].t[0:nt_, :], sm.t[0:nt_, 4:5], gmt.t[0:nt_, hf * 512:(hf + 1) * 512], ALU.mult, ALU.mult,
                            [ps[pv[hf]].b, sm.b, gmt.b], [g32.b])
                    store(g32, gv_o[l].rearrange("t (a f) -> t a f", f=512), g32.t[0:64, 0:2, :], DB["outs"])
                rel(*pv)
            for mg in range(2):
                pu = lin_group(l, "win", 8, 960 + mg * 512, C4, rf, Tn, [x.b])
                for j in range(4):
                    m = mg * 4 + j
                    pm = pget(hold=True)
                    t = nxt(t32, "t32")
                    for i in range(nsub):
                        nt_ = min(128, Tn - i * 128)
                        if prompt:
                            wsT = gmwT.t[:, m * 128:(m + 1) * 128]; bia = gmt.t[:, 1024 + m * 128:1024 + (m + 1) * 128]
                        else:
                            wsT = gmwT.t[0:64, 1024 + m * 64:1024 + (m + 1) * 64]; bia = gmt.t[:, 2048 + m * 64:2048 + (m + 1) * 64]
                        mm(pm, ps[pm].t[:, i * 128:i * 128 + nt_], vtm.t[0:nt_, i, m * 128:(m + 1) * 128], wsT, True, True, [vtm.b, gmwT.b])
                        tt(t.t[:, i * 128:i * 128 + nt_], ps[pm].t[:, i * 128:i * 128 + nt_], bia, ALU.add, [ps[pm].b, gmt.b], [t.b])
                    tt(mxT.t[:, m, 0:Tn], t.t[:, 0:Tn], ps[pu[j]].t[:, 0:Tn], ALU.mult, [t.b, ps[pu[j]].b], [mxT.b])
                    rel(pm, pu[j])
            for mg in range(2):
                pg_ = lin_group(l, "win", 8, 4032 + mg * 512, C4, rf, Tn, [x.b])
                for j in range(4):
                    m = mg * 4 + j
                    sg = nxt(t32, "t32")
                    act(AF.Sigmoid, sg.t[:, 0:Tn], ps[pg_[j]].t[:, 0:Tn], [ps[pg_[j]].b], [sg.b])
                    rel(pg_[j])
                    tt(mxT.t[:, m, 0:Tn], mxT.t[:, m, 0:Tn], sg.t[:, 0:Tn], ALU.mult, [mxT.b, sg.b], [mxT.b])
            for mg in range(2):
                pg_ = lin_group(l, "win", 8, 3008 + mg * 512, C4, rf, Tn, [x.b])
                for j in range(4):
                    m = mg * 4 + j
                    sg = nxt(t32, "t32")
                    act(AF.Sigmoid, sg.t[:, 0:Tn], ps[pg_[j]].t[:, 0:Tn], [ps[pg_[j]].b], [sg.b])
                    rel(pg_[j])
                    tt(sg.t[:, 0:Tn], sg.t[:, 0:Tn], oa.t[:, m, 0:Tn], ALU.mult, [sg.b, oa.b], [sg.b])
                    tt(mxT.t[:, m, 0:Tn], mxT.t[:, m, 0:Tn], sg.t[:, 0:Tn], ALU.add, [mxT.b, sg.b], [mxT.b])
            for half in range(2):
                pis = lin_group(l, "wo", 8, half * 512, C4, lambda kc: mxT.t[:, kc, 0:Tn], Tn, [mxT.b])
                for m in range(4):
                    c = half * 4 + m
                    tt(hT.t[:, c, c0:c0 + Tn], ps[pis[m]].t[:, 0:Tn], hT.t[:, c, c0:c0 + Tn], ALU.add, [ps[pis[m]].b, hB[ti]], [hB[ti]])
                    rel(pis[m])
            ffn(l, ti, "f2n", "f2g", "f2u", "f2d")
            x = rms_h(l, ti, "plen")
            for i in range(nsub):
                nt_ = min(128, Tn - i * 128)
                a = nxt(t32, "t32")
                load(a, a.t[0:nt_, 0:256], pin[l, c0 + i * 128:c0 + i * 128 + nt_, :])
                pi = pget(hold=True)
                for c in range(2):
                    mk.op("pe", lambda e: e.transpose(ps[pi].t[:, c * 128:c * 128 + nt_], a.t[0:nt_, c * 128:(c + 1) * 128], cs.t[0:nt_, 0:nt_]), [a.b, cs.b], [ps[pi].b])
                act(AF.Copy, pT.t[:, 0:2, i * 128:i * 128 + nt_], ps[pi].t[:, 0:256].rearrange("p (c f) -> p c f", f=128)[:, :, 0:nt_], [ps[pi].b], [pT.b])
                rel(pi)
            for q4 in range(4):
                C2 = [(0, 128), (128, 128)]
                pg_ = lin_group(l, "pg", 8, q4 * 256, C2, lambda kc: x.t[:, kc, 0:Tn], Tn, [x.b])
                pp_ = lin_group(l, "pp", 2, q4 * 256, C2, lambda kc: pT.t[:, kc, 0:Tn], Tn, [pT.b])
                for j in range(2):
                    c = q4 * 2 + j
                    sg = nxt(t32, "t32")
                    act(AF.Sigmoid, sg.t[:, 0:Tn], ps[pg_[j]].t[:, 0:Tn], [ps[pg_[j]].b], [sg.b])
                    tt(sg.t[:, 0:Tn], sg.t[:, 0:Tn], ps[pp_[j]].t[:, 0:Tn], ALU.mult, [sg.b, ps[pp_[j]].b], [sg.b])
                    tt(hT.t[:, c, c0:c0 + Tn], hT.t[:, c, c0:c0 + Tn], sg.t[:, 0:Tn], ALU.add, [hB[ti], sg.b], [hB[ti]])
                    rel(pg_[j], pp_[j])

        finals = [DB["outs"]]

        def save_state(l):
            mk.dma("pool", lambda e: [e.dma_start(out=hst_o, in_=hT.t[:])], hB[0], reads=[hB], pwrites=[DB["outs"]])
            store(cbS, cbSd[l], cbS.t[:], DB["outs"])
            store(krS, krSd[l], krS.t[:], DB["outs"])
            finals.extend([DB["qs%d" % l], DB["own%d" % l]])

        def load_state(l):
            mk.dma("sp", lambda e: [e.dma_start(out=hT.t[:], in_=hst_i)], hB[0], writes=[hB])
            load(cbS, cbS.t[:], cbSd[l])
            load(krS, krS.t[:], krSd[l])

        PUMP = [0]

        def rest_of_layer(l):
            load_wukv(l)
            load_gmw(l)
            PUMP[0] = (len(cq) + 35) // 36
            att_sample(l)
            pump(len(cq))
            phase_c(l, 4)
            expand_prompt(l)
            for s_ in range(4):
                att_prompt(l, s_)
                phase_c(l, s_)

        def final_y():
            for ti in range(5):
                c0, Tn = TILES[ti]
                transpose_out(lambda c: hT.t[:, c, c0:c0 + Tn], 8, 128, Tn, lambda i, n: y_o[c0 + i * 128:c0 + i * 128 + n, :], DB["outs"], [hB[ti]])

        if stop == "full":
            for l in range(DEPTH):
                convert(l, 0, WLP)
            for l in range(DEPTH):
                load_wuq(l)
                for ti in range(5):
                    phase_a(l, ti)
                exchange(l)
                rest_of_layer(l)
            final_y()
        elif stop == "s1":
            convert(0, 0, PRE)
            load_wuq(0)
            for ti in range(5):
                phase_a(0, ti)
            save_state(0)
        elif stop == "s2":
            KV0, KV1 = WOFF["wuk"][0], WOFF["wo"][0]
            convert(0, KV0, KV1)
            convert(0, SUF, KV0, background=True)
            convert(0, KV1, WLP, background=True)
            convert(1, 0, PRE, background=True)
            load_state(0)
            rest_of_layer(0)
            load_wuq(1)
            for ti in range(5):
                phase_a(1, ti)
            save_state(1)
        elif stop == "s3":
            KV0, KV1 = WOFF["wuk"][0], WOFF["wo"][0]
            convert(1, KV0, KV1)
            convert(1, SUF, KV0, background=True)
            convert(1, KV1, WLP, background=True)
            load_state(1)
            rest_of_layer(1)
            final_y()
        mk.wait_all("sp", finals)
        stats = mk.emit(nc)
    return nc, stats


_CACHE = {}
DEFAULT_MODE = "staged"


def _consts(r):
    c = np.zeros((128, 528), np.float32)
    c[:, 0:128] = np.eye(128)
    c[:, 128:256] = 1.0
    RT = np.zeros((64, 64), np.float32)
    for m in range(32):
        RT[m + 32, m] = -1.0
    for m in range(32, 64):
        RT[m - 32, m] = 1.0
    c[0:64, 256:320] = RT
    s_ = np.arange(128)
    c[:, 320:448] = (s_[:, None] <= s_[None, :]).astype(np.float32)
    s6 = np.arange(64)
    c[0:64, 448:512] = ((s6[:, None] <= s6[None, :]) & (s6[:, None] // 16 == s6[None, :] // 16)).astype(np.float32)
    c[:, 512:520] = np.where(np.arange(8)[None, :] < r, 0.0, NEG)
    c[:, 520:528] = (np.arange(8)[None, :] == r).astype(np.float32)
    return c


def _rope_table(pos):
    inv = (10000.0 ** (-np.arange(32, dtype=np.float32) / np.float32(32))).astype(np.float32)
    ang = pos.astype(np.float32)[None, :] * inv[:, None]
    cos = np.cos(ang).astype(np.float32); sin = np.sin(ang).astype(np.float32)
    return np.concatenate([np.concatenate([cos, cos], 0), np.concatenate([sin, sin], 0)], 1)


def _get(stage):
    if stage not in _CACHE:
        _CACHE[stage] = build(stage)
    return _CACHE[stage][0]


def kernel(**inp):
    mode = os.environ.get("MK_MODE", DEFAULT_MODE)
    f = lambda k: np.asarray(inp[k], np.float32)
    xp = f("x_prompt")[0]; xs = f("x_sample").reshape(512, D)
    pp = f("p_prompt")[:, 0]; psm = f("p_sample").reshape(DEPTH, 512, PLE)
    cc = f("cache_kv_latent"); ck = f("cache_k_rope")
    names = {"f1g": "ffn1_w_gate", "f1u": "ffn1_w_up", "f1d": "ffn1_w_down", "win": "w_in", "wuq": "w_uq", "wuk": "w_uk", "wuv": "w_uv",
             "wo": "w_o", "f2g": "ffn2_w_gate", "f2u": "ffn2_w_up", "f2d": "ffn2_w_down", "pg": "ple_w_gate", "pp": "ple_w_proj"}
    slab = np.zeros((DEPTH, 128, WLP), np.float32)
    for n_, (off, nk, C) in WOFF.items():
        w = f(names[n_]).reshape(DEPTH, nk, 128, C)
        slab[:, :, off:off + nk * C] = w.transpose(0, 2, 1, 3).reshape(DEPTH, 128, nk * C)
    gains = np.zeros((128, DEPTH * NG), np.float32)
    gsrc = {"f1n": ("ffn1_norm", 8), "mixn": ("mix_norm", 8), "qan": ("q_a_norm", 3), "qnn": ("q_nope_norm", 1), "qrn": ("q_rope_norm", 1),
            "kvan": ("kv_a_norm", 4), "krn": ("k_rope_norm", 1), "knn": ("k_nope_norm", 1), "f2n": ("ffn2_norm", 8), "plen": ("ple_norm", 8)}
    for l in range(DEPTH):
        for g_, (nm, ncol) in gsrc.items():
            v = f(nm)[l]
            if v.shape[0] == 64:
                gains[0:64, l * NG + GCOLS[g_]] = v
            else:
                gains[:, l * NG + GCOLS[g_]:l * NG + GCOLS[g_] + ncol] = v.reshape(ncol, 128).T
    gmtab = np.zeros((DEPTH, 128, 2560), np.float32)
    gmw = np.zeros((DEPTH, 128, 1536), np.float32)
    ws = f("gm_w_s"); bs = f("gm_b_s"); gv = f("gm_v_norm")
    for l in range(DEPTH):
        gmtab[l, :, 0:1024] = gv[l][None, :]
        for g in range(8):
            gmtab[l, :, 1024 + g * 128:1024 + (g + 1) * 128] = bs[l, g][None, :]
            gmtab[l, :, 2048 + g * 64:2048 + (g + 1) * 64] = np.tile(bs[l, g, :16], 4)[None, :]
            gmw[l, :, g * 128:(g + 1) * 128] = ws[l, g].T
            for b in range(4):
                gmw[l, b * 16:(b + 1) * 16, 1024 + g * 64 + b * 16:1024 + g * 64 + (b + 1) * 16] = ws[l, g, :16, :16].T
    base = []
    for r in range(8):
        blocks = [8 * s + r for s in range(4)]
        xin = np.concatenate([xp[b * 512:(b + 1) * 512] for b in blocks] + [xs[r * 64:(r + 1) * 64]], 0)
        pin = np.concatenate([np.concatenate([pp[:, b * 512:(b + 1) * 512] for b in blocks], 1), psm[:, r * 64:(r + 1) * 64]], 1)
        pos = np.concatenate([np.arange(b * 512, (b + 1) * 512) for b in blocks] + [PAST + np.arange(16)] * 4)
        base.append({"xin": np.ascontiguousarray(xin), "pin": np.ascontiguousarray(pin),
                     "cch": np.ascontiguousarray(cc[:, r * 4:(r + 1) * 4]), "ckr": np.ascontiguousarray(ck[:, r * 4:(r + 1) * 4]),
                     "slab": slab, "gains": gains, "gmtab": gmtab, "gmw": gmw, "cst": _consts(r), "rope_t": _rope_table(pos)})
    common = ["slab", "gains", "gmtab", "gmw", "cst", "rope_t"]
    outs = [dict() for _ in range(8)]
    if mode == "fused":
        res = run_bass_kernel_spmd(_get("full"), base, core_ids=list(range(8)))
        outs = res.results
    else:
        res1 = run_bass_kernel_spmd(_get("s1"), [{k: base[r][k] for k in common + ["xin"]} for r in range(8)], core_ids=list(range(8))).results
        prev = res1
        for l, stage in ((0, "s2"), (1, "s3")):
            xco = np.concatenate([np.asarray(prev[r]["ownc%d" % l]).reshape(128, 8192) for r in range(8)], 0)
            xro = np.concatenate([np.asarray(prev[r]["ownr%d" % l]).reshape(64, 2048) for r in range(8)], 0)
            ims = []
            for r in range(8):
                m = {k: base[r][k] for k in common + ["pin", "cch", "ckr"]}
                for k in ("qs_n", "qs_r", "ownc", "ownr", "cbSd", "krSd"):
                    m["%s%d" % (k, l)] = np.asarray(prev[r]["%s%d" % (k, l)])
                m["xco%d" % l] = xco; m["xro%d" % l] = xro
                m["hst_i"] = np.asarray(prev[r]["hst_o"])
                ims.append(m)
            cur = run_bass_kernel_spmd(_get(stage), ims, core_ids=list(range(8))).results
            for r in range(8):
                outs[r].update({k: v for k, v in prev[r].items() if k.startswith(("c_o", "kr_o"))})
                outs[r].update({k: v for k, v in cur[r].items() if k.startswith(("c_o", "kr_o", "gv_o", "y_o"))})
            prev = cur
    y_p = np.zeros((1, SEQ, D), np.float32); y_s = np.zeros((32, DEC, D), np.float32)
    pc = np.zeros((DEPTH, 1, SEQ, KVL), np.float32); pk = np.zeros((DEPTH, 1, SEQ, ROPE), np.float32)
    sc = np.zeros((DEPTH, 32, DEC, KVL), np.float32); sk = np.zeros((DEPTH, 32, DEC, ROPE), np.float32)
    sv = np.zeros((DEPTH, 32, DEC, D), np.float32)
    for r in range(8):
        o = outs[r]
        yo = np.asarray(o["y_o"])
        for l in range(DEPTH):
            co = np.asarray(o["c_o%d" % l]); ko = np.asarray(o["kr_o%d" % l])
            for s in range(4):
                b = 8 * s + r
                pc[l, 0, b * 512:(b + 1) * 512] = co[s * 512:(s + 1) * 512]
                pk[l, 0, b * 512:(b + 1) * 512] = ko[s * 512:(s + 1) * 512]
            sc[l, r * 4:(r + 1) * 4] = co[2048:].reshape(4, DEC, KVL)
            sk[l, r * 4:(r + 1) * 4] = ko[2048:].reshape(4, DEC, ROPE)
            sv[l, r * 4:(r + 1) * 4] = np.asarray(o["gv_o%d" % l]).reshape(4, DEC, D)
        for s in range(4):
            b = 8 * s + r
            y_p[0, b * 512:(b + 1) * 512] = yo[s * 512:(s + 1) * 512]
        y_s[r * 4:(r + 1) * 4] = yo[2048:].reshape(4, DEC, D)
    return (y_p, y_s, pc, pk, sc, sk, sv)
```
